# Optimizing a Trainium2 kernel written in Bass

```python
import jax, jax.numpy as jnp
from jax import lax
import numpy as np

D_MODEL = 1024
BATCH = 8
SEQ = 2048
DEPTH = 1
DEC_BATCH = 128
DEC_SEQ = 1
PAST_LEN = 16384
PAGE_SIZE = 128

HEAD_DIM = 64
RWKV_WIDTH = D_MODEL // 2
RWKV_HEADS = RWKV_WIDTH // HEAD_DIM
DECAY_LORA = 64
AAA_LORA = 64
GATE_LORA = 128
RWKV_PROJ = 3 * RWKV_WIDTH + DECAY_LORA + AAA_LORA + GATE_LORA
LRU_WIDTH = D_MODEL // 2
LRU_BLOCKS = 8
LRU_BLOCK = LRU_WIDTH // LRU_BLOCKS
LRU_CONV = 4
LRU_C = 8.0
IN_COLS = RWKV_PROJ + LRU_WIDTH + 2 * D_MODEL
D_FF = 2816
FFN_CONV = 3
NORM_EPS = 1e-6
LN_X_EPS = 64e-5

kernel_name = 'rwkv7_rglru_gated_hybrid_step'


def rmsnorm(x, g):
    x32 = x.astype(jnp.float32)
    y = x32 * lax.rsqrt(jnp.mean(x32 * x32, axis=-1, keepdims=True) + NORM_EPS)
    return (y * g.astype(jnp.float32)).astype(x.dtype)


def causal_dwconv(u, buf, w, b):
    width = w.shape[0]
    t = u.shape[1]
    full = jnp.concatenate([buf.astype(u.dtype), u], axis=1)
    y = b + sum(full[:, j:j + t] * w[j] for j in range(width))
    return y, full[:, t:]


def wkv7_scan(r, decay, k, v, aa, bb, s0):
    def step(s, inp):
        r_t, w_t, k_t, v_t, a_t, b_t = inp
        sa = jnp.einsum('bhvk,bhk->bhv', s, a_t)
        s = s * w_t[:, :, None, :] + sa[..., None] * b_t[:, :, None, :] + v_t[..., None] * k_t[:, :, None, :]
        return s, jnp.einsum('bhvk,bhk->bhv', s, r_t)
    xs = tuple(jnp.swapaxes(t.astype(jnp.float32), 0, 1) for t in (r, decay, k, v, aa, bb))
    s, ys = lax.scan(step, s0.astype(jnp.float32), xs)
    return jnp.swapaxes(ys, 0, 1), s


def rglru_scan(a, u, h0):
    def step(h, inp):
        a_t, u_t = inp
        h = a_t * h + u_t
        return h, h
    xs = (jnp.swapaxes(a, 0, 1), jnp.swapaxes(u, 0, 1))
    h, hs = lax.scan(step, h0.astype(jnp.float32), xs)
    return jnp.swapaxes(hs, 0, 1), h


def rwkv7_branch(p, shift_buf, s0, mu, w0, w_up, a0, a_up, g_up, k_k, k_a, r_k, lnx_w, lnx_b, w_out):
    b_, t_, _ = p.shape
    prev = jnp.concatenate([shift_buf.astype(p.dtype), p], axis=1)[:, :t_]
    new_shift = p[:, t_ - 1:]
    m = p + mu * (prev - p)
    cut = np.cumsum([RWKV_WIDTH, RWKV_WIDTH, RWKV_WIDTH, DECAY_LORA, AAA_LORA])
    r, k, v, wd, ad, gd = jnp.split(m, cut, axis=-1)
    w = -jax.nn.softplus(-(w0 + jnp.tanh(wd) @ w_up)) - 0.5
    decay = jnp.exp(-jnp.exp(w.astype(jnp.float32)))
    a = jax.nn.sigmoid(a0 + ad @ a_up)
    g = jax.nn.sigmoid(gd) @ g_up
    heads = lambda z: z.reshape(b_, t_, RWKV_HEADS, HEAD_DIM)
    kk = heads(k * k_k).astype(jnp.float32)
    kk = kk / jnp.maximum(jnp.sqrt(jnp.sum(kk * kk, axis=-1, keepdims=True)), 1e-12)
    k = k * (1.0 + (a - 1.0) * k_a)
    rh, kh, vh = heads(r), heads(k), heads(v)
    y, s = wkv7_scan(rh, heads(decay), kh, vh, -kk, kk * heads(a).astype(jnp.float32), s0)
    mean = jnp.mean(y, axis=-1, keepdims=True)
    var = jnp.mean(jnp.square(y - mean), axis=-1, keepdims=True)
    yn = ((y - mean) * lax.rsqrt(var + LN_X_EPS)).reshape(b_, t_, RWKV_WIDTH) * lnx_w + lnx_b
    bonus = (jnp.sum(rh * kh * r_k, axis=-1, keepdims=True) * vh).reshape(b_, t_, RWKV_WIDTH)
    out = ((yn.astype(p.dtype) + bonus) * g) @ w_out
    return out, new_shift, s


def rglru_branch(xb, conv_buf, h0, conv_w, conv_b, gx_w, gx_b, ga_w, ga_b, lam, w_out):
    b_, t_, _ = xb.shape
    xc, new_conv = causal_dwconv(xb, conv_buf, conv_w, conv_b)
    blk = xc.reshape(b_, t_, LRU_BLOCKS, LRU_BLOCK)
    gate_x = jax.nn.sigmoid(jnp.einsum('btni,nij->btnj', blk, gx_w).reshape(b_, t_, LRU_WIDTH) + gx_b)
    gate_a = jax.nn.sigmoid(jnp.einsum('btni,nij->btnj', blk, ga_w).reshape(b_, t_, LRU_WIDTH) + ga_b)
    log_a = -LRU_C * gate_a.astype(jnp.float32) * jax.nn.softplus(-lam.astype(jnp.float32))
    a = jnp.exp(log_a)
    u = jnp.sqrt(-jnp.expm1(2.0 * log_a)) * (gate_x * xc).astype(jnp.float32)
    hs, h = rglru_scan(a, u, h0)
    return hs.astype(xb.dtype) @ w_out, new_conv, h


def conv_ffn(h, buf, up, conv_w, conv_b, down):
    u, new_buf = causal_dwconv(h @ up, buf, conv_w, conv_b)
    gate, val = jnp.split(u, [D_FF], axis=-1)
    return (jax.nn.gelu(gate, approximate=True) * val) @ down, new_buf


def trunk_layer(x, shift_buf, wkv0, lru_buf, h0, ffn_buf,
                norm_pre_mix, norm_post_mix, norm_pre_ffn, norm_post_ffn, w_in,
                rwkv_mu, rwkv_w0, rwkv_w_up, rwkv_a0, rwkv_a_up, rwkv_g_up, rwkv_k_k, rwkv_k_a,
                rwkv_r_k, rwkv_lnx_w, rwkv_lnx_b, rwkv_w_out,
                lru_conv_w, lru_conv_b, lru_gx_w, lru_gx_b, lru_ga_w, lru_ga_b, lru_lambda, lru_w_out,
                w_o, ffn_up, ffn_conv_w, ffn_conv_b, ffn_down):
    xn = rmsnorm(x, norm_pre_mix)
    proj = xn @ w_in
    p_rwkv, xb, gates = jnp.split(proj, [RWKV_PROJ, RWKV_PROJ + LRU_WIDTH], axis=-1)
    oa, new_shift, wkv = rwkv7_branch(p_rwkv, shift_buf, wkv0, rwkv_mu, rwkv_w0, rwkv_w_up, rwkv_a0,
                                      rwkv_a_up, rwkv_g_up, rwkv_k_k, rwkv_k_a, rwkv_r_k,
                                      rwkv_lnx_w, rwkv_lnx_b, rwkv_w_out)
    ob, new_lru_buf, h = rglru_branch(xb, lru_buf, h0, lru_conv_w, lru_conv_b, lru_gx_w, lru_gx_b,
                                      lru_ga_w, lru_ga_b, lru_lambda, lru_w_out)
    ga, gb = jnp.split(jax.nn.sigmoid(gates), 2, axis=-1)
    mix = (ga * oa + gb * ob) @ w_o
    x = x + rmsnorm(mix, norm_post_mix)
    f, new_ffn_buf = conv_ffn(rmsnorm(x, norm_pre_ffn), ffn_buf, ffn_up, ffn_conv_w, ffn_conv_b, ffn_down)
    x = x + rmsnorm(f, norm_post_ffn)
    return x, (new_shift, wkv, new_lru_buf, h, new_ffn_buf)


def setup_inputs(seed: int = 0) -> dict:
    key = jax.random.key(seed)
    ks = iter(jax.random.split(key, 48))
    f32 = jnp.float32
    nrm = lambda shape, scale: jax.random.normal(next(ks), shape, f32) * scale
    L = DEPTH
    W = RWKV_WIDTH
    lam_u = jax.random.uniform(next(ks), (L, LRU_WIDTH), f32, minval=0.9, maxval=0.999)
    return {
        'x_prompt': nrm((BATCH, SEQ, D_MODEL), 1.0),
        'x_sample': nrm((DEC_BATCH, DEC_SEQ, D_MODEL), 1.0),
        'state_rwkv_shift': nrm((L, DEC_BATCH, 1, RWKV_PROJ), 1.0),
        'state_rwkv_wkv': nrm((L, DEC_BATCH, RWKV_HEADS, HEAD_DIM, HEAD_DIM), 0.5),
        'state_lru_conv': nrm((L, DEC_BATCH, LRU_CONV - 1, LRU_WIDTH), 1.0),
        'state_lru_h': nrm((L, DEC_BATCH, LRU_WIDTH), 0.5),
        'state_ffn_conv': nrm((L, DEC_BATCH, FFN_CONV - 1, 2 * D_FF), 1.0),
        'norm_pre_mix': 1.0 + nrm((L, D_MODEL), 0.02),
        'norm_post_mix': 1.0 + nrm((L, D_MODEL), 0.02),
        'norm_pre_ffn': 1.0 + nrm((L, D_MODEL), 0.02),
        'norm_post_ffn': 1.0 + nrm((L, D_MODEL), 0.02),
        'w_in': nrm((L, D_MODEL, IN_COLS), D_MODEL ** -0.5),
        'rwkv_mu': jax.random.uniform(next(ks), (L, RWKV_PROJ), f32),
        'rwkv_w0': jax.random.uniform(next(ks), (L, W), f32, minval=-6.0, maxval=0.0),
        'rwkv_w_up': nrm((L, DECAY_LORA, W), 0.1),
        'rwkv_a0': nrm((L, W), 0.1),
        'rwkv_a_up': nrm((L, AAA_LORA, W), 0.5 * AAA_LORA ** -0.5),
        'rwkv_g_up': nrm((L, GATE_LORA, W), GATE_LORA ** -0.5),
        'rwkv_k_k': 0.85 + nrm((L, W), 0.02),
        'rwkv_k_a': 1.0 + nrm((L, W), 0.02),
        'rwkv_r_k': nrm((L, RWKV_HEADS, HEAD_DIM), 0.1),
        'rwkv_lnx_w': 1.0 + nrm((L, W), 0.02),
        'rwkv_lnx_b': nrm((L, W), 0.02),
        'rwkv_w_out': nrm((L, W, D_MODEL), W ** -0.5),
        'lru_conv_w': nrm((L, LRU_CONV, LRU_WIDTH), LRU_CONV ** -0.5),
        'lru_conv_b': nrm((L, LRU_WIDTH), 0.02),
        'lru_gx_w': nrm((L, LRU_BLOCKS, LRU_BLOCK, LRU_BLOCK), LRU_BLOCK ** -0.5),
        'lru_gx_b': nrm((L, LRU_WIDTH), 0.02),
        'lru_ga_w': nrm((L, LRU_BLOCKS, LRU_BLOCK, LRU_BLOCK), LRU_BLOCK ** -0.5),
        'lru_ga_b': nrm((L, LRU_WIDTH), 0.02),
        'lru_lambda': jnp.log(lam_u) - jnp.log1p(-lam_u),
        'lru_w_out': nrm((L, LRU_WIDTH, D_MODEL), LRU_WIDTH ** -0.5),
        'w_o': nrm((L, D_MODEL, D_MODEL), D_MODEL ** -0.5),
        'ffn_up': nrm((L, D_MODEL, 2 * D_FF), D_MODEL ** -0.5),
        'ffn_conv_w': nrm((L, FFN_CONV, 2 * D_FF), FFN_CONV ** -0.5),
        'ffn_conv_b': nrm((L, 2 * D_FF), 0.02),
        'ffn_down': nrm((L, D_FF, D_MODEL), D_FF ** -0.5),
    }


def reference(x_prompt, x_sample, state_rwkv_shift, state_rwkv_wkv, state_lru_conv, state_lru_h, state_ffn_conv,
              norm_pre_mix, norm_post_mix, norm_pre_ffn, norm_post_ffn, w_in,
              rwkv_mu, rwkv_w0, rwkv_w_up, rwkv_a0, rwkv_a_up, rwkv_g_up, rwkv_k_k, rwkv_k_a,
              rwkv_r_k, rwkv_lnx_w, rwkv_lnx_b, rwkv_w_out,
              lru_conv_w, lru_conv_b, lru_gx_w, lru_gx_b, lru_ga_w, lru_ga_b, lru_lambda, lru_w_out,
              w_o, ffn_up, ffn_conv_w, ffn_conv_b, ffn_down):
    params = (norm_pre_mix, norm_post_mix, norm_pre_ffn, norm_post_ffn, w_in,
              rwkv_mu, rwkv_w0, rwkv_w_up, rwkv_a0, rwkv_a_up, rwkv_g_up, rwkv_k_k, rwkv_k_a,
              rwkv_r_k, rwkv_lnx_w, rwkv_lnx_b, rwkv_w_out,
              lru_conv_w, lru_conv_b, lru_gx_w, lru_gx_b, lru_ga_w, lru_ga_b, lru_lambda, lru_w_out,
              w_o, ffn_up, ffn_conv_w, ffn_conv_b, ffn_down)
    bp = x_prompt.shape[0]
    dt = x_prompt.dtype
    zero_states = (jnp.zeros((bp, 1, RWKV_PROJ), dt),
                   jnp.zeros((bp, RWKV_HEADS, HEAD_DIM, HEAD_DIM), jnp.float32),
                   jnp.zeros((bp, LRU_CONV - 1, LRU_WIDTH), dt),
                   jnp.zeros((bp, LRU_WIDTH), jnp.float32),
                   jnp.zeros((bp, FFN_CONV - 1, 2 * D_FF), dt))
    yp, ys = x_prompt, x_sample
    new_p, new_s = [], []
    for l in range(DEPTH):
        lp = [p[l] for p in params]
        yp, st_p = trunk_layer(yp, *zero_states, *lp)
        ys, st_s = trunk_layer(ys, state_rwkv_shift[l], state_rwkv_wkv[l], state_lru_conv[l],
                               state_lru_h[l], state_ffn_conv[l], *lp)
        new_p.append(st_p)
        new_s.append(st_s)
    stk = lambda lst, i: jnp.stack([s[i] for s in lst], axis=0)
    return (yp, ys,
            stk(new_p, 0), stk(new_p, 1), stk(new_p, 2), stk(new_p, 3), stk(new_p, 4),
            stk(new_s, 0), stk(new_s, 1), stk(new_s, 2), stk(new_s, 3), stk(new_s, 4))
```

```python
import contextlib
import math
import numpy as np
import concourse.bass as bass
import concourse.mybir as mybir
from concourse.bass_utils import run_bass_kernel_spmd

F32 = mybir.dt.float32
BF16 = mybir.dt.bfloat16
AF = mybir.ActivationFunctionType
ALU = mybir.AluOpType
AX = mybir.AxisListType

D = 1024
T = 2048
NS = 16
TT = T + NS
RW = 512
RP = 1792
INC = 4352
DFF = 2816
NCORES = 8
CH = 64
NBA = 128
NBB = 256
CDEC = math.exp(-0.5)

VSPEC = [('g_pre', 8), ('g_post', 8), ('g_pre2', 8), ('g_post2', 8), ('mu', 14), ('w0', 4), ('a0', 4),
         ('k_k', 4), ('k_a', 4), ('r_k', 4), ('lnx_w', 4), ('lnx_b', 4), ('lcw', 16), ('lcb', 4),
         ('gxb', 4), ('gab', 4), ('lam', 4), ('fcw', 132), ('fcb', 44)]
VOFF = {}
_o = 0
for _n, _c in VSPEC:
    VOFF[_n] = _o
    _o += _c
NVEC = _o
C_ONES = 0
C_BONES = 128
C_IDENT = 256
C_MASK1 = 384
C_MASK2 = 512
C_I64 = 576
NCONST = 640


class Tk:
    __slots__ = ("name", "excl", "last_w", "readers")

    def __init__(self, name, excl=False):
        self.name = name
        self.excl = excl
        self.last_w = None
        self.readers = []


class Op:
    __slots__ = ("eng", "fn", "waits", "idx", "signal", "dma_sem")

    def __init__(self, eng, fn):
        self.eng = eng
        self.fn = fn
        self.waits = []
        self.signal = False
        self.dma_sem = None


class Sched:
    ENGS = ("pe", "act", "dve", "pool", "sp")

    def __init__(self, nc):
        self.nc = nc
        self.ops = {e: [] for e in self.ENGS}
        self.waited = {e: {} for e in self.ENGS}
        self.dma_sems = []
        self.final_tokens = []
        self.pending = {e: [] for e in self.ENGS}

    def new_dma_sem(self):
        self.dma_sems.append([None, 0, None])
        return len(self.dma_sems) - 1

    def _add_wait(self, op, tok):
        if tok is None:
            return
        eng = op.eng
        if tok[0] == 'e':
            if tok[1] == 'pe' and eng == 'pe':
                return
            key = ('e', tok[1])
        else:
            key = ('d', tok[1])
        val = tok[2]
        if self.waited[eng].get(key, -1) >= val:
            return
        self.waited[eng][key] = val
        op.waits.append(tok)
        if tok[0] == 'e':
            self.ops[tok[1]][tok[2]].signal = True

    def op(self, eng, fn, reads=(), writes=(), dma_sem=None, extra=()):
        o = Op(eng, fn)
        o.idx = len(self.ops[eng])
        toks = list(extra)
        if self.pending[eng]:
            toks.extend(self.pending[eng])
            self.pending[eng] = []
        for r in reads:
            toks.append(r.last_w)
            if r.excl:
                toks.extend(r.readers)
        for w in writes:
            toks.append(w.last_w)
            toks.extend(w.readers)
        if dma_sem is not None:
            toks.append(self.dma_sems[dma_sem][2])
        for t in toks:
            self._add_wait(o, t)
        self.ops[eng].append(o)
        if dma_sem is not None:
            s = self.dma_sems[dma_sem]
            s[1] += 16
            o.dma_sem = dma_sem
            tok = ('d', dma_sem, s[1])
            s[2] = tok
        else:
            tok = ('e', eng, o.idx)
        for r in reads:
            if r.excl:
                r.last_w = tok
                r.readers = []
            else:
                r.readers.append(tok)
        for w in writes:
            w.last_w = tok
            w.readers = []
        co_switch()
        return tok

    def barrier(self):
        toks = []
        for e in self.ENGS:
            for o in reversed(self.ops[e]):
                if o.dma_sem is None:
                    toks.append(('e', e, o.idx))
                    break
        for i, s in enumerate(self.dma_sems):
            if s[2] is not None:
                toks.append(s[2])
        for e in self.ENGS:
            self.pending[e] = list(toks)

    def finish(self, toks):
        self.final_tokens = list(toks)

    def emit(self):
        nc = self.nc
        fin = Op('sp', None)
        fin.idx = len(self.ops['sp'])
        for s in self.dma_sems:
            self._add_wait(fin, s[2])
        for t in self.final_tokens:
            self._add_wait(fin, t)
        with contextlib.ExitStack() as st:
            esem = {}
            for e in self.ENGS:
                esem[e] = st.enter_context(nc.semaphore("s_" + e))
            for i, s in enumerate(self.dma_sems):
                s[0] = st.enter_context(nc.semaphore("d%d" % i))
            sigval = {}
            for e in self.ENGS:
                c = 0
                for o in self.ops[e]:
                    if o.signal and o.dma_sem is None:
                        c += 1
                        sigval[(e, o.idx)] = c
            block = st.enter_context(nc.Block())

            def mk(ename, extra=None):
                def body(eng):
                    def do_waits(o):
                        for t in o.waits:
                            if t[0] == 'e':
                                eng.wait_ge(esem[t[1]], sigval[(t[1], t[2])])
                            else:
                                eng.wait_ge(self.dma_sems[t[1]][0], t[2])
                    for o in self.ops[ename]:
                        do_waits(o)
                        inst = o.fn(eng)
                        if o.dma_sem is not None:
                            inst.then_inc(self.dma_sems[o.dma_sem][0], 16)
                        elif o.signal:
                            inst.then_inc(esem[ename], 1)
                    if extra is not None:
                        do_waits(extra)
                return body
            block.tensor(mk('pe'))
            block.scalar(mk('act'))
            block.vector(mk('dve'))
            block.gpsimd(mk('pool'))
            block.sync(mk('sp', fin))
        return {e: len(self.ops[e]) for e in self.ENGS}


class Buf:
    def __init__(self, name, ap, nk=1, excl=False):
        self.t = ap
        self.k = [Tk("%s%d" % (name, i), excl) for i in range(nk)]

    @property
    def all(self):
        return list(self.k)


class Arena:
    def __init__(self, tensor, nbytes):
        self.tensor = tensor
        self.nbytes = nbytes
        self.off = 0
        self.top = nbytes

    def alloc(self, name, free_shape, dtype, nk=1, from_top=False):
        sz = 4 if dtype == F32 else 2
        n = int(np.prod(free_shape))
        nb = (n * sz + 63) // 64 * 64
        if from_top:
            self.top -= nb
            o = self.top
        else:
            o = self.off
            self.off += nb
        assert self.off <= self.top, ("arena overflow", name, self.off, self.top)
        ap = self.tensor[:, o // 2:(o + n * sz) // 2]
        if dtype == F32:
            ap = ap.bitcast(F32)
        if len(free_shape) == 2:
            ap = ap.rearrange("p (a b) -> p a b", a=free_shape[0])
        elif len(free_shape) == 3:
            ap = ap.rearrange("p (a b c) -> p a b c", a=free_shape[0], b=free_shape[1])
        elif len(free_shape) == 4:
            ap = ap.rearrange("p (a b c d) -> p a b c d", a=free_shape[0], b=free_shape[1], c=free_shape[2])
        return Buf(name, ap, nk)


import threading
_TL = threading.local()


def co_switch():
    sw = getattr(_TL, 'sw', None)
    if sw is not None:
        sw()


def co_run(funcs):
    n = len(funcs)
    st = {'turn': 0, 'alive': [True] * n, 'err': None}
    cv = threading.Condition()

    def nxt(i):
        for d in range(1, n + 1):
            j = (i + d) % n
            if st['alive'][j]:
                st['turn'] = j
                return
        st['turn'] = -1

    def switch(i):
        with cv:
            nxt(i)
            cv.notify_all()
            while st['turn'] != i:
                cv.wait()

    def worker(i):
        with cv:
            while st['turn'] != i:
                cv.wait()
        _TL.sw = lambda: switch(i)
        try:
            funcs[i]()
        except BaseException as e:
            st['err'] = e
        finally:
            _TL.sw = None
            with cv:
                st['alive'][i] = False
                nxt(i)
                cv.notify_all()
    ths = [threading.Thread(target=worker, args=(i,)) for i in range(n)]
    for t in ths:
        t.start()
    for t in ths:
        t.join()
    if st['err'] is not None:
        raise st['err']


class Builder:
    def __init__(self, debug=None):
        self.debug = debug or []
        self.nc = bass.Bass("TRN2", target_bir_lowering=False)
        self.S = Sched(self.nc)
        self.dbg_outs = {}

    def act(self, out, in_, func, reads, writes, scale=None, bias=None, eng='act'):
        kw = {}
        if scale is not None:
            kw['scale'] = scale
        if bias is not None:
            kw['bias'] = bias
        return self.S.op('act', lambda e: e.activation(out=out, in_=in_, func=func, **kw), reads, writes)

    def tt(self, eng, out, in0, in1, op, reads, writes):
        return self.S.op(eng, lambda e: e.tensor_tensor(out=out, in0=in0, in1=in1, op=op), reads, writes)

    def ts(self, eng, out, in0, s1, op0, reads, writes, s2=None, op1=None):
        if op1 is None:
            return self.S.op(eng, lambda e: e.tensor_scalar(out=out, in0=in0, scalar1=s1, scalar2=None, op0=op0), reads, writes)
        return self.S.op(eng, lambda e: e.tensor_scalar(out=out, in0=in0, scalar1=s1, scalar2=s2, op0=op0, op1=op1), reads, writes)

    def stt(self, out, in0, scalar, in1, op0, op1, reads, writes):
        return self.S.op('dve', lambda e: e.scalar_tensor_tensor(out=out, in0=in0, scalar=scalar, in1=in1, op0=op0, op1=op1), reads, writes)

    def cp(self, eng, out, in_, reads, writes):
        if eng == 'act':
            return self.S.op('act', lambda e: e.copy(out=out, in_=in_), reads, writes)
        return self.S.op(eng, lambda e: e.tensor_copy(out=out, in_=in_), reads, writes)

    def mm(self, out, lhsT, rhs, start, stop, reads, writes):
        return self.S.op('pe', lambda e: e.matmul(out, lhsT=lhsT, rhs=rhs, start=start, stop=stop), reads, writes)

    def tr(self, out, in_, ident, reads, writes):
        return self.S.op('pe', lambda e: e.transpose(out, in_, ident), reads, writes)

    def dma(self, eng, out, in_, reads, writes, sem, extra=()):
        return self.S.op(eng, lambda e: e.dma_start(out=out, in_=in_), reads, writes, dma_sem=sem, extra=extra)

    def recip(self, out, in_, reads, writes):
        return self.S.op('dve', lambda e: e.reciprocal(out=out, in_=in_), reads, writes)

    def memset(self, eng, ap, val, writes):
        return self.S.op(eng, lambda e: e.memset(ap, val), (), writes)

    def scan(self, out, d0, d1, init, reads, writes, op0=None):
        op0 = ALU.mult if op0 is None else op0
        return self.S.op('dve', lambda e: e.tensor_tensor_scan(out=out, data0=d0, data1=d1, initial=init,
                                                              op0=op0, op1=ALU.add), reads, writes)

    def reduce(self, out, in_, reads, writes):
        return self.S.op('dve', lambda e: e.tensor_reduce(out=out, in_=in_, axis=AX.X, op=ALU.add), reads, writes)

    def build(self):
        nc = self.nc
        S = self.S
        dr = lambda name, shape, kind: nc.dram_tensor(name, list(shape), F32, kind=kind).ap()
        I = "ExternalInput"
        O = "ExternalOutput"
        xT = dr("xT", [D, TT], I)
        w_in = dr("w_in", [D, INC], I)
        vecs = dr("vecs", [128, NVEC], I)
        consts = dr("consts", [128, NCONST], I)
        lora = dr("lora", [128, RW], I)
        gup = dr("gup", [128, RW], I)
        gxa = dr("gxa", [128, 2 * 4 * 64], I)
        wor = dr("wor", [RW, D], I)
        wol = dr("wol", [RW, D], I)
        wo = dr("wo", [D, D], I)
        wup = dr("wup", [D, 2 * DFF], I)
        wdn = dr("wdn", [DFF, D], I)
        s_shift = dr("s_shift", [128, 14 * NS], I)
        s_lconv = dr("s_lconv", [128, 4 * 3 * NS], I)
        s_lh = dr("s_lh", [128, 4 * NS], I)
        s_fconv = dr("s_fconv", [128, 44 * 2 * NS], I)
        s_wkv = dr("s_wkv", [128, 4 * NS * 64], I)
        yT = dr("yT", [D, TT], O)
        o_shift = dr("o_shift", [128, 14 * 17], O)
        o_lconv = dr("o_lconv", [128, 4 * 3 * 17], O)
        o_lh = dr("o_lh", [128, 4 * 17], O)
        o_fconv = dr("o_fconv", [128, 44 * 2 * 17], O)
        o_wkvp = dr("o_wkvp", [128, 4 * 64], O)
        o_wkvs = dr("o_wkvs", [128, 4 * NS * 64], O)
        x1s = nc.dram_tensor("x1s", [D, TT], F32, kind="Internal").ap()
        X1S = Tk("x1s")
        for name, shape in self.debug:
            self.dbg_outs[name] = dr("dbg_" + name, shape, O)

        with contextlib.ExitStack() as st:
            ARB = 212736
            art = st.enter_context(nc.sbuf_tensor("arena", [128, ARB // 2], BF16))
            ps = [st.enter_context(nc.psum_tensor("ps%d" % i, [128, 512], F32)) for i in range(7)]
            pst = st.enter_context(nc.psum_tensor("pst", [128, 1024], BF16))
            PM = [Buf("pm0", ps[0][:, :], 1, True), Buf("pm1", ps[1][:, :], 1, True)]
            PST = Buf("pstat", ps[2][:, :], 1, True)
            PA = Buf("pa", ps[3][:, :], 1, True)
            PB_ = Buf("pb", ps[4][:, :], 1, True)
            PX = Buf("px", ps[5][:, :], 1, True)
            PW = Buf("pw", ps[6][:, :], 1, True)
            PT = Buf("pt", pst[:, :], 1, True)
            self.pmi = 0

            A = Arena(art, ARB)
            VEC = A.alloc("vec", [NVEC], F32, from_top=True)
            CON = A.alloc("con", [NCONST], F32, from_top=True)
            DER = A.alloc("der", [8], F32, from_top=True)
            DR2 = A.alloc("dr2", [28], F32, from_top=True)
            IDB = A.alloc("idb", [128], BF16, from_top=True)
            ONB = A.alloc("onb", [256], BF16, from_top=True)
            M1B = A.alloc("m1b", [128], F32, from_top=True)
            SHO = A.alloc("sho", [14, 17], F32, from_top=True)
            LCO = A.alloc("lco", [4, 3, 17], F32, from_top=True)
            LHO = A.alloc("lho", [4, 17], F32, from_top=True)
            WKP = A.alloc("wkp", [4, 64], F32, from_top=True)
            SQ = [A.alloc("sq%d" % i, [max(NBA, NBB)], F32, from_top=True) for i in range(2)]
            RSTD = A.alloc("rstd", [max(NBA, NBB)], F32, from_top=True)

            def V(name, j=0):
                o = VOFF[name] + j
                return VEC.t[:, o:o + 1]
            ONESF = CON.t[:, C_ONES:C_ONES + 128]
            BONES = CON.t[:, C_BONES:C_BONES + 128]
            MASK1 = CON.t[:, C_MASK1:C_MASK1 + 128]
            MASK2 = CON.t[:, C_MASK2:C_MASK2 + 64]
            I64 = CON.t[:, C_I64:C_I64 + 64]
            ONES64 = CON.t[:, C_ONES:C_ONES + 64]

            sem_c = S.new_dma_sem()
            self.dma('sp', VEC.t, vecs[:, :], [], VEC.all, sem_c)
            sem_c2 = S.new_dma_sem()
            self.dma('sp', CON.t, consts[:, :], [], CON.all, sem_c2)
            self.cp('dve', IDB.t, CON.t[:, C_IDENT:C_IDENT + 128], CON.all, IDB.all)
            self.cp('dve', ONB.t, CON.t[:, C_ONES:C_ONES + 256], CON.all, ONB.all)
            ONESB = ONB.t[:, 0:128]
            BONESB = ONB.t[:, 128:256]
            self.act(DER.t[:, 0:4], VEC.t[:, VOFF['lam']:VOFF['lam'] + 4], AF.Exp, VEC.all, DER.all, scale=-1.0)
            self.act(DER.t[:, 0:4], DER.t[:, 0:4], AF.Ln, DER.all, DER.all, bias=1.0)
            self.ts('dve', DER.t[:, 4:8], DER.t[:, 0:4], -16.0, ALU.mult, DER.all, DER.all)
            self.ts('dve', DER.t[:, 0:4], DER.t[:, 0:4], -8.0, ALU.mult, DER.all, DER.all)
            for b_ in (SHO, LCO, LHO):
                self.memset('pool', b_.t, 0.0, b_.all)
            for i_, nm in enumerate(('w0', 'a0', 'k_a', 'gxb', 'gab')):
                self.ts('dve', DR2.t[:, 4 * i_:4 * i_ + 4], VEC.t[:, VOFF[nm]:VOFF[nm] + 4], 0.5, ALU.mult, VEC.all + DR2.all, DR2.all)
            self.ts('dve', DR2.t[:, 20:24], DER.t[:, 0:4], 0.5, ALU.mult, DER.all + DR2.all, DR2.all)
            self.cp('dve', DR2.t[:, 24:28], DER.t[:, 0:4], DER.all + DR2.all, DR2.all)
            H2 = lambda i_, j_: DR2.t[:, 4 * i_ + j_:4 * i_ + j_ + 1]
            OMM = A.alloc("omm", [14], F32, from_top=True)
            self.ts('dve', OMM.t, VEC.t[:, VOFF['mu']:VOFF['mu'] + 14], -1.0, ALU.mult, VEC.all, OMM.all, s2=1.0, op1=ALU.add)

            mark = (A.off, A.top)
            W_IN = A.alloc("w_in", [8, INC], BF16, nk=8)
            WOR = A.alloc("wor", [4, D], BF16)
            WOL = A.alloc("wol", [4, D], BF16)
            WO = A.alloc("wo", [8, D], BF16)
            LORA = A.alloc("lora", [RW], BF16)
            GUP = A.alloc("gup", [RW], BF16)
            GXA = A.alloc("gxa", [2, 4, 64], BF16)
            wsems = [S.new_dma_sem() for _ in range(8)]
            wiv = w_in.rearrange("(k p) n -> p k n", p=128)
            WIG = [(0, 1792), (1792, 2304), (2304, 3328), (3328, 4352)]
            WINT = [Tk("wint%d" % g_) for g_ in range(4)]
            for g_, (c0_, c1_) in enumerate(WIG):
                self.dma('pool', W_IN.t[:, :, c0_:c1_], wiv[:, :, c0_:c1_], [], [WINT[g_]], wsems[g_])

            def WK(oc):
                c_ = oc * 128
                return WINT[0 if c_ < 1792 else 1 if c_ < 2304 else 2 if c_ < 3328 else 3]
            ws2 = [S.new_dma_sem() for _ in range(6)]
            self.dma('pool', LORA.t, lora[:, :], [], LORA.all, ws2[0])
            self.dma('pool', GUP.t, gup[:, :], [], GUP.all, ws2[1])
            self.dma('pool', GXA.t, gxa.rearrange("p (a b c) -> p a b c", a=2, b=4), [], GXA.all, ws2[2])
            self.dma('pool', WOR.t, wor.rearrange("(k p) n -> p k n", p=128), [], WOR.all, ws2[3])
            self.dma('pool', WOL.t, wol.rearrange("(k p) n -> p k n", p=128), [], WOL.all, ws2[4])
            self.dma('pool', WO.t, wo.rearrange("(k p) n -> p k n", p=128), [], WO.all, ws2[5])


            markA = A.off
            xTv = xT.rearrange("(k p) n -> p k n", p=128)
            x1v = x1s.rearrange("(k p) n -> p k n", p=128)
            yTv = yT.rearrange("(k p) n -> p k n", p=128)
            N = NBA
            NCK = N // CH
            pXB = [A.alloc("pxb%d" % i, [8, N], F32, nk=8) for i in range(3)]
            pXN = [A.alloc("pxn%d" % i, [8, N + 3], BF16, nk=8) for i in range(3)]
            _sqv = [SQ[i].t[:, :].bitcast(BF16) for i in range(2)]
            SQA = [Buf("sqa%d" % i, _sqv[i // 2][:, (i % 2) * 128:(i % 2) * 128 + 128]) for i in range(4)]
            RSH = [Buf("rsh%d" % i, RSTD.t[:, i * 128:(i + 1) * 128]) for i in range(2)]
            pflags = {}

            def wait_pflags(keys):
                while not all(k_ in pflags for k_ in keys):
                    yield
            pPBF = [A.alloc("ppbf%d" % i, [N], F32) for i in range(3)]
            pR, pK, pV, pSG, pCSG, pASIG, pKKN, pECW, pENCW = [A.alloc("pq%d" % i, [4, N], F32, nk=4) for i in range(9)]
            pLIN = A.alloc("plin", [N], BF16)
            pGS = A.alloc("pgs", [N], BF16)
            pT = [[A.alloc("pt%d_%d" % (j, i), [N], F32) for i in range(2)] for j in range(4)]
            pL = [[A.alloc("pl%d_%d" % (j, i), [N], F32) for i in range(5)] for j in range(4)]
            pXCB = [A.alloc("pxcb%d" % j, [N], BF16) for j in range(4)]
            dbl = lambda nm, sh, dt, nk=1: [A.alloc("%s%d" % (nm, i), sh, dt, nk=nk) for i in range(2)]
            pAR = dbl("par", [4, 2, N], BF16, 4)
            pBT = dbl("pbt", [4, N], BF16, 4)
            pKT = dbl("pkt", [4, N], BF16, 4)
            pVB = dbl("pvb", [4, N], BF16, 4)
            pPC = dbl("ppc", [4, NCK], F32, 4)
            pG = dbl("pg", [4, N], F32, 4)
            pBON = dbl("pbon", [4, N], F32, 4)
            pHSB = dbl("phsb", [4, N], BF16, 4)
            pYFM = dbl("pyfm", [4, N], F32, 4)
            TOK = A.alloc("ptok", [3, 4, CH], BF16)
            MBS = A.alloc("pmbs", [4, 128], BF16)
            MKS = A.alloc("pmks", [4, 128], BF16)
            XX = [A.alloc("pxx%d" % i, [2, 4, CH], BF16) for i in range(2)]
            XT0 = A.alloc("pxt0", [4, CH], BF16)
            WF = A.alloc("pwf", [4, CH], F32)
            WB = [A.alloc("pwb%d" % i, [4, CH], BF16) for i in range(2)]
            HF = A.alloc("phf", [4, CH], F32)
            HB = A.alloc("phb", [4, CH], BF16)
            HTMP = A.alloc("phtmp", [4, CH], F32)
            HCAR = A.alloc("phcar", [4], F32)
            pZB = A.alloc("pzb", [4, N], BF16, nk=4)
            pMIX = A.alloc("pmix", [8, N], BF16, nk=8)
            pGT = [A.alloc("pgt%d" % i, [N], F32) for i in range(4)]
            print("phase A prompt arena used", A.off, "top", A.top)
            self.memset('pool', HF.t, 0.0, HF.all)
            self.memset('pool', HB.t, 0.0, HB.all)
            self.memset('pool', HCAR.t, 0.0, HCAR.all)
            pxsem = [S.new_dma_sem() for _ in range(3)]
            px1sem = [S.new_dma_sem() for _ in range(3)]
            PMA = [PM[0], PM[1], PST]
            pm_busy = [False, False, False]
            NBLK = T // N

            def acquire():
                while True:
                    for i_ in range(3):
                        if not pm_busy[i_]:
                            pm_busy[i_] = True
                            return i_
                    yield

            def acquire2():
                while True:
                    fr = [i_ for i_ in range(3) if not pm_busy[i_]]
                    if len(fr) >= 2:
                        pm_busy[fr[0]] = True
                        pm_busy[fr[1]] = True
                        return fr[0], fr[1]
                    yield

            def release(i_):
                pm_busy[i_] = False

            class SyncPt:
                def __init__(self, n):
                    self.n = n
                    self.c = 0

            def wait_sync(sp):
                sp.c += 1
                while sp.c < sp.n:
                    yield

            def chain(*gs):
                for g_ in gs:
                    yield from g_

            def window(gens, width):
                gens = list(gens)
                act_ = []
                idx = 0
                while idx < len(gens) or act_:
                    while len(act_) < width and idx < len(gens):
                        act_.append(gens[idx])
                        idx += 1
                    for g_ in list(act_):
                        try:
                            next(g_)
                        except StopIteration:
                            act_.remove(g_)
                        yield

            def rr(streams, weights):
                streams = list(streams)
                alive = [True] * len(streams)
                while any(alive):
                    for si, g_ in enumerate(streams):
                        if not alive[si]:
                            continue
                        for _ in range(weights[si]):
                            try:
                                next(g_)
                            except StopIteration:
                                alive[si] = False
                                break

            def rms_p(src_aps, src_ks, half):
                pi = yield from acquire()
                pstat = PMA[pi]
                rs = RSH[half]
                for k in range(8):
                    sq = SQA[2 * half + k % 2]
                    self.act(sq.t[:, :N], src_aps[k], AF.Square, [src_ks[k]], sq.all)
                    self.mm(pstat.t[:, :N], ONESB, sq.t[:, :N], k == 0, k == 7, sq.all + ONB.all, pstat.all)
                    yield
                self.act(rs.t[:, :N], pstat.t[:, :N], AF.Ln, pstat.all, rs.all, scale=1.0 / D, bias=1e-6)
                release(pi)
                self.act(rs.t[:, :N], rs.t[:, :N], AF.Exp, rs.all, rs.all, scale=-0.5)
                yield

            def rw_proj(b, oc):
                par = b % 2
                last = (b == NBLK - 1)
                xn = pXN[b % 3]
                pi = yield from acquire()
                pm = PMA[pi]
                for k in range(8):
                    self.mm(pm.t[:, :N + 1], W_IN.t[:, k, oc * 128:(oc + 1) * 128], xn.t[:, k, 2:N + 3], k == 0, k == 7,
                            [WK(oc), xn.k[k]], pm.all)
                yield
                pc_ = pPBF[(oc + 2) % 14 % 3]
                self.act(pc_.t[:, :N], pm.t[:, 1:N + 1], AF.Identity, pm.all + OMM.all, pc_.all, scale=OMM.t[:, oc:oc + 1])
                if last:
                    self.cp('act', SHO.t[:, oc, 0:1], pm.t[:, N:N + 1], pm.all, SHO.all)
                yield
                if oc < 12:
                    dst = (pR, pK, pV)[oc // 4]
                    dst_ap, dst_k = dst.t[:, oc % 4, :N], [dst.k[oc % 4]]
                    dt_ = pL[oc % 4][3 + oc // 4] if oc < 8 else pL[oc % 4][0]
                else:
                    dst_ = pL[oc - 12][1]
                    dst_ap, dst_k = dst_.t[:, :N], dst_.all
                    dt_ = pL[oc - 12][2]
                self.stt(dst_ap, pm.t[:, 0:N], V('mu', oc), pc_.t[:, :N], ALU.mult, ALU.add, pm.all + pc_.all + VEC.all, dst_k)
                release(pi)
                yield
                if oc == 12:
                    self.act(pLIN.t[0:64, :N], dst_ap[0:64, :], AF.Tanh, dst_k, pLIN.all)
                    self.cp('pool', pLIN.t[64:128, :N], dst_ap[64:128, :], dst_k, pLIN.all)
                    yield
                if oc == 13:
                    self.act(dst_ap, dst_ap, AF.Tanh, dst_k, dst_k, scale=0.5)
                    self.ts('pool', pGS.t[:, :N], dst_ap, 1.0, ALU.add, dst_k, pGS.all, s2=0.5, op1=ALU.mult)
                    yield

            def rwkv_chain(b, j, sp):
                par = b % 2
                cs = slice(j * 128, (j + 1) * 128)
                t0, t1 = pT[j]
                G_, BON, AR_, BT, KT, VBT, PC = pG[par], pBON[par], pAR[par], pBT[par], pKT[par], pVB[par], pPC[par]
                if b >= 2:
                    yield from wait_pflags([('tail', b - 2)])
                pi = yield from acquire()
                pm = PMA[pi]
                self.mm(pm.t[:, :N], LORA.t[0:64, cs], pLIN.t[0:64, :N], True, True, LORA.all + pLIN.all, pm.all)
                yield
                self.act(pSG.t[:, j, :N], pm.t[:, :N], AF.Tanh, pm.all + DR2.all, [pSG.k[j]], scale=0.5, bias=H2(0, j))
                release(pi)
                yield
                pi = yield from acquire()
                pm = PMA[pi]
                self.mm(pm.t[:, :N], LORA.t[64:128, cs], pLIN.t[64:128, :N], True, True, LORA.all + pLIN.all, pm.all)
                yield
                self.act(pASIG.t[:, j, :N], pm.t[:, :N], AF.Tanh, pm.all + DR2.all, [pASIG.k[j]], scale=0.5, bias=H2(1, j))
                release(pi)
                yield
                pi = yield from acquire()
                pm = PMA[pi]
                self.mm(pm.t[:, :N], GUP.t[:, cs], pGS.t[:, :N], True, True, GUP.all + pGS.all, pm.all)
                yield
                self.cp('act', G_.t[:, j, :N], pm.t[:, :N], pm.all, [G_.k[j]])
                release(pi)
                yield
                t0b = t0.t[:, :].bitcast(BF16)
                self.act(t0b[:, :N], pK.t[:, j, :N], AF.Square, [pK.k[j]] + VEC.all, t0.all, scale=V('k_k', j))
                yield
                pi = yield from acquire()
                pm = PMA[pi]
                self.mm(pm.t[:, :N], BONESB, t0b[:, :N], True, True, t0.all + ONB.all, pm.all)
                yield
                self.ts('dve', t1.t[:, :N], pm.t[:, :N], 2.0 ** -60, ALU.max, pm.all, t1.all)
                release(pi)
                yield
                yield from wait_sync(sp)
                self.act(t1.t[:, :N], t1.t[:, :N], AF.Ln, t1.all, t1.all)
                yield
                self.act(t1.t[:, :N], t1.t[:, :N], AF.Exp, t1.all, t1.all, scale=-0.5)
                yield
                self.stt(pKKN.t[:, j, :N], pK.t[:, j, :N], V('k_k', j), t1.t[:, :N], ALU.mult, ALU.mult,
                         [pK.k[j]] + VEC.all + t1.all, [pKKN.k[j]])
                yield
                self.ts('dve', t0.t[:, :N], pASIG.t[:, j, :N], -1.0, ALU.add, [pASIG.k[j]] + DR2.all, t0.all,
                        s2=H2(2, j), op1=ALU.mult)
                yield
                self.stt(pK.t[:, j, :N], t0.t[:, :N], 1.0, pK.t[:, j, :N], ALU.add, ALU.mult, t0.all + [pK.k[j]], [pK.k[j]])
                yield
                self.stt(t0b[:, :N], pR.t[:, j, :N], V('r_k', j), pK.t[:, j, :N], ALU.mult, ALU.mult,
                         [pR.k[j], pK.k[j]] + VEC.all, t0.all)
                yield
                pi = yield from acquire()
                pm = PMA[pi]
                self.mm(pm.t[:, :N], BONESB, t0b[:, :N], True, True, t0.all + ONB.all, pm.all)
                yield
                self.tt('dve', BON.t[:, j, :N], pm.t[:, :N], pV.t[:, j, :N], ALU.mult, pm.all + [pV.k[j]], [BON.k[j]])
                release(pi)
                yield
                for ci in range(NCK):
                    sl = slice(ci * CH, (ci + 1) * CH)
                    self.scan(pCSG.t[:, j, sl], ONES64, pSG.t[:, j, sl], 0.0, [pSG.k[j]] + CON.all, [pCSG.k[j]], op0=ALU.add)
                    yield
                self.act(pECW.t[:, j, :N], pCSG.t[:, j, :N], AF.Exp, [pCSG.k[j]], [pECW.k[j]], scale=-0.5 * CDEC)
                yield
                self.act(pENCW.t[:, j, :N], pCSG.t[:, j, :N], AF.Exp, [pCSG.k[j]], [pENCW.k[j]], scale=0.5 * CDEC)
                yield
                v3 = lambda b_, a, bnd: b_.t[:, j, :N].rearrange("p (c t) -> p c t", t=CH)[:, :, a:bnd]
                at3 = AR_.t[:, j, 0, :N].rearrange("p (c t) -> p c t", t=CH)
                self.stt(at3[:, :, 1:CH], v3(pKKN, 1, CH), -1.0, v3(pECW, 0, CH - 1), ALU.mult, ALU.mult,
                         [pKKN.k[j], pECW.k[j]], [AR_.k[j]])
                self.ts('pool', at3[:, :, 0:1], v3(pKKN, 0, 1), -1.0, ALU.mult, [pKKN.k[j]], [AR_.k[j]])
                yield
                self.tt('pool', AR_.t[:, j, 1, :N], pR.t[:, j, :N], pECW.t[:, j, :N], ALU.mult, [pR.k[j], pECW.k[j]], [AR_.k[j]])
                yield
                self.stt(t0.t[:, :N], pASIG.t[:, j, :N], 1.0, pKKN.t[:, j, :N], ALU.add, ALU.mult, [pKKN.k[j], pASIG.k[j]], t0.all)
                yield
                self.stt(BT.t[:, j, :N], t0.t[:, :N], 0.5, pENCW.t[:, j, :N], ALU.mult, ALU.mult, t0.all + [pENCW.k[j]], [BT.k[j]])
                yield
                self.tt('pool', KT.t[:, j, :N], pK.t[:, j, :N], pENCW.t[:, j, :N], ALU.mult, [pK.k[j], pENCW.k[j]], [KT.k[j]])
                yield
                self.cp('pool', VBT.t[:, j, :N], pV.t[:, j, :N], [pV.k[j]], [VBT.k[j]])
                self.cp('pool', PC.t[:, j, :], v3(pECW, CH - 1, CH), [pECW.k[j]], [PC.k[j]])
                yield

            def lru_chain(b, j, sp):
                par = b % 2
                last = (b == NBLK - 1)
                xn = pXN[b % 3]
                xc, gxs, gas, a_, uu = pL[j]
                oc = 14 + j
                pi = yield from acquire()
                pm = PMA[pi]
                for k in range(8):
                    self.mm(pm.t[:, :N + 3], W_IN.t[:, k, oc * 128:(oc + 1) * 128], xn.t[:, k, 0:N + 3], k == 0, k == 7,
                            [WK(oc), xn.k[k]], pm.all)
                yield
                self.act(xc.t[:, :N], pm.t[:, 3:N + 3], AF.Identity, pm.all + VEC.all, xc.all, scale=V('lcw', 12 + j), bias=V('lcb', j))
                if last:
                    self.cp('act', LCO.t[:, j, :, 0], pm.t[:, N:N + 3], pm.all, LCO.all)
                yield
                for i in range(3):
                    self.stt(xc.t[:, :N], pm.t[:, i:i + N], V('lcw', 4 * i + j), xc.t[:, :N], ALU.mult, ALU.add,
                             pm.all + VEC.all + xc.all, xc.all)
                    yield
                release(pi)
                xcb = pXCB[j]
                self.cp('pool', xcb.t[:, :N], xc.t[:, :N], xc.all, xcb.all)
                yield
                pi1, pi2 = yield from acquire2()
                pgx, pga = PMA[pi1], PMA[pi2]
                for h2 in range(2):
                    ps_ = slice(64 * h2, 64 * h2 + 64)
                    self.mm(pgx.t[ps_, :N], GXA.t[ps_, 0, j, :], xcb.t[ps_, :N], True, True, GXA.all + xcb.all, pgx.all)
                    self.mm(pga.t[ps_, :N], GXA.t[ps_, 1, j, :], xcb.t[ps_, :N], True, True, GXA.all + xcb.all, pga.all)
                yield
                self.act(gxs.t[:, :N], pgx.t[:, :N], AF.Tanh, pgx.all + DR2.all, gxs.all, scale=0.5, bias=H2(3, j))
                release(pi1)
                self.act(gas.t[:, :N], pga.t[:, :N], AF.Tanh, pga.all + DR2.all, gas.all, scale=0.5, bias=H2(4, j))
                release(pi2)
                yield
                self.act(a_.t[:, :N], gas.t[:, :N], AF.Exp, gas.all + DR2.all, a_.all, scale=H2(5, j), bias=H2(5, j))
                self.act(gas.t[:, :N], gas.t[:, :N], AF.Exp, gas.all + DR2.all, gas.all, scale=H2(6, j), bias=H2(6, j))
                yield
                self.ts('dve', gas.t[:, :N], gas.t[:, :N], 1.0 - 2.0 ** -23, ALU.min, gas.all, gas.all)
                self.stt(uu.t[:, :N], gxs.t[:, :N], 1.0, xc.t[:, :N], ALU.add, ALU.mult, gxs.all + xc.all, uu.all)
                yield
                yield from wait_sync(sp)
                self.act(gas.t[:, :N], gas.t[:, :N], AF.Ln, gas.all, gas.all, scale=-1.0, bias=1.0)
                yield
                self.act(gas.t[:, :N], gas.t[:, :N], AF.Exp, gas.all, gas.all, scale=0.5, bias=math.log(0.5))
                yield
                self.tt('dve', uu.t[:, :N], uu.t[:, :N], gas.t[:, :N], ALU.mult, uu.all + gas.all, uu.all)
                yield
                self.scan(xc.t[:, :N], a_.t[:, :N], uu.t[:, :N], HCAR.t[:, j:j + 1], a_.all + uu.all + HCAR.all, xc.all)
                yield
                self.cp('pool', HCAR.t[:, j:j + 1], xc.t[:, N - 1:N], xc.all, HCAR.all)
                if last:
                    self.cp('pool', LHO.t[:, j, 0:1], xc.t[:, N - 1:N], xc.all, LHO.all)
                self.cp('pool', pHSB[par].t[:, j, :N], xc.t[:, :N], xc.all, [pHSB[par].k[j]])
                yield

            def S1head(b):
                xb, xn = pXB[b % 3], pXN[b % 3]
                self.dma('sp', xb.t[:, :, :N], xTv[:, :, b * N:(b + 1) * N], [], xb.all, pxsem[b % 3])
                yield
                yield from rms_p([xb.t[:, k, :N] for k in range(8)], xb.k, 0)
                if b == 0:
                    self.memset('pool', xn.t[:, :, 0:3], 0.0, xn.all)
                else:
                    self.cp('pool', xn.t[:, :, 0:3], pXN[(b - 1) % 3].t[:, :, N:N + 3], pXN[(b - 1) % 3].all, xn.all)
                for k in range(8):
                    self.stt(xn.t[:, k, 3:N + 3], xb.t[:, k, :N], V('g_pre', k), RSH[0].t[:, :N], ALU.mult, ALU.mult,
                             [xb.k[k]] + VEC.all + RSH[0].all, [xn.k[k]])
                    yield
                pflags[('head', b)] = True

            def S1rest(b):
                yield from wait_pflags([('head', b)])
                yield from window([rw_proj(b, oc) for oc in (12, 13, 0, 1, 2, 3, 4, 5, 6, 7, 8, 9, 10, 11)], 3)
                chains = []
                sp = SyncPt(8)
                for j in range(4):
                    chains.append(lru_chain(b, j, sp))
                    chains.append(rwkv_chain(b, j, sp))
                yield from window(chains, 8)

            def S2(b):
                par = b % 2
                AR_, BT, KT, VBT, PC, YFM = pAR[par], pBT[par], pKT[par], pVB[par], pPC[par], pYFM[par]
                m1b = MASK1.unsqueeze(1).to_broadcast([128, 4, 128])
                m2b = MASK2.unsqueeze(1).to_broadcast([128, 4, CH])
                pa4 = PA.t[:, :].rearrange("p (a b) -> p a b", a=4)
                pb4 = PB_.t[:, :].rearrange("p (a b) -> p a b", a=4)
                px = PX.t[:, :].rearrange("p (s a b) -> p s a b", s=2, a=4)
                pw = PW.t[:, :].rearrange("p (s a b) -> p s a b", s=2, a=4)
                pt = PT.t[:, 0:3 * 4 * CH].rearrange("p (s a b) -> p s a b", s=3, a=4)
                H8 = [(h // 2, slice(64 * (h % 2), 64 * (h % 2) + 64), h % 2) for h in range(8)]
                for ci in range(NCK):
                    sl = slice(ci * CH, (ci + 1) * CH)
                    for si, src in enumerate((BT, KT, VBT)):
                        for j, p_, h2 in H8:
                            self.tr(pt[p_, si, j, :], src.t[p_, j, sl], IDB.t[p_, 64 * h2:64 * h2 + 64], [src.k[j]] + IDB.all, PT.all)
                    self.cp('act', TOK.t, pt, PT.all, TOK.all)
                    yield
                    for j, p_, h2 in H8:
                        self.mm(pa4[p_, j, :], BT.t[p_, j, sl], AR_.t[p_, j, :, sl], True, True, [BT.k[j], AR_.k[j]], PA.all)
                    self.tt('dve', MBS.t, pa4, m1b, ALU.mult, PA.all + CON.all, MBS.all)
                    yield
                    for j, p_, h2 in H8:
                        self.mm(pb4[p_, j, :], KT.t[p_, j, sl], AR_.t[p_, j, :, sl], True, True, [KT.k[j], AR_.k[j]], PB_.all)
                    self.tt('dve', MKS.t, pb4, m1b, ALU.mult, PB_.all + CON.all, MKS.all)
                    yield
                    for j, p_, h2 in H8:
                        self.mm(px[p_, 1, j, :], AR_.t[p_, j, 0, sl], BT.t[p_, j, sl], True, True, [BT.k[j], AR_.k[j]], PX.all)
                    self.tt('dve', XT0.t, px[:, 1], m2b, ALU.mult, PX.all + CON.all, XT0.all)
                    yield
                    for j, p_, h2 in H8:
                        self.mm(pw[p_, 0, j, :], AR_.t[p_, j, 0, sl], HB.t[p_, j, :], True, False, [AR_.k[j]] + HB.all, PW.all)
                        self.mm(pw[p_, 0, j, :], MKS.t[p_, j, 0:CH], TOK.t[p_, 2, j, :], False, True, MKS.all + TOK.all, PW.all)
                    self.cp('act', WF.t, pw[:, 0], PW.all, WF.all)
                    self.cp('dve', WB[0].t, pw[:, 0], PW.all, WB[0].all)
                    yield
                    Xc, XTc = (MBS.t[:, :, 0:CH], MBS.all), (XT0.t, XT0.all)
                    for it in range(6):
                        wb_in, wb_out = WB[it % 2], WB[(it + 1) % 2]
                        for j, p_, h2 in H8:
                            self.mm(pw[p_, 1, j, :], Xc[0][p_, j, :], wb_in.t[p_, j, :], True, True, Xc[1] + wb_in.all, PW.all)
                        if it < 5:
                            xx = XX[it % 2]
                            for j, p_, h2 in H8:
                                self.mm(px[p_, 0, j, :], XTc[0][p_, j, :], Xc[0][p_, j, :], True, True, Xc[1] + XTc[1], PX.all)
                                self.mm(px[p_, 1, j, :], Xc[0][p_, j, :], XTc[0][p_, j, :], True, True, Xc[1] + XTc[1], PX.all)
                        yield
                        self.tt('dve', WF.t, WF.t, pw[:, 1], ALU.add, WF.all + PW.all, WF.all)
                        if it < 5:
                            self.cp('act', xx.t, px, PX.all, xx.all)
                            Xc, XTc = (xx.t[:, 0], xx.all), (xx.t[:, 1], xx.all)
                        yield
                        self.cp('pool', wb_out.t, WF.t, WF.all, wb_out.all)
                        yield
                    UB = WB[0]
                    for j, p_, h2 in H8:
                        self.mm(pa4[p_, j, 0:CH], HB.t[p_, j, :], AR_.t[p_, j, 1, sl], True, False, HB.all + [AR_.k[j]], PA.all)
                        self.mm(pa4[p_, j, 0:CH], UB.t[p_, j, :], MBS.t[p_, j, CH:128], False, False, UB.all + MBS.all, PA.all)
                        self.mm(pa4[p_, j, 0:CH], TOK.t[p_, 2, j, :], MKS.t[p_, j, CH:128], False, True, TOK.all + MKS.all, PA.all)
                    self.cp('act', YFM.t[:, :, sl], pa4[:, :, 0:CH], PA.all, YFM.all)
                    yield
                    for j, p_, h2 in H8:
                        self.mm(pb4[p_, j, 0:CH], TOK.t[p_, 0, j, :], UB.t[p_, j, :], True, False, TOK.all + UB.all, PB_.all)
                        self.mm(pb4[p_, j, 0:CH], TOK.t[p_, 1, j, :], TOK.t[p_, 2, j, :], False, True, TOK.all, PB_.all)
                    self.tt('dve', HTMP.t, pb4[:, :, 0:CH], HF.t, ALU.add, PB_.all + HF.all, HTMP.all)
                    yield
                    pc = PC.t[:, :, ci:ci + 1].to_broadcast([128, 4, CH])
                    self.tt('dve', HF.t, HTMP.t, pc, ALU.mult, HTMP.all + PC.all, HF.all)
                    self.cp('pool', HB.t, HF.t, HF.all, HB.all)
                    yield

            def gn_chain(b, j):
                par = b % 2
                YFM, BON, G_ = pYFM[par], pBON[par], pG[par]
                t0, t1 = pT[j]
                t2 = pL[j][0]
                pi = yield from acquire()
                pm = PMA[pi]
                self.mm(pm.t[:, :N], BONES, YFM.t[:, j, :N], True, True, [YFM.k[j]] + CON.all, pm.all)
                yield
                self.stt(t0.t[:, :N], pm.t[:, :N], -1.0 / 64, YFM.t[:, j, :N], ALU.mult, ALU.add, pm.all + [YFM.k[j]], t0.all)
                release(pi)
                yield
                t1b = t1.t[:, :].bitcast(BF16)
                self.act(t1b[:, :N], t0.t[:, :N], AF.Square, t0.all, t1.all)
                yield
                pi = yield from acquire()
                pm = PMA[pi]
                self.mm(pm.t[:, :N], BONESB, t1b[:, :N], True, True, t1.all + ONB.all, pm.all)
                yield
                self.act(t2.t[:, :N], pm.t[:, :N], AF.Ln, pm.all, t2.all, scale=1.0 / 64, bias=64e-5)
                release(pi)
                yield
                self.act(t2.t[:, :N], t2.t[:, :N], AF.Exp, t2.all, t2.all, scale=-0.5)
                yield
                self.tt('dve', t0.t[:, :N], t0.t[:, :N], t2.t[:, :N], ALU.mult, t0.all + t2.all, t0.all)
                yield
                self.ts('dve', t0.t[:, :N], t0.t[:, :N], V('lnx_w', j), ALU.mult, t0.all + VEC.all, t0.all,
                        s2=V('lnx_b', j), op1=ALU.add)
                yield
                self.tt('pool', t0.t[:, :N], t0.t[:, :N], BON.t[:, j, :N], ALU.add, t0.all + [BON.k[j]], t0.all)
                yield
                self.tt('pool', pZB.t[:, j, :N], t0.t[:, :N], G_.t[:, j, :N], ALU.mult, t0.all + [G_.k[j]], [pZB.k[j]])
                yield

            def gproj(b, oc, gt):
                xn = pXN[b % 3]
                pi = yield from acquire()
                pm = PMA[pi]
                for k in range(8):
                    self.mm(pm.t[:, :N], W_IN.t[:, k, oc * 128:(oc + 1) * 128], xn.t[:, k, 3:N + 3], k == 0, k == 7,
                            [WK(oc), xn.k[k]], pm.all)
                yield
                self.act(gt.t[:, :N], pm.t[:, :N], AF.Tanh, pm.all, gt.all, scale=0.5)
                release(pi)
                yield

            def mix_chain(b, oc):
                par = b % 2
                cs = slice(oc * 128, (oc + 1) * 128)
                ga, gb = pGT[(2 * oc) % 4], pGT[(2 * oc + 1) % 4]
                yield from gproj(b, 18 + oc, ga)
                yield from gproj(b, 26 + oc, gb)
                pi = yield from acquire()
                pm = PMA[pi]
                for j in range(4):
                    self.mm(pm.t[:, :N], WOR.t[:, j, cs], pZB.t[:, j, :N], j == 0, j == 3, WOR.all + [pZB.k[j]], pm.all)
                yield
                self.stt(ga.t[:, :N], ga.t[:, :N], 1.0, pm.t[:, :N], ALU.add, ALU.mult, pm.all + ga.all, ga.all)
                release(pi)
                yield
                pi = yield from acquire()
                pm = PMA[pi]
                for j in range(4):
                    self.mm(pm.t[:, :N], WOL.t[:, j, cs], pHSB[par].t[:, j, :N], j == 0, j == 3, WOL.all + [pHSB[par].k[j]], pm.all)
                yield
                self.stt(gb.t[:, :N], gb.t[:, :N], 1.0, pm.t[:, :N], ALU.add, ALU.mult, pm.all + gb.all, gb.all)
                release(pi)
                yield
                self.tt('pool', pMIX.t[:, oc, :N], ga.t[:, :N], gb.t[:, :N], ALU.add, ga.all + gb.all, [pMIX.k[oc]])
                yield

            def wo_chain(b, oc):
                cs = slice(oc * 128, (oc + 1) * 128)
                mo = pT[oc % 4][oc // 4]
                pi = yield from acquire()
                pm = PMA[pi]
                for k in range(8):
                    self.mm(pm.t[:, :N], WO.t[:, k, cs], pMIX.t[:, k, :N], k == 0, k == 7, WO.all + [pMIX.k[k]], pm.all)
                yield
                self.act(mo.t[:, :N], pm.t[:, :N], AF.Identity, pm.all, mo.all, scale=0.5)
                release(pi)
                yield

            class _MO:
                pass
            MOB = _MO()
            MOB.k = pR.k + pK.k

            def S3front(b):
                yield from window([gn_chain(b, j) for j in range(4)], 4)
                yield from window([mix_chain(b, oc) for oc in range(8)], 2)

            def S3tail(b):
                xb = pXB[b % 3]
                yield from window([wo_chain(b, oc) for oc in range(8)], 4)
                mos = [pT[k % 4][k // 4] for k in range(8)]
                yield from rms_p([m_.t[:, :N] for m_ in mos], [m_.k[0] for m_ in mos], 1)
                for k in range(8):
                    mo = mos[k]
                    self.stt(mo.t[:, :N], mo.t[:, :N], V('g_post', k), RSH[1].t[:, :N], ALU.mult, ALU.mult,
                             mo.all + VEC.all + RSH[1].all, mo.all)
                    self.tt('pool', xb.t[:, k, :N], xb.t[:, k, :N], mo.t[:, :N], ALU.add, [xb.k[k]] + mo.all, [xb.k[k]])
                    yield
                self.dma('sp', x1v[:, :, b * N:(b + 1) * N], xb.t[:, :, :N], xb.all, [X1S], px1sem[b % 3])
                pflags[('tail', b)] = True
                yield

            def par(*gs):
                gs = list(gs)
                while gs:
                    for g_ in list(gs):
                        try:
                            next(g_)
                        except StopIteration:
                            gs.remove(g_)
                        yield

            rr([chain(S1head(0), S1rest(0))], [1])
            for b in range(NBLK):
                later = []
                if b > 0:
                    later.append(S3tail(b - 1))
                if b + 1 < NBLK:
                    later.append(S1rest(b + 1))
                mparts = []
                if b > 0:
                    mparts.append(S3front(b - 1))
                mparts.append(par(*later))
                streams = [chain(*mparts), S2(b)]
                weights = [8, 1]
                if b + 1 < NBLK:
                    streams.append(S1head(b + 1))
                    weights.append(1)
                rr(streams, weights)
            rr([chain(S3front(NBLK - 1), S3tail(NBLK - 1))], [1])
            self.cp('pool', WKP.t, HF.t, HF.all, WKP.all)
            S.barrier()
            A.off = markA
            N_ = NS
            XB = A.alloc("xb", [8, N_], F32, nk=8)
            XN = A.alloc("xn", [8, N_], BF16, nk=8)
            PBF = [A.alloc("pbf%d" % i, [N_ + 1], F32) for i in range(3)]
            DT = [A.alloc("dt%d" % i, [N_], F32) for i in range(3)]
            Q = [A.alloc("q%d" % i, [4, N_], F32, nk=4) for i in range(11)]
            LIN = A.alloc("lin", [N_], BF16)
            GS = A.alloc("gs", [N_], BF16)
            AR_ = A.alloc("ar", [4, 2, N_], BF16, nk=4)
            BT = A.alloc("bt", [4, N_], BF16, nk=4)
            KT = A.alloc("kt", [4, N_], BF16, nk=4)
            VBT = A.alloc("vbt", [4, N_], BF16, nk=4)
            TOK = A.alloc("tok", [3, 4, CH], BF16)
            MBS = A.alloc("mbs", [4, 128], BF16)
            MKS = A.alloc("mks", [4, 128], BF16)
            XX = [A.alloc("xx%d" % i, [2, 4, CH], BF16) for i in range(2)]
            XT0 = A.alloc("xt0", [4, CH], BF16)
            WF = A.alloc("wf", [4, CH], F32)
            WB = [A.alloc("wb%d" % i, [4, CH], BF16) for i in range(2)]
            HF = A.alloc("hf", [4, CH], F32)
            HB = A.alloc("hb", [4, CH], BF16)
            HTMP = A.alloc("htmp", [4, CH], F32)
            XBB = A.alloc("xbb", [4, N_ + 3], F32, nk=4)
            XCB = A.alloc("xcb", [N_], BF16)
            HSB = A.alloc("hsb", [4, N_], BF16, nk=4)
            HCAR = A.alloc("hcar", [4], F32)
            ZB = A.alloc("zb", [4, N_], BF16, nk=4)
            MIXB = A.alloc("mixb", [8, N_], BF16, nk=8)
            GT = [A.alloc("gt%d" % i, [N_], F32) for i in range(2)]
            DTJ = [[A.alloc("dtj%d_%d" % (j_, i), [N_], F32) for i in range(3)] for j_ in range(4)]
            GTJ = [[A.alloc("gtj%d_%d" % (j_, i), [N_], F32) for i in range(2)] for j_ in range(4)]
            XCBJ = [A.alloc("xcbj%d" % j_, [N_], BF16) for j_ in range(4)]
            HS = A.alloc("hs", [4, NS, 64], F32, nk=4)
            SPCS = [[A.alloc("spc%d_%d" % (h_, i), [512], F32) for i in range(5)] for h_ in range(2)]
            SST = {n_: A.alloc(n_, sh, F32) for n_, sh in
                   (("sshift", [14, NS]), ("slconv", [4, 3, NS]), ("slh", [4, NS]))}
            print("phase A arena used", A.off, "top", A.top)

            self.memset('pool', HF.t, 0.0, HF.all)
            self.memset('pool', HB.t, 0.0, HB.all)
            self.memset('pool', HCAR.t, 0.0, HCAR.all)
            self.memset('pool', XBB.t, 0.0, XBB.all)
            ssem = [S.new_dma_sem() for _ in range(4)]
            self.dma('sp', SST["sshift"].t, s_shift.rearrange("p (a b) -> p a b", a=14), [], SST["sshift"].all, ssem[0])
            self.dma('sp', SST["slconv"].t, s_lconv.rearrange("p (a b c) -> p a b c", a=4, b=3), [], SST["slconv"].all, ssem[1])
            self.dma('sp', SST["slh"].t, s_lh.rearrange("p (a b) -> p a b", a=4), [], SST["slh"].all, ssem[2])
            self.dma('sp', HS.t, s_wkv.rearrange("p (a b c) -> p a b c", a=4, b=NS), [], HS.all, ssem[3])

            xsem = S.new_dma_sem()
            x1sem = S.new_dma_sem()
            R_, K_, V_, SG, CSG, ASIG, KKN, ECW, ENCW, G_, BON = Q
            YFM = SG
            MOF = None
            xTv = xT.rearrange("(k p) n -> p k n", p=128)
            x1v = x1s.rearrange("(k p) n -> p k n", p=128)
            yTv = yT.rearrange("(k p) n -> p k n", p=128)

            PMS = [PM[0], PM[1], PW]
            pm_res = {}

            def pm_next():
                while True:
                    for d_ in range(3):
                        i_ = (self.pmi + d_) % 3
                        b_ = PMS[i_]
                        lw = b_.k[0].last_w
                        if i_ in pm_res:
                            if lw is not None and lw != pm_res[i_] and lw[1] != 'pe':
                                del pm_res[i_]
                        if i_ not in pm_res:
                            pm_res[i_] = lw
                            self.pmi = (i_ + 1) % 3
                            return b_
                    assert getattr(_TL, 'sw', None) is not None, "no free PSUM bank in sequential mode"
                    co_switch()

            def rms(src, nch, N, scale_div):
                for k in range(nch):
                    sq = SQ[k % 2]
                    self.act(sq.t[:, :N], src.t[:, k, :N], AF.Square, [src.k[k]], sq.all)
                    self.mm(PST.t[:, :N], ONESF, sq.t[:, :N], k == 0, k == nch - 1, sq.all + CON.all, PST.all)
                self.act(RSTD.t[:, :N], PST.t[:, :N], AF.Sqrt, PST.all, RSTD.all, scale=1.0 / scale_div, bias=1e-6)
                self.recip(RSTD.t[:, :N], RSTD.t[:, :N], RSTD.all, RSTD.all)

            def proj(oc, N):
                pm = pm_next()
                for k in range(8):
                    self.mm(pm.t[:, :N], W_IN.t[:, k, oc * 128:(oc + 1) * 128], XN.t[:, k, :N], k == 0, k == 7,
                            [WK(oc), XN.k[k]], pm.all)
                return pm

            def blockA(c0, N, sample, last):
                self.dma('sp', XB.t[:, :, :N], xTv[:, :, c0:c0 + N], [], XB.all, xsem)
                rms(XB, 8, N, D)
                for k in range(8):
                    self.stt(XN.t[:, k, :N], XB.t[:, k, :N], V('g_pre', k), RSTD.t[:, :N], ALU.mult, ALU.mult,
                             [XB.k[k], VEC.k[0], RSTD.k[0]], [XN.k[k]])
                WA = DT[2]
                for oc in range(14):
                    pm = proj(oc, N)
                    if oc < 12:
                        dst = Q[oc // 4]
                        dst_ap, dst_k = dst.t[:, oc % 4, :N], [dst.k[oc % 4]]
                    else:
                        dst_ap, dst_k = WA.t[:, :N], WA.all
                    dt_ = DT[oc % 2]
                    if not sample:
                        pb = PBF[oc % 3]
                        self.cp('act', pb.t[:, 1:N + 1], pm.t[:, :N], pm.all, pb.all)
                        self.cp('pool', pb.t[:, 0:1], SHO.t[:, oc, 0:1], SHO.all, pb.all)
                        self.cp('pool', SHO.t[:, oc, 0:1], pb.t[:, N:N + 1], pb.all, SHO.all)
                        self.tt('dve', dt_.t[:, :N], pb.t[:, 0:N], pb.t[:, 1:N + 1], ALU.subtract, pb.all, dt_.all)
                        self.stt(dst_ap, dt_.t[:, :N], V('mu', oc), pb.t[:, 1:N + 1], ALU.mult, ALU.add,
                                 dt_.all + pb.all + VEC.all, dst_k)
                    else:
                        pcur = SHO.t[:, oc, 1:1 + NS]
                        self.cp('act', pcur, pm.t[:, :N], pm.all, SHO.all)
                        self.tt('dve', dt_.t[:, :N], SST["sshift"].t[:, oc, :], pcur, ALU.subtract,
                                SHO.all + SST["sshift"].all, dt_.all)
                        self.stt(dst_ap, dt_.t[:, :N], V('mu', oc), pcur, ALU.mult, ALU.add,
                                 dt_.all + SHO.all + VEC.all, dst_k)
                    if oc == 12:
                        self.act(LIN.t[0:64, :N], WA.t[0:64, :N], AF.Tanh, WA.all, LIN.all)
                        self.cp('pool', LIN.t[64:128, :N], WA.t[64:128, :N], WA.all, LIN.all)
                    if oc == 13:
                        self.act(GS.t[:, :N], WA.t[:, :N], AF.Sigmoid, WA.all, GS.all)
                for j in range(4):
                    pm = proj(14 + j, N)
                    if not sample:
                        self.cp('act', XBB.t[:, j, 3:N + 3], pm.t[:, :N], pm.all, [XBB.k[j]])
                    else:
                        self.cp('act', LCO.t[:, j, 2, 1:1 + NS], pm.t[:, :N], pm.all, LCO.all)
                def _body1(j):
                    DT, GT, XCB = DTJ[j], GTJ[j], XCBJ[j]
                    cs = slice(j * 128, (j + 1) * 128)
                    pm = pm_next()
                    self.mm(pm.t[:, :N], LORA.t[0:64, cs], LIN.t[0:64, :N], True, True, LORA.all + LIN.all, pm.all)
                    self.act(SG.t[:, j, :N], pm.t[:, :N], AF.Sigmoid, pm.all + VEC.all, [SG.k[j]], bias=V('w0', j))
                    pm = pm_next()
                    self.mm(pm.t[:, :N], LORA.t[64:128, cs], LIN.t[64:128, :N], True, True, LORA.all + LIN.all, pm.all)
                    self.act(ASIG.t[:, j, :N], pm.t[:, :N], AF.Sigmoid, pm.all + VEC.all, [ASIG.k[j]], bias=V('a0', j))
                    pm = pm_next()
                    self.mm(pm.t[:, :N], GUP.t[:, cs], GS.t[:, :N], True, True, GUP.all + GS.all, pm.all)
                    self.cp('act', G_.t[:, j, :N], pm.t[:, :N], pm.all, [G_.k[j]])
                    t0, t1 = DT[0], DT[1]
                    self.act(t0.t[:, :N], K_.t[:, j, :N], AF.Square, [K_.k[j]] + VEC.all, t0.all, scale=V('k_k', j))
                    pm = pm_next()
                    self.mm(pm.t[:, :N], BONES, t0.t[:, :N], True, True, t0.all + CON.all, pm.all)
                    self.act(t1.t[:, :N], pm.t[:, :N], AF.Sqrt, pm.all, t1.all)
                    self.ts('dve', t1.t[:, :N], t1.t[:, :N], 1e-12, ALU.max, t1.all, t1.all)
                    self.recip(t1.t[:, :N], t1.t[:, :N], t1.all, t1.all)
                    self.stt(KKN.t[:, j, :N], K_.t[:, j, :N], V('k_k', j), t1.t[:, :N], ALU.mult, ALU.mult,
                             [K_.k[j]] + VEC.all + t1.all, [KKN.k[j]])
                    self.ts('dve', t0.t[:, :N], ASIG.t[:, j, :N], -1.0, ALU.add, [ASIG.k[j]] + VEC.all, t0.all,
                            s2=V('k_a', j), op1=ALU.mult)
                    self.stt(K_.t[:, j, :N], t0.t[:, :N], 1.0, K_.t[:, j, :N], ALU.add, ALU.mult,
                             t0.all + [K_.k[j]], [K_.k[j]])
                    self.stt(t0.t[:, :N], R_.t[:, j, :N], V('r_k', j), K_.t[:, j, :N], ALU.mult, ALU.mult,
                             [R_.k[j], K_.k[j]] + VEC.all, t0.all)
                    pm = pm_next()
                    self.mm(pm.t[:, :N], BONES, t0.t[:, :N], True, True, t0.all + CON.all, pm.all)
                    self.tt('dve', BON.t[:, j, :N], pm.t[:, :N], V_.t[:, j, :N], ALU.mult, pm.all + [V_.k[j]], [BON.k[j]])
                co_run([(lambda j=j: _body1(j)) for j in range(4)])
                if not sample:
                    wkv_prompt(N)
                else:
                    wkv_sample()
                def _body2(j):
                    DT, GT, XCB = DTJ[j], GTJ[j], XCBJ[j]
                    xc, gxs, gas, uu = GT[0], DT[0], DT[1], DT[2]
                    if not sample:
                        taps = [XBB.t[:, j, i:i + N] for i in range(4)]
                        tr_ = [XBB.k[j]]
                    else:
                        taps = [SST["slconv"].t[:, j, i, :] for i in range(3)] + [LCO.t[:, j, 2, 1:1 + NS]]
                        tr_ = SST["slconv"].all + LCO.all
                    self.act(xc.t[:, :N], taps[3], AF.Identity, tr_ + VEC.all, xc.all, scale=V('lcw', 12 + j), bias=V('lcb', j))
                    for i in range(3):
                        self.stt(xc.t[:, :N], taps[i], V('lcw', 4 * i + j), xc.t[:, :N], ALU.mult, ALU.add,
                                 tr_ + VEC.all + xc.all, xc.all)
                    if not sample:
                        self.cp('pool', XBB.t[:, j, 0:3], XBB.t[:, j, N:N + 3], [XBB.k[j]], [XBB.k[j]])
                        if last:
                            self.cp('pool', LCO.t[:, j, :, 0], XBB.t[:, j, N:N + 3], [XBB.k[j]], LCO.all)
                    else:
                        self.cp('pool', LCO.t[:, j, 0:2, 1:1 + NS], SST["slconv"].t[:, j, 1:3, :], SST["slconv"].all, LCO.all)
                    self.cp('act', XCB.t[:, :N], xc.t[:, :N], xc.all, XCB.all)
                    pgx, pga = pm_next(), pm_next()
                    for h2 in range(2):
                        ps_ = slice(64 * h2, 64 * h2 + 64)
                        self.mm(pgx.t[ps_, :N], GXA.t[ps_, 0, j, :], XCB.t[ps_, :N], True, True, GXA.all + XCB.all, pgx.all)
                        self.mm(pga.t[ps_, :N], GXA.t[ps_, 1, j, :], XCB.t[ps_, :N], True, True, GXA.all + XCB.all, pga.all)
                    self.act(gxs.t[:, :N], pgx.t[:, :N], AF.Sigmoid, pgx.all + VEC.all, gxs.all, bias=V('gxb', j))
                    self.act(gas.t[:, :N], pga.t[:, :N], AF.Sigmoid, pga.all + VEC.all, gas.all, bias=V('gab', j))
                    a_ = GT[1]
                    self.act(a_.t[:, :N], gas.t[:, :N], AF.Exp, gas.all + DER.all, a_.all, scale=DER.t[:, j:j + 1])
                    self.act(gas.t[:, :N], gas.t[:, :N], AF.Exp, gas.all + DER.all, gas.all, scale=DER.t[:, 4 + j:5 + j])
                    self.ts('dve', gas.t[:, :N], gas.t[:, :N], -1.0, ALU.mult, gas.all, gas.all, s2=1.0, op1=ALU.add)
                    self.act(gas.t[:, :N], gas.t[:, :N], AF.Sqrt, gas.all, gas.all)
                    self.tt('dve', uu.t[:, :N], gxs.t[:, :N], xc.t[:, :N], ALU.mult, gxs.all + xc.all, uu.all)
                    self.tt('dve', uu.t[:, :N], uu.t[:, :N], gas.t[:, :N], ALU.mult, uu.all + gas.all, uu.all)
                    hs_ = xc
                    if not sample:
                        self.scan(hs_.t[:, :N], a_.t[:, :N], uu.t[:, :N], HCAR.t[:, j:j + 1], a_.all + uu.all + HCAR.all, hs_.all)
                        self.cp('pool', HCAR.t[:, j:j + 1], hs_.t[:, N - 1:N], hs_.all, HCAR.all)
                        if last:
                            self.cp('pool', LHO.t[:, j, 0:1], hs_.t[:, N - 1:N], hs_.all, LHO.all)
                    else:
                        self.tt('dve', hs_.t[:, :N], a_.t[:, :N], SST["slh"].t[:, j, :], ALU.mult, a_.all + SST["slh"].all, hs_.all)
                        self.tt('dve', hs_.t[:, :N], hs_.t[:, :N], uu.t[:, :N], ALU.add, hs_.all + uu.all, hs_.all)
                        self.cp('pool', LHO.t[:, j, 1:1 + NS], hs_.t[:, :N], hs_.all, LHO.all)
                    self.cp('act', HSB.t[:, j, :N], hs_.t[:, :N], hs_.all, [HSB.k[j]])
                co_run([(lambda j=j: _body2(j)) for j in range(4)])
                def _body3(j):
                    DT, GT, XCB = DTJ[j], GTJ[j], XCBJ[j]
                    t0, t1, t2 = DT[0], DT[1], DT[2]
                    pm = pm_next()
                    self.mm(pm.t[:, :N], BONES, YFM.t[:, j, :N], True, True, [YFM.k[j]] + CON.all, pm.all)
                    self.stt(t0.t[:, :N], pm.t[:, :N], -1.0 / 64, YFM.t[:, j, :N], ALU.mult, ALU.add, pm.all + [YFM.k[j]], t0.all)
                    self.act(t1.t[:, :N], t0.t[:, :N], AF.Square, t0.all, t1.all)
                    pm = pm_next()
                    self.mm(pm.t[:, :N], BONES, t1.t[:, :N], True, True, t1.all + CON.all, pm.all)
                    self.act(t2.t[:, :N], pm.t[:, :N], AF.Sqrt, pm.all, t2.all, scale=1.0 / 64, bias=64e-5)
                    self.recip(t2.t[:, :N], t2.t[:, :N], t2.all, t2.all)
                    self.tt('dve', t0.t[:, :N], t0.t[:, :N], t2.t[:, :N], ALU.mult, t0.all + t2.all, t0.all)
                    self.ts('dve', t0.t[:, :N], t0.t[:, :N], V('lnx_w', j), ALU.mult, t0.all + VEC.all, t0.all,
                            s2=V('lnx_b', j), op1=ALU.add)
                    self.tt('dve', t0.t[:, :N], t0.t[:, :N], BON.t[:, j, :N], ALU.add, t0.all + [BON.k[j]], t0.all)
                    self.tt('dve', ZB.t[:, j, :N], t0.t[:, :N], G_.t[:, j, :N], ALU.mult, t0.all + [G_.k[j]], [ZB.k[j]])
                co_run([(lambda j=j: _body3(j)) for j in range(4)])
                def _body4(oc):
                    DT, GT = DTJ[oc % 4], GTJ[oc % 4]
                    cs = slice(oc * 128, (oc + 1) * 128)
                    pga = proj(18 + oc, N)
                    self.act(GT[0].t[:, :N], pga.t[:, :N], AF.Sigmoid, pga.all, GT[0].all)
                    pgb = proj(26 + oc, N)
                    self.act(GT[1].t[:, :N], pgb.t[:, :N], AF.Sigmoid, pgb.all, GT[1].all)
                    poa = pm_next()
                    for j in range(4):
                        self.mm(poa.t[:, :N], WOR.t[:, j, cs], ZB.t[:, j, :N], j == 0, j == 3, WOR.all + [ZB.k[j]], poa.all)
                    self.tt('dve', GT[0].t[:, :N], poa.t[:, :N], GT[0].t[:, :N], ALU.mult, poa.all + GT[0].all, GT[0].all)
                    pob = pm_next()
                    for j in range(4):
                        self.mm(pob.t[:, :N], WOL.t[:, j, cs], HSB.t[:, j, :N], j == 0, j == 3, WOL.all + [HSB.k[j]], pob.all)
                    self.tt('dve', GT[1].t[:, :N], pob.t[:, :N], GT[1].t[:, :N], ALU.mult, pob.all + GT[1].all, GT[1].all)
                    self.tt('dve', MIXB.t[:, oc, :N], GT[0].t[:, :N], GT[1].t[:, :N], ALU.add, GT[0].all + GT[1].all, [MIXB.k[oc]])
                co_run([(lambda oc=oc: _body4(oc)) for oc in range(4)])
                co_run([(lambda oc=oc: _body4(oc)) for oc in range(4, 8)])
                MO = [R_, K_]
                def _body5(oc):
                    DT, GT = DTJ[oc % 4], GTJ[oc % 4]
                    cs = slice(oc * 128, (oc + 1) * 128)
                    pm = pm_next()
                    for k in range(8):
                        self.mm(pm.t[:, :N], WO.t[:, k, cs], MIXB.t[:, k, :N], k == 0, k == 7, WO.all + [MIXB.k[k]], pm.all)
                    mo = MO[oc // 4]
                    self.cp('act', mo.t[:, oc % 4, :N], pm.t[:, :N], pm.all, [mo.k[oc % 4]])
                co_run([(lambda oc=oc: _body5(oc)) for oc in range(4)])
                co_run([(lambda oc=oc: _body5(oc)) for oc in range(4, 8)])
                for k in range(8):
                    mo = MO[k // 4]
                    sq = SQ[k % 2]
                    self.act(sq.t[:, :N], mo.t[:, k % 4, :N], AF.Square, [mo.k[k % 4]], sq.all)
                    self.mm(PST.t[:, :N], ONESF, sq.t[:, :N], k == 0, k == 7, sq.all + CON.all, PST.all)
                self.act(RSTD.t[:, :N], PST.t[:, :N], AF.Sqrt, PST.all, RSTD.all, scale=1.0 / D, bias=1e-6)
                self.recip(RSTD.t[:, :N], RSTD.t[:, :N], RSTD.all, RSTD.all)
                for k in range(8):
                    mo = MO[k // 4]
                    self.stt(mo.t[:, k % 4, :N], mo.t[:, k % 4, :N], V('g_post', k), RSTD.t[:, :N], ALU.mult, ALU.mult,
                             [mo.k[k % 4]] + VEC.all + RSTD.all, [mo.k[k % 4]])
                    self.tt('dve', XB.t[:, k, :N], XB.t[:, k, :N], mo.t[:, k % 4, :N], ALU.add, [XB.k[k], mo.k[k % 4]], [XB.k[k]])
                self.dma('sp', x1v[:, :, c0:c0 + N], XB.t[:, :, :N], XB.all, [X1S], x1sem)

            def wkv_prompt(N):
                nchk = N // CH
                c = CDEC
                for j in range(4):
                    for ci in range(nchk):
                        sl = slice(ci * CH, (ci + 1) * CH)
                        self.scan(CSG.t[:, j, sl], ONES64, SG.t[:, j, sl], 0.0, [SG.k[j]] + CON.all, [CSG.k[j]])
                    self.act(ECW.t[:, j, :N], CSG.t[:, j, :N], AF.Exp, [CSG.k[j]], [ECW.k[j]], scale=-c)
                    self.act(ENCW.t[:, j, :N], CSG.t[:, j, :N], AF.Exp, [CSG.k[j]], [ENCW.k[j]], scale=c)
                    v3 = lambda b_, a, bnd: b_.t[:, j, :N].rearrange("p (c t) -> p c t", t=CH)[:, :, a:bnd]
                    at3 = AR_.t[:, j, 0, :N].rearrange("p (c t) -> p c t", t=CH)
                    self.stt(at3[:, :, 1:CH], v3(KKN, 1, CH), -1.0, v3(ECW, 0, CH - 1), ALU.mult, ALU.mult,
                             [KKN.k[j], ECW.k[j]], [AR_.k[j]])
                    self.ts('dve', at3[:, :, 0:1], v3(KKN, 0, 1), -1.0, ALU.mult, [KKN.k[j]], [AR_.k[j]])
                    self.tt('dve', AR_.t[:, j, 1, :N], R_.t[:, j, :N], ECW.t[:, j, :N], ALU.mult, [R_.k[j], ECW.k[j]], [AR_.k[j]])
                    t0 = DT[0]
                    self.tt('dve', t0.t[:, :N], KKN.t[:, j, :N], ASIG.t[:, j, :N], ALU.mult, [KKN.k[j], ASIG.k[j]], t0.all)
                    self.tt('dve', BT.t[:, j, :N], t0.t[:, :N], ENCW.t[:, j, :N], ALU.mult, t0.all + [ENCW.k[j]], [BT.k[j]])
                    self.tt('dve', KT.t[:, j, :N], K_.t[:, j, :N], ENCW.t[:, j, :N], ALU.mult, [K_.k[j], ENCW.k[j]], [KT.k[j]])
                    self.cp('act', VBT.t[:, j, :N], V_.t[:, j, :N], [V_.k[j]], [VBT.k[j]])
                m1b = MASK1.unsqueeze(1).to_broadcast([128, 4, 128])
                m2b = MASK2.unsqueeze(1).to_broadcast([128, 4, CH])
                pa4 = PA.t[:, :].rearrange("p (a b) -> p a b", a=4)
                pb4 = PB_.t[:, :].rearrange("p (a b) -> p a b", a=4)
                px = PX.t[:, :].rearrange("p (s a b) -> p s a b", s=2, a=4)
                pw = PW.t[:, :].rearrange("p (s a b) -> p s a b", s=2, a=4)
                pt = PT.t[:, 0:3 * 4 * CH].rearrange("p (s a b) -> p s a b", s=3, a=4)
                for ci in range(nchk):
                    sl = slice(ci * CH, (ci + 1) * CH)
                    allk = lambda b_: b_.all
                    for si, src in enumerate((BT, KT, VBT)):
                        for h in range(8):
                            j, h2 = h // 2, h % 2
                            p_ = slice(64 * h2, 64 * h2 + 64)
                            self.tr(pt[p_, si, j, :], src.t[p_, j, sl], IDB.t[p_, 64 * h2:64 * h2 + 64], [src.k[j]] + IDB.all, PT.all)
                    self.cp('act', TOK.t, pt, PT.all, TOK.all)
                    for h in range(8):
                        j, h2 = h // 2, h % 2
                        p_ = slice(64 * h2, 64 * h2 + 64)
                        self.mm(pa4[p_, j, :], BT.t[p_, j, sl], AR_.t[p_, j, :, sl], True, True, [BT.k[j], AR_.k[j]], PA.all)
                        self.mm(pb4[p_, j, :], KT.t[p_, j, sl], AR_.t[p_, j, :, sl], True, True, [KT.k[j], AR_.k[j]], PB_.all)
                        self.mm(px[p_, 1, j, :], AR_.t[p_, j, 0, sl], BT.t[p_, j, sl], True, True, [BT.k[j], AR_.k[j]], PX.all)
                    self.tt('dve', MBS.t, pa4, m1b, ALU.mult, PA.all + CON.all, MBS.all)
                    self.tt('dve', MKS.t, pb4, m1b, ALU.mult, PB_.all + CON.all, MKS.all)
                    self.tt('dve', XT0.t, px[:, 1], m2b, ALU.mult, PX.all + CON.all, XT0.all)
                    for h in range(8):
                        j, h2 = h // 2, h % 2
                        p_ = slice(64 * h2, 64 * h2 + 64)
                        self.mm(pw[p_, 0, j, :], AR_.t[p_, j, 0, sl], HB.t[p_, j, :], True, False, [AR_.k[j]] + HB.all, PW.all)
                        self.mm(pw[p_, 0, j, :], MKS.t[p_, j, 0:CH], TOK.t[p_, 2, j, :], False, True, MKS.all + TOK.all, PW.all)
                    self.cp('act', WF.t, pw[:, 0], PW.all, WF.all)
                    self.cp('dve', WB[0].t, pw[:, 0], PW.all, WB[0].all)
                    Xc, XTc = (MBS.t[:, :, 0:CH], MBS.all), (XT0.t, XT0.all)
                    for it in range(6):
                        wb_in, wb_out = WB[it % 2], WB[(it + 1) % 2]
                        for h in range(8):
                            j, h2 = h // 2, h % 2
                            p_ = slice(64 * h2, 64 * h2 + 64)
                            self.mm(pw[p_, 1, j, :], Xc[0][p_, j, :], wb_in.t[p_, j, :], True, True, Xc[1] + wb_in.all, PW.all)
                        if it < 5:
                            xx = XX[it % 2]
                            for h in range(8):
                                j, h2 = h // 2, h % 2
                                p_ = slice(64 * h2, 64 * h2 + 64)
                                self.mm(px[p_, 0, j, :], XTc[0][p_, j, :], Xc[0][p_, j, :], True, True, Xc[1] + XTc[1], PX.all)
                                self.mm(px[p_, 1, j, :], Xc[0][p_, j, :], XTc[0][p_, j, :], True, True, Xc[1] + XTc[1], PX.all)
                        self.tt('dve', WF.t, WF.t, pw[:, 1], ALU.add, WF.all + PW.all, WF.all)
                        self.cp('act', wb_out.t, WF.t, WF.all, wb_out.all)
                        if it < 5:
                            self.cp('act', xx.t, px, PX.all, xx.all)
                            Xc, XTc = (xx.t[:, 0], xx.all), (xx.t[:, 1], xx.all)
                    UB = WB[0]
                    for h in range(8):
                        j, h2 = h // 2, h % 2
                        p_ = slice(64 * h2, 64 * h2 + 64)
                        self.mm(pa4[p_, j, 0:CH], HB.t[p_, j, :], AR_.t[p_, j, 1, sl], True, False, HB.all + [AR_.k[j]], PA.all)
                        self.mm(pa4[p_, j, 0:CH], UB.t[p_, j, :], MBS.t[p_, j, CH:128], False, False, UB.all + MBS.all, PA.all)
                        self.mm(pa4[p_, j, 0:CH], TOK.t[p_, 2, j, :], MKS.t[p_, j, CH:128], False, True, TOK.all + MKS.all, PA.all)
                    self.cp('act', YFM.t[:, :, sl], pa4[:, :, 0:CH], PA.all, YFM.all)
                    for h in range(8):
                        j, h2 = h // 2, h % 2
                        p_ = slice(64 * h2, 64 * h2 + 64)
                        self.mm(pb4[p_, j, 0:CH], TOK.t[p_, 0, j, :], UB.t[p_, j, :], True, False, TOK.all + UB.all, PB_.all)
                        self.mm(pb4[p_, j, 0:CH], TOK.t[p_, 1, j, :], TOK.t[p_, 2, j, :], False, True, TOK.all, PB_.all)
                    self.tt('dve', HTMP.t, pb4[:, :, 0:CH], HF.t, ALU.add, PB_.all + HF.all, HTMP.all)
                    pc = ECW.t[:, :, ci * CH + CH - 1:ci * CH + CH].to_broadcast([128, 4, CH])
                    self.tt('dve', HF.t, HTMP.t, pc, ALU.mult, HTMP.all + ECW.all, HF.all)
                    self.cp('act', HB.t, HF.t, HF.all, HB.all)

            def wkv_sample():
                N = NS
                WD, BV = ECW, ENCW
                for j in range(4):
                    self.act(WD.t[:, j, :N], SG.t[:, j, :N], AF.Exp, [SG.k[j]], [WD.k[j]], scale=-CDEC)
                    self.tt('dve', BV.t[:, j, :N], KKN.t[:, j, :N], ASIG.t[:, j, :N], ALU.mult, [KKN.k[j], ASIG.k[j]], [BV.k[j]])
                i64b = I64.unsqueeze(1).to_broadcast([128, 8, 64])
                ov = o_wkvs.rearrange("p (a b c) -> p a b c", a=4, b=NS)

                def piece(j, half, bk, spc, sems, pi_):
                    PA_, PB2, PX_ = bk
                    ns = slice(half * 8, half * 8 + 8)
                    bc = lambda b_: b_.t[:, j, ns].unsqueeze(2).to_broadcast([128, 8, 64])
                    hs3 = HS.t[:, j, ns, :]
                    r3 = lambda s_: s_.t[:, :].rearrange("p (a b) -> p a b", a=8)
                    P1, VD, HN, TMP = spc[0], spc[1], spc[2 + pi_ % 2], spc[4]
                    p1, vd, hn, tmp = r3(P1), r3(VD), r3(HN), r3(TMP)
                    self.stt(p1, hs3, -1.0, bc(KKN), ALU.mult, ALU.mult, [HS.k[j], KKN.k[j]], P1.all)
                    self.mm(PA_.t[:, :], BONES, P1.t[:, :], True, True, P1.all + CON.all, PA_.all)
                    self.tt('pool', vd, bc(V_), i64b, ALU.mult, [V_.k[j]] + CON.all, VD.all)
                    self.mm(PB2.t[:, :], BONES, VD.t[:, :], True, True, VD.all + CON.all, PB2.all)
                    self.tt('pool', hn, hs3, bc(WD), ALU.mult, [HS.k[j], WD.k[j]], HN.all)
                    self.tt('dve', tmp, r3(PA_), bc(BV), ALU.mult, PA_.all + [BV.k[j]], TMP.all)
                    self.tt('pool', hn, hn, tmp, ALU.add, HN.all + TMP.all, HN.all)
                    self.tt('dve', tmp, r3(PB2), bc(K_), ALU.mult, PB2.all + [K_.k[j]], TMP.all)
                    self.tt('pool', hn, hn, tmp, ALU.add, HN.all + TMP.all, HN.all)
                    self.dma('sp', ov[:, j, ns, :], hn, HN.all, [], sems[pi_ % 2])
                    self.tt('pool', p1, hn, bc(R_), ALU.mult, HN.all + [R_.k[j]], P1.all)
                    self.mm(PX_.t[:, :], BONES, P1.t[:, :], True, True, P1.all + CON.all, PX_.all)
                    self.tt('dve', tmp, r3(PX_), i64b, ALU.mult, PX_.all + CON.all, TMP.all)
                    self.reduce(YFM.t[:, j, ns], tmp, TMP.all, [YFM.k[j]])

                def runner(half, bk, spc, sems):
                    for j in range(4):
                        piece(j, half, bk, spc, sems, j)
                co_run([lambda: runner(0, (PA, PB_, PX), SPCS[0], self.wkvs_sems[0:2]),
                        lambda: runner(1, (PM[0], PM[1], PST), SPCS[1], self.wkvs_sems[2:4])])

            self.wkvs_sems = [S.new_dma_sem() for _ in range(4)]
            blockA(T, NS, True, False)

            S.barrier()
            A.off, A.top = mark
            WUP = A.alloc("wup", [8, 2 * DFF], BF16, nk=8)
            WDN = A.alloc("wdn", [22, D], BF16, nk=22)
            N_ = NBB
            XB2 = [A.alloc("xb2_%d" % i, [8, N_], F32, nk=8) for i in range(2)]
            XN2 = [A.alloc("xn2_%d" % i, [8, N_ + 2], BF16, nk=8) for i in range(2)]
            UU = [A.alloc("uu%d" % i, [N_], F32) for i in range(10)]
            HM = A.alloc("hm", [22, N_], BF16, nk=22)
            FO = A.alloc("fo", [8, N_], F32, nk=8)
            SFC = A.alloc("sfc", [44, 2, NS], F32)
            FCO = A.alloc("fco", [44, 2, 17], F32)
            print("phase B arena used", A.off, "top", A.top)
            PMB = [Buf("pmb%d" % i, ps[i][:, :], 1, True) for i in (0, 1, 3, 4, 5, 6)]
            PDN = Buf("pdn", pst[:, :].bitcast(F32), 1, True)
            wsb = [S.new_dma_sem() for _ in range(10)]
            wuv = wup.rearrange("(k p) n -> p k n", p=128)
            WUPT = {}
            qi = 0
            for g_ in range(4):
                for half in range(2):
                    c0_ = half * DFF + 768 * g_
                    c1_ = half * DFF + min(768 * (g_ + 1), DFF)
                    WUPT[(g_, half)] = Tk("wupt%d_%d" % (g_, half))
                    self.dma('pool', WUP.t[:, :, c0_:c1_], wuv[:, :, c0_:c1_], [], [WUPT[(g_, half)]], wsb[qi])
                    qi += 1
            wdv = wdn.rearrange("(k p) n -> p k n", p=128)
            self.dma('pool', WDN.t[:, 0:11, :], wdv[:, 0:11, :], [], WDN.k[0:11], wsb[8])
            self.dma('pool', WDN.t[:, 11:22, :], wdv[:, 11:22, :], [], WDN.k[11:22], wsb[9])
            sfs = S.new_dma_sem()
            self.dma('sp', SFC.t, s_fconv.rearrange("p (a b c) -> p a b c", a=44, b=2), [], SFC.all, sfs)
            self.memset('pool', FCO.t, 0.0, FCO.all)
            x2sem = [S.new_dma_sem() for _ in range(2)]
            ysem = S.new_dma_sem()
            blocks = [(bi * NBB, NBB, False) for bi in range(T // NBB)] + [(T, NS, True)]
            nb_ = len(blocks)

            def rms_g(src, N):
                for k in range(8):
                    sq = SQ[k % 2]
                    sqb = sq.t[:, :].bitcast(BF16)
                    self.act(sqb[:, :N], src.t[:, k, :N], AF.Square, [src.k[k]], sq.all)
                    self.mm(PST.t[:, :N], ONESB, sqb[:, :N], k == 0, k == 7, sq.all + ONB.all, PST.all)
                    yield
                self.act(RSTD.t[:, :N], PST.t[:, :N], AF.Ln, PST.all, RSTD.all, scale=1.0 / D, bias=1e-6)
                self.act(RSTD.t[:, :N], RSTD.t[:, :N], AF.Exp, RSTD.all, RSTD.all, scale=-0.5)
                yield

            def P1(b):
                c0, N, sample = blocks[b]
                xb, xn = XB2[b % 2], XN2[b % 2]
                self.dma('sp', xb.t[:, :, :N], x1v[:, :, c0:c0 + N], [X1S], xb.all, x2sem[b % 2])
                yield
                yield from rms_g(xb, N)
                if b == 0:
                    self.memset('pool', xn.t[:, :, 0:2], 0.0, xn.all)
                elif not sample:
                    self.cp('pool', xn.t[:, :, 0:2], XN2[(b - 1) % 2].t[:, :, NBB:NBB + 2], XN2[(b - 1) % 2].all, xn.all)
                for k in range(8):
                    self.stt(xn.t[:, k, 2:N + 2], xb.t[:, k, :N], V('g_pre2', k), RSTD.t[:, :N], ALU.mult, ALU.mult,
                             [xb.k[k]] + VEC.all + RSTD.all, [xn.k[k]])
                    yield
                bflags[('p1', b)] = True

            pmb_busy = [False] * 6

            def acq_b():
                while True:
                    for i_ in range(6):
                        if not pmb_busy[i_]:
                            pmb_busy[i_] = True
                            return i_
                    yield

            def up_chunk_g(b, oc, pm, out_u):
                c0, N, sample = blocks[b]
                pbi = yield from acq_b()
                pm = PMB[pbi]
                xn = XN2[b % 2]
                last = (b == nb_ - 2)
                NN = N if sample else N + 2
                lo = 2 if sample else 0
                for k in range(8):
                    self.mm(pm.t[:, :NN], WUP.t[:, k, oc * 128:(oc + 1) * 128], xn.t[:, k, lo:lo + NN], k == 0, k == 7,
                            [WUPT[((oc % 22) // 6, oc // 22)], xn.k[k]], pm.all)
                yield
                if not sample:
                    taps = [pm.t[:, i:i + N] for i in range(3)]
                    tr_ = pm.all
                    if last:
                        self.cp('act', FCO.t[:, oc, :, 0], pm.t[:, N:N + 2], pm.all, FCO.all)
                else:
                    self.cp('act', FCO.t[:, oc, 1, 1:1 + NS], pm.t[:, :N], pm.all, FCO.all)
                    self.cp('pool', FCO.t[:, oc, 0, 1:1 + NS], SFC.t[:, oc, 1, :], SFC.all, FCO.all)
                    taps = [SFC.t[:, oc, 0, :], SFC.t[:, oc, 1, :], FCO.t[:, oc, 1, 1:1 + NS]]
                    tr_ = SFC.all + FCO.all
                self.act(out_u.t[:, :N], taps[2], AF.Identity, tr_ + VEC.all, out_u.all, scale=V('fcw', 88 + oc), bias=V('fcb', oc))
                yield
                for i in (1, 0):
                    self.stt(out_u.t[:, :N], taps[i], V('fcw', 44 * i + oc), out_u.t[:, :N], ALU.mult, ALU.add,
                             tr_ + VEC.all + out_u.all, out_u.all)
                    if i == 0:
                        pmb_busy[pbi] = False
                    yield

            hms_v = XN2[0].t[:, 0:2, 32:208].rearrange("p a (b c) -> p a b c", b=11)
            HMSK = [Tk("hms%d" % i_) for i_ in range(22)]
            bflags = {}
            uu_seq = [0]

            def hm_of(b, i):
                if blocks[b][2]:
                    return hms_v[:, i // 11, i % 11, :], HMSK[i]
                return HM.t[:, i, :blocks[b][1]], HM.k[i]

            def pair_g(b, i):
                c0, N, sample = blocks[b]
                if sample:
                    while ('p1', b) not in bflags:
                        yield
                q_ = uu_seq[0]
                uu_seq[0] += 1
                ug, uv = UU[(2 * q_) % 10], UU[(2 * q_ + 1) % 10]
                yield from up_chunk_g(b, i, None, ug)
                self.act(ug.t[:, :N], ug.t[:, :N], AF.Gelu_apprx_tanh, ug.all, ug.all)
                yield
                yield from up_chunk_g(b, 22 + i, None, uv)
                hm_ap, hm_k = hm_of(b, i)
                self.tt('pool', hm_ap, ug.t[:, :N], uv.t[:, :N], ALU.mult, ug.all + uv.all, [hm_k])
                yield

            def down_g(b):
                c0, N, sample = blocks[b]
                for oc in range(8):
                    pm = PDN
                    for i in range(22):
                        hm_ap, hm_k = hm_of(b, i)
                        self.mm(pm.t[:, :N], WDN.t[:, i, oc * 128:(oc + 1) * 128], hm_ap, i == 0, i == 21,
                                [WDN.k[i], hm_k], pm.all)
                    self.cp('act', FO.t[:, oc, :N], pm.t[:, :N], pm.all, [FO.k[oc]])
                    yield

            def tail_g(b):
                c0, N, sample = blocks[b]
                xb = XB2[b % 2]
                yield from rms_g(FO, N)
                for k in range(8):
                    self.stt(FO.t[:, k, :N], FO.t[:, k, :N], V('g_post2', k), RSTD.t[:, :N], ALU.mult, ALU.mult,
                             [FO.k[k]] + VEC.all + RSTD.all, [FO.k[k]])
                    self.tt('pool', FO.t[:, k, :N], FO.t[:, k, :N], xb.t[:, k, :N], ALU.add, [FO.k[k], xb.k[k]], [FO.k[k]])
                    yield
                self.dma('sp', yTv[:, :, c0:c0 + N], FO.t[:, :, :N], FO.all, [], ysem)
                yield

            def chain(*gs):
                for g in gs:
                    yield from g

            def window(gens, width):
                gens = list(gens)
                act_ = []
                idx = 0
                while idx < len(gens) or act_:
                    if len(act_) < width and idx < len(gens):
                        act_.append(gens[idx])
                        idx += 1
                    for g in list(act_):
                        try:
                            next(g)
                        except StopIteration:
                            act_.remove(g)
                        yield

            def rr(streams, weights):
                streams = list(streams)
                alive = [True] * len(streams)
                while any(alive):
                    for si, g in enumerate(streams):
                        if not alive[si]:
                            continue
                        for _ in range(weights[si]):
                            try:
                                next(g)
                            except StopIteration:
                                alive[si] = False
                                break

            rr([P1(0)], [1])
            for b in range(nb_ - 1):
                plist = [pair_g(b, i) for i in range(22)]
                if b == nb_ - 2:
                    plist += [pair_g(nb_ - 1, i) for i in range(22)]
                main = [window(plist, 5)]
                if b > 0:
                    main = [down_g(b - 1)] + main
                side = []
                if b > 0:
                    side.append(tail_g(b - 1))
                if b + 1 < nb_:
                    side.append(P1(b + 1))
                rr([chain(*main), chain(*side)], [12, 1])
            rr([chain(down_g(nb_ - 2), tail_g(nb_ - 2), down_g(nb_ - 1), tail_g(nb_ - 1))], [1])

            osem = [S.new_dma_sem() for _ in range(5)]
            self.dma('sp', o_shift.rearrange("p (a b) -> p a b", a=14), SHO.t, SHO.all, [], osem[0])
            self.dma('sp', o_lconv.rearrange("p (a b c) -> p a b c", a=4, b=3), LCO.t, LCO.all, [], osem[1])
            self.dma('sp', o_lh.rearrange("p (a b) -> p a b", a=4), LHO.t, LHO.all, [], osem[2])
            self.dma('sp', o_fconv.rearrange("p (a b c) -> p a b c", a=44, b=2), FCO.t, FCO.all, [], osem[3])
            self.dma('sp', o_wkvp.rearrange("p (a b) -> p a b", a=4), WKP.t, WKP.all, [], osem[4])
            S.finish([])
            cnt = S.emit()
            print("instr counts", cnt)
        return nc


def _colpack(v):
    v = np.asarray(v, np.float32).reshape(-1)
    n = v.shape[0] // 128
    return np.ascontiguousarray(v.reshape(n, 128).T)


def _consts():
    c = np.zeros((128, NCONST), np.float32)
    p = np.arange(128)[:, None]
    q = np.arange(128)[None, :]
    c[:, C_ONES:C_ONES + 128] = 1.0
    c[:, C_BONES:C_BONES + 128] = (p // 64 == q // 64)
    c[:, C_IDENT:C_IDENT + 128] = (p == q)
    s = p % 64
    t = q % 64
    m1 = np.where(q < 64, s < t, s <= t)
    c[:, C_MASK1:C_MASK1 + 128] = m1
    q64 = np.arange(64)[None, :]
    c[:, C_MASK2:C_MASK2 + 64] = (s > q64)
    c[:, C_I64:C_I64 + 64] = (s == q64)
    return c


_NC_CACHE = {}


def _get_nc(debug=None):
    key = tuple(debug or [])
    if key not in _NC_CACHE:
        _NC_CACHE[key] = Builder(debug).build()
    return _NC_CACHE[key]


def _prep_inputs(inp):
    f = lambda a: np.ascontiguousarray(np.asarray(a, np.float32))
    g = lambda n: f(inp[n])[0]
    vec = np.zeros((128, NVEC), np.float32)

    def put(name, arr):
        a = _colpack(arr)
        vec[:, VOFF[name]:VOFF[name] + a.shape[1]] = a
    put('g_pre', g('norm_pre_mix'))
    put('g_post', g('norm_post_mix'))
    put('g_pre2', g('norm_pre_ffn'))
    put('g_post2', g('norm_post_ffn'))
    put('mu', g('rwkv_mu'))
    put('w0', g('rwkv_w0'))
    put('a0', g('rwkv_a0'))
    put('k_k', g('rwkv_k_k'))
    put('k_a', g('rwkv_k_a'))
    put('r_k', g('rwkv_r_k'))
    put('lnx_w', g('rwkv_lnx_w'))
    put('lnx_b', g('rwkv_lnx_b'))
    put('lcw', g('lru_conv_w'))
    put('lcb', g('lru_conv_b'))
    put('gxb', g('lru_gx_b'))
    put('gab', g('lru_ga_b'))
    put('lam', g('lru_lambda'))
    put('fcw', g('ffn_conv_w'))
    put('fcb', g('ffn_conv_b'))
    shared = {
        "w_in": g('w_in'), "vecs": vec, "consts": _consts(),
        "lora": np.ascontiguousarray(np.concatenate([g('rwkv_w_up'), g('rwkv_a_up')], axis=0)),
        "gup": g('rwkv_g_up'),
        "wor": g('rwkv_w_out'), "wol": g('lru_w_out'), "wo": g('w_o'),
        "wup": g('ffn_up'), "wdn": g('ffn_down'),
    }
    gx, ga = g('lru_gx_w'), g('lru_ga_w')
    gxa = np.stack([gx, ga], 0)
    gxa = gxa.reshape(2, 4, 2, 64, 64).transpose(2, 3, 0, 1, 4)
    shared["gxa"] = np.ascontiguousarray(gxa.reshape(128, 2 * 4 * 64))
    xp = f(inp['x_prompt'])
    xs = f(inp['x_sample'])[:, 0, :]
    sh = g('state_rwkv_shift')[:, 0, :]
    wkv = g('state_rwkv_wkv')
    lc = g('state_lru_conv')
    lh = g('state_lru_h')
    fc = g('state_ffn_conv')
    maps = []
    for c in range(NCORES):
        n0 = c * NS
        m = dict(shared)
        m["xT"] = np.ascontiguousarray(np.concatenate([xp[c].T, xs[n0:n0 + NS].T], axis=1))
        m["s_shift"] = np.ascontiguousarray(sh[n0:n0 + NS].reshape(NS, 14, 128).transpose(2, 1, 0).reshape(128, -1))
        m["s_lconv"] = np.ascontiguousarray(lc[n0:n0 + NS].reshape(NS, 3, 4, 128).transpose(3, 2, 1, 0).reshape(128, -1))
        m["s_lh"] = np.ascontiguousarray(lh[n0:n0 + NS].reshape(NS, 4, 128).transpose(2, 1, 0).reshape(128, -1))
        m["s_fconv"] = np.ascontiguousarray(fc[n0:n0 + NS].reshape(NS, 2, 44, 128).transpose(3, 2, 1, 0).reshape(128, -1))
        w = wkv[n0:n0 + NS].reshape(NS, 4, 2, 64, 64)
        m["s_wkv"] = np.ascontiguousarray(w.transpose(2, 4, 1, 0, 3).reshape(128, -1))
        maps.append(m)
    return maps


def _assemble(results):
    yp = np.zeros((NCORES, T, D), np.float32)
    ys = np.zeros((NCORES * NS, 1, D), np.float32)
    p_shift = np.zeros((1, NCORES, 1, RP), np.float32)
    p_wkv = np.zeros((1, NCORES, 8, 64, 64), np.float32)
    p_lconv = np.zeros((1, NCORES, 3, RW), np.float32)
    p_lh = np.zeros((1, NCORES, RW), np.float32)
    p_fconv = np.zeros((1, NCORES, 2, 2 * DFF), np.float32)
    s_shift = np.zeros((1, NCORES * NS, 1, RP), np.float32)
    s_wkv = np.zeros((1, NCORES * NS, 8, 64, 64), np.float32)
    s_lconv = np.zeros((1, NCORES * NS, 3, RW), np.float32)
    s_lh = np.zeros((1, NCORES * NS, RW), np.float32)
    s_fconv = np.zeros((1, NCORES * NS, 2, 2 * DFF), np.float32)
    for c, r in enumerate(results):
        n0 = c * NS
        yT = r["yT"]
        yp[c] = yT[:, :T].T
        ys[n0:n0 + NS, 0] = yT[:, T:].T
        a = r["o_shift"].reshape(128, 14, 17).transpose(2, 1, 0).reshape(17, RP)
        p_shift[0, c, 0] = a[0]
        s_shift[0, n0:n0 + NS, 0] = a[1:]
        a = r["o_lconv"].reshape(128, 4, 3, 17).transpose(3, 2, 1, 0).reshape(17, 3, RW)
        p_lconv[0, c] = a[0]
        s_lconv[0, n0:n0 + NS] = a[1:]
        a = r["o_lh"].reshape(128, 4, 17).transpose(2, 1, 0).reshape(17, RW)
        p_lh[0, c] = a[0]
        s_lh[0, n0:n0 + NS] = a[1:]
        a = r["o_fconv"].reshape(128, 44, 2, 17).transpose(3, 2, 1, 0).reshape(17, 2, 2 * DFF)
        p_fconv[0, c] = a[0]
        s_fconv[0, n0:n0 + NS] = a[1:]
        a = r["o_wkvp"].reshape(2, 64, 4, 64)
        p_wkv[0, c] = a.transpose(2, 0, 3, 1).reshape(8, 64, 64)
        a = r["o_wkvs"].reshape(2, 64, 4, NS, 64)
        s_wkv[0, n0:n0 + NS] = a.transpose(3, 2, 0, 4, 1).reshape(NS, 8, 64, 64)
    return (yp, ys, p_shift, p_wkv, p_lconv, p_lh, p_fconv, s_shift, s_wkv, s_lconv, s_lh, s_fconv)


def kernel(**inputs):
    nc = _get_nc()
    maps = _prep_inputs(inputs)
    res = run_bass_kernel_spmd(nc, maps, core_ids=list(range(NCORES)))
    return _assemble(res.results)
```

```python
import contextlib
import math
import numpy as np
import concourse.bass as bass
import concourse.mybir as mybir
from concourse.bass_utils import run_bass_kernel_spmd

F32 = mybir.dt.float32
BF16 = mybir.dt.bfloat16
AF = mybir.ActivationFunctionType
ALU = mybir.AluOpType
AX = mybir.AxisListType

D = 1024
T = 2048
NS = 16
TT = T + NS
RW = 512
RP = 1792
INC = 4352
DFF = 2816
NCORES = 8
CH = 64
NBA = 128
NBB = 256
CDEC = math.exp(-0.5)

VSPEC = [('g_pre', 8), ('g_post', 8), ('g_pre2', 8), ('g_post2', 8), ('mu', 14), ('w0', 4), ('a0', 4),
         ('k_k', 4), ('k_a', 4), ('r_k', 4), ('lnx_w', 4), ('lnx_b', 4), ('lcw', 16), ('lcb', 4),
         ('gxb', 4), ('gab', 4), ('lam', 4), ('fcw', 132), ('fcb', 44)]
VOFF = {}
_o = 0
for _n, _c in VSPEC:
    VOFF[_n] = _o
    _o += _c
NVEC = _o
C_ONES = 0
C_BONES = 128
C_IDENT = 256
C_MASK1 = 384
C_MASK2 = 512
C_I64 = 576
NCONST = 640


class Tk:
    __slots__ = ("name", "excl", "last_w", "readers")

    def __init__(self, name, excl=False):
        self.name = name
        self.excl = excl
        self.last_w = None
        self.readers = []


class Op:
    __slots__ = ("eng", "fn", "waits", "idx", "signal", "dma_sem")

    def __init__(self, eng, fn):
        self.eng = eng
        self.fn = fn
        self.waits = []
        self.signal = False
        self.dma_sem = None


class Sched:
    ENGS = ("pe", "act", "dve", "pool", "sp")

    def __init__(self, nc):
        self.nc = nc
        self.ops = {e: [] for e in self.ENGS}
        self.waited = {e: {} for e in self.ENGS}
        self.dma_sems = []
        self.final_tokens = []
        self.pending = {e: [] for e in self.ENGS}

    def new_dma_sem(self):
        self.dma_sems.append([None, 0, None])
        return len(self.dma_sems) - 1

    def _add_wait(self, op, tok):
        if tok is None:
            return
        eng = op.eng
        if tok[0] == 'e':
            if tok[1] == 'pe' and eng == 'pe':
                return
            key = ('e', tok[1])
        else:
            key = ('d', tok[1])
        val = tok[2]
        if self.waited[eng].get(key, -1) >= val:
            return
        self.waited[eng][key] = val
        op.waits.append(tok)
        if tok[0] == 'e':
            self.ops[tok[1]][tok[2]].signal = True

    def op(self, eng, fn, reads=(), writes=(), dma_sem=None, extra=()):
        o = Op(eng, fn)
        o.idx = len(self.ops[eng])
        toks = list(extra)
        if self.pending[eng]:
            toks.extend(self.pending[eng])
            self.pending[eng] = []
        for r in reads:
            toks.append(r.last_w)
            if r.excl:
                toks.extend(r.readers)
        for w in writes:
            toks.append(w.last_w)
            toks.extend(w.readers)
        if dma_sem is not None:
            toks.append(self.dma_sems[dma_sem][2])
        for t in toks:
            self._add_wait(o, t)
        self.ops[eng].append(o)
        if dma_sem is not None:
            s = self.dma_sems[dma_sem]
            s[1] += 16
            o.dma_sem = dma_sem
            tok = ('d', dma_sem, s[1])
            s[2] = tok
        else:
            tok = ('e', eng, o.idx)
        for r in reads:
            if r.excl:
                r.last_w = tok
                r.readers = []
            else:
                r.readers.append(tok)
        for w in writes:
            w.last_w = tok
            w.readers = []
        co_switch()
        return tok

    def barrier(self):
        toks = []
        for e in self.ENGS:
            for o in reversed(self.ops[e]):
                if o.dma_sem is None:
                    toks.append(('e', e, o.idx))
                    break
        for i, s in enumerate(self.dma_sems):
            if s[2] is not None:
                toks.append(s[2])
        for e in self.ENGS:
            self.pending[e] = list(toks)

    def finish(self, toks):
        self.final_tokens = list(toks)

    def emit(self):
        nc = self.nc
        fin = Op('sp', None)
        fin.idx = len(self.ops['sp'])
        for s in self.dma_sems:
            self._add_wait(fin, s[2])
        for t in self.final_tokens:
            self._add_wait(fin, t)
        with contextlib.ExitStack() as st:
            esem = {}
            for e in self.ENGS:
                esem[e] = st.enter_context(nc.semaphore("s_" + e))
            for i, s in enumerate(self.dma_sems):
                s[0] = st.enter_context(nc.semaphore("d%d" % i))
            sigval = {}
            for e in self.ENGS:
                c = 0
                for o in self.ops[e]:
                    if o.signal and o.dma_sem is None:
                        c += 1
                        sigval[(e, o.idx)] = c
            block = st.enter_context(nc.Block())

            def mk(ename, extra=None):
                def body(eng):
                    def do_waits(o):
                        for t in o.waits:
                            if t[0] == 'e':
                                eng.wait_ge(esem[t[1]], sigval[(t[1], t[2])])
                            else:
                                eng.wait_ge(self.dma_sems[t[1]][0], t[2])
                    for o in self.ops[ename]:
                        do_waits(o)
                        inst = o.fn(eng)
                        if o.dma_sem is not None:
                            inst.then_inc(self.dma_sems[o.dma_sem][0], 16)
                        elif o.signal:
                            inst.then_inc(esem[ename], 1)
                    if extra is not None:
                        do_waits(extra)
                return body
            block.tensor(mk('pe'))
            block.scalar(mk('act'))
            block.vector(mk('dve'))
            block.gpsimd(mk('pool'))
            block.sync(mk('sp', fin))
        return {e: len(self.ops[e]) for e in self.ENGS}


class Buf:
    def __init__(self, name, ap, nk=1, excl=False):
        self.t = ap
        self.k = [Tk("%s%d" % (name, i), excl) for i in range(nk)]

    @property
    def all(self):
        return list(self.k)


class Arena:
    def __init__(self, tensor, nbytes):
        self.tensor = tensor
        self.nbytes = nbytes
        self.off = 0
        self.top = nbytes

    def alloc(self, name, free_shape, dtype, nk=1, from_top=False):
        sz = 4 if dtype == F32 else 2
        n = int(np.prod(free_shape))
        nb = (n * sz + 63) // 64 * 64
        if from_top:
            self.top -= nb
            o = self.top
        else:
            o = self.off
            self.off += nb
        assert self.off <= self.top, ("arena overflow", name, self.off, self.top)
        ap = self.tensor[:, o // 2:(o + n * sz) // 2]
        if dtype == F32:
            ap = ap.bitcast(F32)
        if len(free_shape) == 2:
            ap = ap.rearrange("p (a b) -> p a b", a=free_shape[0])
        elif len(free_shape) == 3:
            ap = ap.rearrange("p (a b c) -> p a b c", a=free_shape[0], b=free_shape[1])
        elif len(free_shape) == 4:
            ap = ap.rearrange("p (a b c d) -> p a b c d", a=free_shape[0], b=free_shape[1], c=free_shape[2])
        return Buf(name, ap, nk)


import threading
_TL = threading.local()


def co_switch():
    sw = getattr(_TL, 'sw', None)
    if sw is not None:
        sw()


def co_run(funcs):
    n = len(funcs)
    st = {'turn': 0, 'alive': [True] * n, 'err': None}
    cv = threading.Condition()

    def nxt(i):
        for d in range(1, n + 1):
            j = (i + d) % n
            if st['alive'][j]:
                st['turn'] = j
                return
        st['turn'] = -1

    def switch(i):
        with cv:
            nxt(i)
            cv.notify_all()
            while st['turn'] != i:
                cv.wait()

    def worker(i):
        with cv:
            while st['turn'] != i:
                cv.wait()
        _TL.sw = lambda: switch(i)
        try:
            funcs[i]()
        except BaseException as e:
            st['err'] = e
        finally:
            _TL.sw = None
            with cv:
                st['alive'][i] = False
                nxt(i)
                cv.notify_all()
    ths = [threading.Thread(target=worker, args=(i,)) for i in range(n)]
    for t in ths:
        t.start()
    for t in ths:
        t.join()
    if st['err'] is not None:
        raise st['err']


class Builder:
    def __init__(self, debug=None):
        self.debug = debug or []
        self.nc = bass.Bass("TRN2", target_bir_lowering=False)
        self.S = Sched(self.nc)
        self.dbg_outs = {}

    def act(self, out, in_, func, reads, writes, scale=None, bias=None, eng='act'):
        kw = {}
        if scale is not None:
            kw['scale'] = scale
        if bias is not None:
            kw['bias'] = bias
        return self.S.op('act', lambda e: e.activation(out=out, in_=in_, func=func, **kw), reads, writes)

    def tt(self, eng, out, in0, in1, op, reads, writes):
        return self.S.op(eng, lambda e: e.tensor_tensor(out=out, in0=in0, in1=in1, op=op), reads, writes)

    def ts(self, eng, out, in0, s1, op0, reads, writes, s2=None, op1=None):
        if op1 is None:
            return self.S.op(eng, lambda e: e.tensor_scalar(out=out, in0=in0, scalar1=s1, scalar2=None, op0=op0), reads, writes)
        return self.S.op(eng, lambda e: e.tensor_scalar(out=out, in0=in0, scalar1=s1, scalar2=s2, op0=op0, op1=op1), reads, writes)

    def stt(self, out, in0, scalar, in1, op0, op1, reads, writes):
        return self.S.op('dve', lambda e: e.scalar_tensor_tensor(out=out, in0=in0, scalar=scalar, in1=in1, op0=op0, op1=op1), reads, writes)

    def cp(self, eng, out, in_, reads, writes):
        if eng == 'act':
            return self.S.op('act', lambda e: e.copy(out=out, in_=in_), reads, writes)
        return self.S.op(eng, lambda e: e.tensor_copy(out=out, in_=in_), reads, writes)

    def mm(self, out, lhsT, rhs, start, stop, reads, writes):
        return self.S.op('pe', lambda e: e.matmul(out, lhsT=lhsT, rhs=rhs, start=start, stop=stop), reads, writes)

    def tr(self, out, in_, ident, reads, writes):
        return self.S.op('pe', lambda e: e.transpose(out, in_, ident), reads, writes)

    def dma(self, eng, out, in_, reads, writes, sem, extra=()):
        return self.S.op(eng, lambda e: e.dma_start(out=out, in_=in_), reads, writes, dma_sem=sem, extra=extra)

    def recip(self, out, in_, reads, writes):
        return self.S.op('dve', lambda e: e.reciprocal(out=out, in_=in_), reads, writes)

    def memset(self, eng, ap, val, writes):
        return self.S.op(eng, lambda e: e.memset(ap, val), (), writes)

    def scan(self, out, d0, d1, init, reads, writes, op0=None):
        op0 = ALU.mult if op0 is None else op0
        return self.S.op('dve', lambda e: e.tensor_tensor_scan(out=out, data0=d0, data1=d1, initial=init,
                                                              op0=op0, op1=ALU.add), reads, writes)

    def reduce(self, out, in_, reads, writes):
        return self.S.op('dve', lambda e: e.tensor_reduce(out=out, in_=in_, axis=AX.X, op=ALU.add), reads, writes)

    def build(self):
        nc = self.nc
        S = self.S
        dr = lambda name, shape, kind: nc.dram_tensor(name, list(shape), F32, kind=kind).ap()
        I = "ExternalInput"
        O = "ExternalOutput"
        xT = dr("xT", [D, TT], I)
        w_in = dr("w_in", [D, INC], I)
        vecs = dr("vecs", [128, NVEC], I)
        consts = dr("consts", [128, NCONST], I)
        lora = dr("lora", [128, RW], I)
        gup = dr("gup", [128, RW], I)
        gxa = dr("gxa", [128, 2 * 4 * 64], I)
        wor = dr("wor", [RW, D], I)
        wol = dr("wol", [RW, D], I)
        wo = dr("wo", [D, D], I)
        wup = dr("wup", [D, 2 * DFF], I)
        wdn = dr("wdn", [DFF, D], I)
        s_shift = dr("s_shift", [128, 14 * NS], I)
        s_lconv = dr("s_lconv", [128, 4 * 3 * NS], I)
        s_lh = dr("s_lh", [128, 4 * NS], I)
        s_fconv = dr("s_fconv", [128, 44 * 2 * NS], I)
        s_wkv = dr("s_wkv", [128, 4 * NS * 64], I)
        yT = dr("yT", [D, TT], O)
        o_shift = dr("o_shift", [128, 14 * 17], O)
        o_lconv = dr("o_lconv", [128, 4 * 3 * 17], O)
        o_lh = dr("o_lh", [128, 4 * 17], O)
        o_fconv = dr("o_fconv", [128, 44 * 2 * 17], O)
        o_wkvp = dr("o_wkvp", [128, 4 * 64], O)
        o_wkvs = dr("o_wkvs", [128, 4 * NS * 64], O)
        x1s = nc.dram_tensor("x1s", [D, TT], F32, kind="Internal").ap()
        X1S = Tk("x1s")
        for name, shape in self.debug:
            self.dbg_outs[name] = dr("dbg_" + name, shape, O)

        with contextlib.ExitStack() as st:
            ARB = 212736
            art = st.enter_context(nc.sbuf_tensor("arena", [128, ARB // 2], BF16))
            ps = [st.enter_context(nc.psum_tensor("ps%d" % i, [128, 512], F32)) for i in range(7)]
            pst = st.enter_context(nc.psum_tensor("pst", [128, 1024], BF16))
            PM = [Buf("pm0", ps[0][:, :], 1, True), Buf("pm1", ps[1][:, :], 1, True)]
            PST = Buf("pstat", ps[2][:, :], 1, True)
            PA = Buf("pa", ps[3][:, :], 1, True)
            PB_ = Buf("pb", ps[4][:, :], 1, True)
            PX = Buf("px", ps[5][:, :], 1, True)
            PW = Buf("pw", ps[6][:, :], 1, True)
            PT = Buf("pt", pst[:, :], 1, True)
            self.pmi = 0

            A = Arena(art, ARB)
            VEC = A.alloc("vec", [NVEC], F32, from_top=True)
            CON = A.alloc("con", [NCONST], F32, from_top=True)
            DER = A.alloc("der", [8], F32, from_top=True)
            DR2 = A.alloc("dr2", [28], F32, from_top=True)
            IDB = A.alloc("idb", [128], BF16, from_top=True)
            ONB = A.alloc("onb", [256], BF16, from_top=True)
            M1B = A.alloc("m1b", [128], F32, from_top=True)
            SHO = A.alloc("sho", [14, 17], F32, from_top=True)
            LCO = A.alloc("lco", [4, 3, 17], F32, from_top=True)
            LHO = A.alloc("lho", [4, 17], F32, from_top=True)
            WKP = A.alloc("wkp", [4, 64], F32, from_top=True)
            SQ = [A.alloc("sq%d" % i, [max(NBA, NBB)], F32, from_top=True) for i in range(2)]
            RSTD = A.alloc("rstd", [max(NBA, NBB)], F32, from_top=True)

            def V(name, j=0):
                o = VOFF[name] + j
                return VEC.t[:, o:o + 1]
            ONESF = CON.t[:, C_ONES:C_ONES + 128]
            BONES = CON.t[:, C_BONES:C_BONES + 128]
            MASK1 = CON.t[:, C_MASK1:C_MASK1 + 128]
            MASK2 = CON.t[:, C_MASK2:C_MASK2 + 64]
            I64 = CON.t[:, C_I64:C_I64 + 64]
            ONES64 = CON.t[:, C_ONES:C_ONES + 64]

            sem_c = S.new_dma_sem()
            self.dma('sp', VEC.t, vecs[:, :], [], VEC.all, sem_c)
            sem_c2 = S.new_dma_sem()
            self.dma('sp', CON.t, consts[:, :], [], CON.all, sem_c2)
            self.cp('dve', IDB.t, CON.t[:, C_IDENT:C_IDENT + 128], CON.all, IDB.all)
            self.cp('dve', ONB.t, CON.t[:, C_ONES:C_ONES + 256], CON.all, ONB.all)
            ONESB = ONB.t[:, 0:128]
            BONESB = ONB.t[:, 128:256]
            self.act(DER.t[:, 0:4], VEC.t[:, VOFF['lam']:VOFF['lam'] + 4], AF.Exp, VEC.all, DER.all, scale=-1.0)
            self.act(DER.t[:, 0:4], DER.t[:, 0:4], AF.Ln, DER.all, DER.all, bias=1.0)
            self.ts('dve', DER.t[:, 4:8], DER.t[:, 0:4], -16.0, ALU.mult, DER.all, DER.all)
            self.ts('dve', DER.t[:, 0:4], DER.t[:, 0:4], -8.0, ALU.mult, DER.all, DER.all)
            for b_ in (SHO, LCO, LHO):
                self.memset('pool', b_.t, 0.0, b_.all)
            for i_, nm in enumerate(('w0', 'a0', 'k_a', 'gxb', 'gab')):
                self.ts('dve', DR2.t[:, 4 * i_:4 * i_ + 4], VEC.t[:, VOFF[nm]:VOFF[nm] + 4], 0.5, ALU.mult, VEC.all + DR2.all, DR2.all)
            self.ts('dve', DR2.t[:, 20:24], DER.t[:, 0:4], 0.5, ALU.mult, DER.all + DR2.all, DR2.all)
            self.cp('dve', DR2.t[:, 24:28], DER.t[:, 0:4], DER.all + DR2.all, DR2.all)
            H2 = lambda i_, j_: DR2.t[:, 4 * i_ + j_:4 * i_ + j_ + 1]
            OMM = A.alloc("omm", [14], F32, from_top=True)
            self.ts('dve', OMM.t, VEC.t[:, VOFF['mu']:VOFF['mu'] + 14], -1.0, ALU.mult, VEC.all, OMM.all, s2=1.0, op1=ALU.add)

            mark = (A.off, A.top)
            W_IN = A.alloc("w_in", [8, INC], BF16, nk=8)
            WOR = A.alloc("wor", [4, D], BF16)
            WOL = A.alloc("wol", [4, D], BF16)
            WO = A.alloc("wo", [8, D], BF16)
            LORA = A.alloc("lora", [RW], BF16)
            GUP = A.alloc("gup", [RW], BF16)
            GXA = A.alloc("gxa", [2, 4, 64], BF16)
            wsems = [S.new_dma_sem() for _ in range(8)]
            wiv = w_in.rearrange("(k p) n -> p k n", p=128)
            WIG = [(0, 1792), (1792, 2304), (2304, 3328), (3328, 4352)]
            WINT = [Tk("wint%d" % g_) for g_ in range(4)]
            for g_, (c0_, c1_) in enumerate(WIG):
                self.dma('pool', W_IN.t[:, :, c0_:c1_], wiv[:, :, c0_:c1_], [], [WINT[g_]], wsems[g_])

            def WK(oc):
                c_ = oc * 128
                return WINT[0 if c_ < 1792 else 1 if c_ < 2304 else 2 if c_ < 3328 else 3]
            ws2 = [S.new_dma_sem() for _ in range(6)]
            self.dma('pool', LORA.t, lora[:, :], [], LORA.all, ws2[0])
            self.dma('pool', GUP.t, gup[:, :], [], GUP.all, ws2[1])
            self.dma('pool', GXA.t, gxa.rearrange("p (a b c) -> p a b c", a=2, b=4), [], GXA.all, ws2[2])
            self.dma('pool', WOR.t, wor.rearrange("(k p) n -> p k n", p=128), [], WOR.all, ws2[3])
            self.dma('pool', WOL.t, wol.rearrange("(k p) n -> p k n", p=128), [], WOL.all, ws2[4])
            self.dma('pool', WO.t, wo.rearrange("(k p) n -> p k n", p=128), [], WO.all, ws2[5])


            markA = A.off
            xTv = xT.rearrange("(k p) n -> p k n", p=128)
            x1v = x1s.rearrange("(k p) n -> p k n", p=128)
            yTv = yT.rearrange("(k p) n -> p k n", p=128)
            N = NBA
            NCK = N // CH
            pXB = [A.alloc("pxb%d" % i, [8, N], F32, nk=8) for i in range(3)]
            pXN = [A.alloc("pxn%d" % i, [8, N + 3], BF16, nk=8) for i in range(3)]
            _sqv = [SQ[i].t[:, :].bitcast(BF16) for i in range(2)]
            SQA = [Buf("sqa%d" % i, _sqv[i // 2][:, (i % 2) * 128:(i % 2) * 128 + 128]) for i in range(4)]
            RSH = [Buf("rsh%d" % i, RSTD.t[:, i * 128:(i + 1) * 128]) for i in range(2)]
            pflags = {}

            def wait_pflags(keys):
                while not all(k_ in pflags for k_ in keys):
                    yield
            pPBF = [A.alloc("ppbf%d" % i, [N], F32) for i in range(3)]
            pR, pK, pV, pSG, pCSG, pASIG, pKKN, pECW, pENCW = [A.alloc("pq%d" % i, [4, N], F32, nk=4) for i in range(9)]
            pLIN = A.alloc("plin", [N], BF16)
            pGS = A.alloc("pgs", [N], BF16)
            pT = [[A.alloc("pt%d_%d" % (j, i), [N], F32) for i in range(2)] for j in range(4)]
            pL = [[A.alloc("pl%d_%d" % (j, i), [N], F32) for i in range(5)] for j in range(4)]
            pXCB = [A.alloc("pxcb%d" % j, [N], BF16) for j in range(4)]
            dbl = lambda nm, sh, dt, nk=1: [A.alloc("%s%d" % (nm, i), sh, dt, nk=nk) for i in range(2)]
            pAR = dbl("par", [4, 2, N], BF16, 4)
            pBT = dbl("pbt", [4, N], BF16, 4)
            pKT = dbl("pkt", [4, N], BF16, 4)
            pVB = dbl("pvb", [4, N], BF16, 4)
            pPC = dbl("ppc", [4, NCK], F32, 4)
            pG = dbl("pg", [4, N], F32, 4)
            pBON = dbl("pbon", [4, N], F32, 4)
            pHSB = dbl("phsb", [4, N], BF16, 4)
            pYFM = dbl("pyfm", [4, N], F32, 4)
            TOK = A.alloc("ptok", [3, 4, CH], BF16)
            MBS = A.alloc("pmbs", [4, 128], BF16)
            MKS = A.alloc("pmks", [4, 128], BF16)
            XX = [A.alloc("pxx%d" % i, [2, 4, CH], BF16) for i in range(2)]
            XT0 = A.alloc("pxt0", [4, CH], BF16)
            WF = A.alloc("pwf", [4, CH], F32)
            WB = [A.alloc("pwb%d" % i, [4, CH], BF16) for i in range(2)]
            HF = A.alloc("phf", [4, CH], F32)
            HB = A.alloc("phb", [4, CH], BF16)
            HTMP = A.alloc("phtmp", [4, CH], F32)
            HCAR = A.alloc("phcar", [4], F32)
            pZB = A.alloc("pzb", [4, N], BF16, nk=4)
            pMIX = A.alloc("pmix", [8, N], BF16, nk=8)
            pGT = [A.alloc("pgt%d" % i, [N], F32) for i in range(4)]
            print("phase A prompt arena used", A.off, "top", A.top)
            self.memset('pool', HF.t, 0.0, HF.all)
            self.memset('pool', HB.t, 0.0, HB.all)
            self.memset('pool', HCAR.t, 0.0, HCAR.all)
            pxsem = [S.new_dma_sem() for _ in range(3)]
            px1sem = [S.new_dma_sem() for _ in range(3)]
            PMA = [PM[0], PM[1], PST]
            pm_busy = [False, False, False]
            NBLK = T // N

            def acquire():
                while True:
                    for i_ in range(3):
                        if not pm_busy[i_]:
                            pm_busy[i_] = True
                            return i_
                    yield

            def acquire2():
                while True:
                    fr = [i_ for i_ in range(3) if not pm_busy[i_]]
                    if len(fr) >= 2:
                        pm_busy[fr[0]] = True
                        pm_busy[fr[1]] = True
                        return fr[0], fr[1]
                    yield

            def release(i_):
                pm_busy[i_] = False

            class SyncPt:
                def __init__(self, n):
                    self.n = n
                    self.c = 0

            def wait_sync(sp):
                sp.c += 1
                while sp.c < sp.n:
                    yield

            def chain(*gs):
                for g_ in gs:
                    yield from g_

            def window(gens, width):
                gens = list(gens)
                act_ = []
                idx = 0
                while idx < len(gens) or act_:
                    while len(act_) < width and idx < len(gens):
                        act_.append(gens[idx])
                        idx += 1
                    for g_ in list(act_):
                        try:
                            next(g_)
                        except StopIteration:
                            act_.remove(g_)
                        yield

            def rr(streams, weights):
                streams = list(streams)
                alive = [True] * len(streams)
                while any(alive):
                    for si, g_ in enumerate(streams):
                        if not alive[si]:
                            continue
                        for _ in range(weights[si]):
                            try:
                                next(g_)
                            except StopIteration:
                                alive[si] = False
                                break

            def rms_p(src_aps, src_ks, half):
                pi = yield from acquire()
                pstat = PMA[pi]
                rs = RSH[half]
                for k in range(8):
                    sq = SQA[2 * half + k % 2]
                    self.act(sq.t[:, :N], src_aps[k], AF.Square, [src_ks[k]], sq.all)
                    self.mm(pstat.t[:, :N], ONESB, sq.t[:, :N], k == 0, k == 7, sq.all + ONB.all, pstat.all)
                    yield
                self.act(rs.t[:, :N], pstat.t[:, :N], AF.Ln, pstat.all, rs.all, scale=1.0 / D, bias=1e-6)
                release(pi)
                self.act(rs.t[:, :N], rs.t[:, :N], AF.Exp, rs.all, rs.all, scale=-0.5)
                yield

            def rw_proj(b, oc):
                par = b % 2
                last = (b == NBLK - 1)
                xn = pXN[b % 3]
                pi = yield from acquire()
                pm = PMA[pi]
                for k in range(8):
                    self.mm(pm.t[:, :N + 1], W_IN.t[:, k, oc * 128:(oc + 1) * 128], xn.t[:, k, 2:N + 3], k == 0, k == 7,
                            [WK(oc), xn.k[k]], pm.all)
                yield
                pc_ = pPBF[(oc + 2) % 14 % 3]
                self.act(pc_.t[:, :N], pm.t[:, 1:N + 1], AF.Identity, pm.all + OMM.all, pc_.all, scale=OMM.t[:, oc:oc + 1])
                if last:
                    self.cp('act', SHO.t[:, oc, 0:1], pm.t[:, N:N + 1], pm.all, SHO.all)
                yield
                if oc < 12:
                    dst = (pR, pK, pV)[oc // 4]
                    dst_ap, dst_k = dst.t[:, oc % 4, :N], [dst.k[oc % 4]]
                    dt_ = pL[oc % 4][3 + oc // 4] if oc < 8 else pL[oc % 4][0]
                else:
                    dst_ = pL[oc - 12][1]
                    dst_ap, dst_k = dst_.t[:, :N], dst_.all
                    dt_ = pL[oc - 12][2]
                self.stt(dst_ap, pm.t[:, 0:N], V('mu', oc), pc_.t[:, :N], ALU.mult, ALU.add, pm.all + pc_.all + VEC.all, dst_k)
                release(pi)
                yield
                if oc == 12:
                    self.act(pLIN.t[0:64, :N], dst_ap[0:64, :], AF.Tanh, dst_k, pLIN.all)
                    self.cp('pool', pLIN.t[64:128, :N], dst_ap[64:128, :], dst_k, pLIN.all)
                    yield
                if oc == 13:
                    self.act(dst_ap, dst_ap, AF.Tanh, dst_k, dst_k, scale=0.5)
                    self.ts('pool', pGS.t[:, :N], dst_ap, 1.0, ALU.add, dst_k, pGS.all, s2=0.5, op1=ALU.mult)
                    yield

            def rwkv_chain(b, j, sp):
                par = b % 2
                cs = slice(j * 128, (j + 1) * 128)
                t0, t1 = pT[j]
                G_, BON, AR_, BT, KT, VBT, PC = pG[par], pBON[par], pAR[par], pBT[par], pKT[par], pVB[par], pPC[par]
                if b >= 2:
                    yield from wait_pflags([('tail', b - 2)])
                pi = yield from acquire()
                pm = PMA[pi]
                self.mm(pm.t[:, :N], LORA.t[0:64, cs], pLIN.t[0:64, :N], True, True, LORA.all + pLIN.all, pm.all)
                yield
                self.act(pSG.t[:, j, :N], pm.t[:, :N], AF.Tanh, pm.all + DR2.all, [pSG.k[j]], scale=0.5, bias=H2(0, j))
                release(pi)
                yield
                pi = yield from acquire()
                pm = PMA[pi]
                self.mm(pm.t[:, :N], LORA.t[64:128, cs], pLIN.t[64:128, :N], True, True, LORA.all + pLIN.all, pm.all)
                yield
                self.act(pASIG.t[:, j, :N], pm.t[:, :N], AF.Tanh, pm.all + DR2.all, [pASIG.k[j]], scale=0.5, bias=H2(1, j))
                release(pi)
                yield
                pi = yield from acquire()
                pm = PMA[pi]
                self.mm(pm.t[:, :N], GUP.t[:, cs], pGS.t[:, :N], True, True, GUP.all + pGS.all, pm.all)
                yield
                self.cp('act', G_.t[:, j, :N], pm.t[:, :N], pm.all, [G_.k[j]])
                release(pi)
                yield
                t0b = t0.t[:, :].bitcast(BF16)
                self.act(t0b[:, :N], pK.t[:, j, :N], AF.Square, [pK.k[j]] + VEC.all, t0.all, scale=V('k_k', j))
                yield
                pi = yield from acquire()
                pm = PMA[pi]
                self.mm(pm.t[:, :N], BONESB, t0b[:, :N], True, True, t0.all + ONB.all, pm.all)
                yield
                self.ts('dve', t1.t[:, :N], pm.t[:, :N], 2.0 ** -60, ALU.max, pm.all, t1.all)
                release(pi)
                yield
                yield from wait_sync(sp)
                self.act(t1.t[:, :N], t1.t[:, :N], AF.Ln, t1.all, t1.all)
                yield
                self.act(t1.t[:, :N], t1.t[:, :N], AF.Exp, t1.all, t1.all, scale=-0.5)
                yield
                self.stt(pKKN.t[:, j, :N], pK.t[:, j, :N], V('k_k', j), t1.t[:, :N], ALU.mult, ALU.mult,
                         [pK.k[j]] + VEC.all + t1.all, [pKKN.k[j]])
                yield
                self.ts('dve', t0.t[:, :N], pASIG.t[:, j, :N], -1.0, ALU.add, [pASIG.k[j]] + DR2.all, t0.all,
                        s2=H2(2, j), op1=ALU.mult)
                yield
                self.stt(pK.t[:, j, :N], t0.t[:, :N], 1.0, pK.t[:, j, :N], ALU.add, ALU.mult, t0.all + [pK.k[j]], [pK.k[j]])
                yield
                self.stt(t0b[:, :N], pR.t[:, j, :N], V('r_k', j), pK.t[:, j, :N], ALU.mult, ALU.mult,
                         [pR.k[j], pK.k[j]] + VEC.all, t0.all)
                yield
                pi = yield from acquire()
                pm = PMA[pi]
                self.mm(pm.t[:, :N], BONESB, t0b[:, :N], True, True, t0.all + ONB.all, pm.all)
                yield
                self.tt('dve', BON.t[:, j, :N], pm.t[:, :N], pV.t[:, j, :N], ALU.mult, pm.all + [pV.k[j]], [BON.k[j]])
                release(pi)
                yield
                for ci in range(NCK):
                    sl = slice(ci * CH, (ci + 1) * CH)
                    self.scan(pCSG.t[:, j, sl], ONES64, pSG.t[:, j, sl], 0.0, [pSG.k[j]] + CON.all, [pCSG.k[j]], op0=ALU.add)
                    yield
                self.act(pECW.t[:, j, :N], pCSG.t[:, j, :N], AF.Exp, [pCSG.k[j]], [pECW.k[j]], scale=-0.5 * CDEC)
                yield
                self.act(pENCW.t[:, j, :N], pCSG.t[:, j, :N], AF.Exp, [pCSG.k[j]], [pENCW.k[j]], scale=0.5 * CDEC)
                yield
                v3 = lambda b_, a, bnd: b_.t[:, j, :N].rearrange("p (c t) -> p c t", t=CH)[:, :, a:bnd]
                at3 = AR_.t[:, j, 0, :N].rearrange("p (c t) -> p c t", t=CH)
                self.stt(at3[:, :, 1:CH], v3(pKKN, 1, CH), -1.0, v3(pECW, 0, CH - 1), ALU.mult, ALU.mult,
                         [pKKN.k[j], pECW.k[j]], [AR_.k[j]])
                self.ts('pool', at3[:, :, 0:1], v3(pKKN, 0, 1), -1.0, ALU.mult, [pKKN.k[j]], [AR_.k[j]])
                yield
                self.tt('pool', AR_.t[:, j, 1, :N], pR.t[:, j, :N], pECW.t[:, j, :N], ALU.mult, [pR.k[j], pECW.k[j]], [AR_.k[j]])
                yield
                self.stt(t0.t[:, :N], pASIG.t[:, j, :N], 1.0, pKKN.t[:, j, :N], ALU.add, ALU.mult, [pKKN.k[j], pASIG.k[j]], t0.all)
                yield
                self.stt(BT.t[:, j, :N], t0.t[:, :N], 0.5, pENCW.t[:, j, :N], ALU.mult, ALU.mult, t0.all + [pENCW.k[j]], [BT.k[j]])
                yield
                self.tt('pool', KT.t[:, j, :N], pK.t[:, j, :N], pENCW.t[:, j, :N], ALU.mult, [pK.k[j], pENCW.k[j]], [KT.k[j]])
                yield
                self.cp('pool', VBT.t[:, j, :N], pV.t[:, j, :N], [pV.k[j]], [VBT.k[j]])
                self.cp('pool', PC.t[:, j, :], v3(pECW, CH - 1, CH), [pECW.k[j]], [PC.k[j]])
                yield

            def lru_chain(b, j, sp):
                par = b % 2
                last = (b == NBLK - 1)
                xn = pXN[b % 3]
                xc, gxs, gas, a_, uu = pL[j]
                oc = 14 + j
                pi = yield from acquire()
                pm = PMA[pi]
                for k in range(8):
                    self.mm(pm.t[:, :N + 3], W_IN.t[:, k, oc * 128:(oc + 1) * 128], xn.t[:, k, 0:N + 3], k == 0, k == 7,
                            [WK(oc), xn.k[k]], pm.all)
                yield
                self.act(xc.t[:, :N], pm.t[:, 3:N + 3], AF.Identity, pm.all + VEC.all, xc.all, scale=V('lcw', 12 + j), bias=V('lcb', j))
                if last:
                    self.cp('act', LCO.t[:, j, :, 0], pm.t[:, N:N + 3], pm.all, LCO.all)
                yield
                for i in range(3):
                    self.stt(xc.t[:, :N], pm.t[:, i:i + N], V('lcw', 4 * i + j), xc.t[:, :N], ALU.mult, ALU.add,
                             pm.all + VEC.all + xc.all, xc.all)
                    yield
                release(pi)
                xcb = pXCB[j]
                self.cp('pool', xcb.t[:, :N], xc.t[:, :N], xc.all, xcb.all)
                yield
                pi1, pi2 = yield from acquire2()
                pgx, pga = PMA[pi1], PMA[pi2]
                for h2 in range(2):
                    ps_ = slice(64 * h2, 64 * h2 + 64)
                    self.mm(pgx.t[ps_, :N], GXA.t[ps_, 0, j, :], xcb.t[ps_, :N], True, True, GXA.all + xcb.all, pgx.all)
                    self.mm(pga.t[ps_, :N], GXA.t[ps_, 1, j, :], xcb.t[ps_, :N], True, True, GXA.all + xcb.all, pga.all)
                yield
                self.act(gxs.t[:, :N], pgx.t[:, :N], AF.Tanh, pgx.all + DR2.all, gxs.all, scale=0.5, bias=H2(3, j))
                release(pi1)
                self.act(gas.t[:, :N], pga.t[:, :N], AF.Tanh, pga.all + DR2.all, gas.all, scale=0.5, bias=H2(4, j))
                release(pi2)
                yield
                self.act(a_.t[:, :N], gas.t[:, :N], AF.Exp, gas.all + DR2.all, a_.all, scale=H2(5, j), bias=H2(5, j))
                self.act(gas.t[:, :N], gas.t[:, :N], AF.Exp, gas.all + DR2.all, gas.all, scale=H2(6, j), bias=H2(6, j))
                yield
                self.ts('dve', gas.t[:, :N], gas.t[:, :N], 1.0 - 2.0 ** -23, ALU.min, gas.all, gas.all)
                self.stt(uu.t[:, :N], gxs.t[:, :N], 1.0, xc.t[:, :N], ALU.add, ALU.mult, gxs.all + xc.all, uu.all)
                yield
                yield from wait_sync(sp)
                self.act(gas.t[:, :N], gas.t[:, :N], AF.Ln, gas.all, gas.all, scale=-1.0, bias=1.0)
                yield
                self.act(gas.t[:, :N], gas.t[:, :N], AF.Exp, gas.all, gas.all, scale=0.5, bias=math.log(0.5))
                yield
                self.tt('dve', uu.t[:, :N], uu.t[:, :N], gas.t[:, :N], ALU.mult, uu.all + gas.all, uu.all)
                yield
                self.scan(xc.t[:, :N], a_.t[:, :N], uu.t[:, :N], HCAR.t[:, j:j + 1], a_.all + uu.all + HCAR.all, xc.all)
                yield
                self.cp('pool', HCAR.t[:, j:j + 1], xc.t[:, N - 1:N], xc.all, HCAR.all)
                if last:
                    self.cp('pool', LHO.t[:, j, 0:1], xc.t[:, N - 1:N], xc.all, LHO.all)
                self.cp('pool', pHSB[par].t[:, j, :N], xc.t[:, :N], xc.all, [pHSB[par].k[j]])
                yield

            def S1head(b):
                xb, xn = pXB[b % 3], pXN[b % 3]
                self.dma('sp', xb.t[:, :, :N], xTv[:, :, b * N:(b + 1) * N], [], xb.all, pxsem[b % 3])
                yield
                yield from rms_p([xb.t[:, k, :N] for k in range(8)], xb.k, 0)
                if b == 0:
                    self.memset('pool', xn.t[:, :, 0:3], 0.0, xn.all)
                else:
                    self.cp('pool', xn.t[:, :, 0:3], pXN[(b - 1) % 3].t[:, :, N:N + 3], pXN[(b - 1) % 3].all, xn.all)
                for k in range(8):
                    self.stt(xn.t[:, k, 3:N + 3], xb.t[:, k, :N], V('g_pre', k), RSH[0].t[:, :N], ALU.mult, ALU.mult,
                             [xb.k[k]] + VEC.all + RSH[0].all, [xn.k[k]])
                    yield
                pflags[('head', b)] = True

            def S1rest(b):
                yield from wait_pflags([('head', b)])
                yield from window([rw_proj(b, oc) for oc in (12, 13, 0, 1, 2, 3, 4, 5, 6, 7, 8, 9, 10, 11)], 3)
                chains = []
                sp = SyncPt(8)
                for j in range(4):
                    chains.append(lru_chain(b, j, sp))
                    chains.append(rwkv_chain(b, j, sp))
                yield from window(chains, 8)

            def S2(b):
                par = b % 2
                AR_, BT, KT, VBT, PC, YFM = pAR[par], pBT[par], pKT[par], pVB[par], pPC[par], pYFM[par]
                m1b = MASK1.unsqueeze(1).to_broadcast([128, 4, 128])
                m2b = MASK2.unsqueeze(1).to_broadcast([128, 4, CH])
                pa4 = PA.t[:, :].rearrange("p (a b) -> p a b", a=4)
                pb4 = PB_.t[:, :].rearrange("p (a b) -> p a b", a=4)
                px = PX.t[:, :].rearrange("p (s a b) -> p s a b", s=2, a=4)
                pw = PW.t[:, :].rearrange("p (s a b) -> p s a b", s=2, a=4)
                pt = PT.t[:, 0:3 * 4 * CH].rearrange("p (s a b) -> p s a b", s=3, a=4)
                H8 = [(h // 2, slice(64 * (h % 2), 64 * (h % 2) + 64), h % 2) for h in range(8)]
                for ci in range(NCK):
                    sl = slice(ci * CH, (ci + 1) * CH)
                    for si, src in enumerate((BT, KT, VBT)):
                        for j, p_, h2 in H8:
                            self.tr(pt[p_, si, j, :], src.t[p_, j, sl], IDB.t[p_, 64 * h2:64 * h2 + 64], [src.k[j]] + IDB.all, PT.all)
                    self.cp('act', TOK.t, pt, PT.all, TOK.all)
                    yield
                    for j, p_, h2 in H8:
                        self.mm(pa4[p_, j, :], BT.t[p_, j, sl], AR_.t[p_, j, :, sl], True, True, [BT.k[j], AR_.k[j]], PA.all)
                    self.tt('dve', MBS.t, pa4, m1b, ALU.mult, PA.all + CON.all, MBS.all)
                    yield
                    for j, p_, h2 in H8:
                        self.mm(pb4[p_, j, :], KT.t[p_, j, sl], AR_.t[p_, j, :, sl], True, True, [KT.k[j], AR_.k[j]], PB_.all)
                    self.tt('dve', MKS.t, pb4, m1b, ALU.mult, PB_.all + CON.all, MKS.all)
                    yield
                    for j, p_, h2 in H8:
                        self.mm(px[p_, 1, j, :], AR_.t[p_, j, 0, sl], BT.t[p_, j, sl], True, True, [BT.k[j], AR_.k[j]], PX.all)
                    self.tt('dve', XT0.t, px[:, 1], m2b, ALU.mult, PX.all + CON.all, XT0.all)
                    yield
                    for j, p_, h2 in H8:
                        self.mm(pw[p_, 0, j, :], AR_.t[p_, j, 0, sl], HB.t[p_, j, :], True, False, [AR_.k[j]] + HB.all, PW.all)
                        self.mm(pw[p_, 0, j, :], MKS.t[p_, j, 0:CH], TOK.t[p_, 2, j, :], False, True, MKS.all + TOK.all, PW.all)
                    self.cp('act', WF.t, pw[:, 0], PW.all, WF.all)
                    self.cp('dve', WB[0].t, pw[:, 0], PW.all, WB[0].all)
                    yield
                    Xc, XTc = (MBS.t[:, :, 0:CH], MBS.all), (XT0.t, XT0.all)
                    for it in range(6):
                        wb_in, wb_out = WB[it % 2], WB[(it + 1) % 2]
                        for j, p_, h2 in H8:
                            self.mm(pw[p_, 1, j, :], Xc[0][p_, j, :], wb_in.t[p_, j, :], True, True, Xc[1] + wb_in.all, PW.all)
                        if it < 5:
                            xx = XX[it % 2]
                            for j, p_, h2 in H8:
                                self.mm(px[p_, 0, j, :], XTc[0][p_, j, :], Xc[0][p_, j, :], True, True, Xc[1] + XTc[1], PX.all)
                                self.mm(px[p_, 1, j, :], Xc[0][p_, j, :], XTc[0][p_, j, :], True, True, Xc[1] + XTc[1], PX.all)
                        yield
                        self.tt('dve', WF.t, WF.t, pw[:, 1], ALU.add, WF.all + PW.all, WF.all)
                        if it < 5:
                            self.cp('act', xx.t, px, PX.all, xx.all)
                            Xc, XTc = (xx.t[:, 0], xx.all), (xx.t[:, 1], xx.all)
                        yield
                        self.cp('pool', wb_out.t, WF.t, WF.all, wb_out.all)
                        yield
                    UB = WB[0]
                    for j, p_, h2 in H8:
                        self.mm(pa4[p_, j, 0:CH], HB.t[p_, j, :], AR_.t[p_, j, 1, sl], True, False, HB.all + [AR_.k[j]], PA.all)
                        self.mm(pa4[p_, j, 0:CH], UB.t[p_, j, :], MBS.t[p_, j, CH:128], False, False, UB.all + MBS.all, PA.all)
                        self.mm(pa4[p_, j, 0:CH], TOK.t[p_, 2, j, :], MKS.t[p_, j, CH:128], False, True, TOK.all + MKS.all, PA.all)
                    self.cp('act', YFM.t[:, :, sl], pa4[:, :, 0:CH], PA.all, YFM.all)
                    yield
                    for j, p_, h2 in H8:
                        self.mm(pb4[p_, j, 0:CH], TOK.t[p_, 0, j, :], UB.t[p_, j, :], True, False, TOK.all + UB.all, PB_.all)
                        self.mm(pb4[p_, j, 0:CH], TOK.t[p_, 1, j, :], TOK.t[p_, 2, j, :], False, True, TOK.all, PB_.all)
                    self.tt('dve', HTMP.t, pb4[:, :, 0:CH], HF.t, ALU.add, PB_.all + HF.all, HTMP.all)
                    yield
                    pc = PC.t[:, :, ci:ci + 1].to_broadcast([128, 4, CH])
                    self.tt('dve', HF.t, HTMP.t, pc, ALU.mult, HTMP.all + PC.all, HF.all)
                    self.cp('pool', HB.t, HF.t, HF.all, HB.all)
                    yield

            def gn_chain(b, j):
                par = b % 2
                YFM, BON, G_ = pYFM[par], pBON[par], pG[par]
                t0, t1 = pT[j]
                t2 = pL[j][0]
                pi = yield from acquire()
                pm = PMA[pi]
                self.mm(pm.t[:, :N], BONES, YFM.t[:, j, :N], True, True, [YFM.k[j]] + CON.all, pm.all)
                yield
                self.stt(t0.t[:, :N], pm.t[:, :N], -1.0 / 64, YFM.t[:, j, :N], ALU.mult, ALU.add, pm.all + [YFM.k[j]], t0.all)
                release(pi)
                yield
                t1b = t1.t[:, :].bitcast(BF16)
                self.act(t1b[:, :N], t0.t[:, :N], AF.Square, t0.all, t1.all)
                yield
                pi = yield from acquire()
                pm = PMA[pi]
                self.mm(pm.t[:, :N], BONESB, t1b[:, :N], True, True, t1.all + ONB.all, pm.all)
                yield
                self.act(t2.t[:, :N], pm.t[:, :N], AF.Ln, pm.all, t2.all, scale=1.0 / 64, bias=64e-5)
                release(pi)
                yield
                self.act(t2.t[:, :N], t2.t[:, :N], AF.Exp, t2.all, t2.all, scale=-0.5)
                yield
                self.tt('dve', t0.t[:, :N], t0.t[:, :N], t2.t[:, :N], ALU.mult, t0.all + t2.all, t0.all)
                yield
                self.ts('dve', t0.t[:, :N], t0.t[:, :N], V('lnx_w', j), ALU.mult, t0.all + VEC.all, t0.all,
                        s2=V('lnx_b', j), op1=ALU.add)
                yield
                self.tt('pool', t0.t[:, :N], t0.t[:, :N], BON.t[:, j, :N], ALU.add, t0.all + [BON.k[j]], t0.all)
                yield
                self.tt('pool', pZB.t[:, j, :N], t0.t[:, :N], G_.t[:, j, :N], ALU.mult, t0.all + [G_.k[j]], [pZB.k[j]])
                yield

            def gproj(b, oc, gt):
                xn = pXN[b % 3]
                pi = yield from acquire()
                pm = PMA[pi]
                for k in range(8):
                    self.mm(pm.t[:, :N], W_IN.t[:, k, oc * 128:(oc + 1) * 128], xn.t[:, k, 3:N + 3], k == 0, k == 7,
                            [WK(oc), xn.k[k]], pm.all)
                yield
                self.act(gt.t[:, :N], pm.t[:, :N], AF.Tanh, pm.all, gt.all, scale=0.5)
                release(pi)
                yield

            def mix_chain(b, oc):
                par = b % 2
                cs = slice(oc * 128, (oc + 1) * 128)
                ga, gb = pGT[(2 * oc) % 4], pGT[(2 * oc + 1) % 4]
                yield from gproj(b, 18 + oc, ga)
                yield from gproj(b, 26 + oc, gb)
                pi = yield from acquire()
                pm = PMA[pi]
                for j in range(4):
                    self.mm(pm.t[:, :N], WOR.t[:, j, cs], pZB.t[:, j, :N], j == 0, j == 3, WOR.all + [pZB.k[j]], pm.all)
                yield
                self.stt(ga.t[:, :N], ga.t[:, :N], 1.0, pm.t[:, :N], ALU.add, ALU.mult, pm.all + ga.all, ga.all)
                release(pi)
                yield
                pi = yield from acquire()
                pm = PMA[pi]
                for j in range(4):
                    self.mm(pm.t[:, :N], WOL.t[:, j, cs], pHSB[par].t[:, j, :N], j == 0, j == 3, WOL.all + [pHSB[par].k[j]], pm.all)
                yield
                self.stt(gb.t[:, :N], gb.t[:, :N], 1.0, pm.t[:, :N], ALU.add, ALU.mult, pm.all + gb.all, gb.all)
                release(pi)
                yield
                self.tt('pool', pMIX.t[:, oc, :N], ga.t[:, :N], gb.t[:, :N], ALU.add, ga.all + gb.all, [pMIX.k[oc]])
                yield

            def wo_chain(b, oc):
                cs = slice(oc * 128, (oc + 1) * 128)
                mo = pT[oc % 4][oc // 4]
                pi = yield from acquire()
                pm = PMA[pi]
                for k in range(8):
                    self.mm(pm.t[:, :N], WO.t[:, k, cs], pMIX.t[:, k, :N], k == 0, k == 7, WO.all + [pMIX.k[k]], pm.all)
                yield
                self.act(mo.t[:, :N], pm.t[:, :N], AF.Identity, pm.all, mo.all, scale=0.5)
                release(pi)
                yield

            class _MO:
                pass
            MOB = _MO()
            MOB.k = pR.k + pK.k

            def S3front(b):
                yield from window([gn_chain(b, j) for j in range(4)], 4)
                yield from window([mix_chain(b, oc) for oc in range(8)], 2)

            def S3tail(b):
                xb = pXB[b % 3]
                yield from window([wo_chain(b, oc) for oc in range(8)], 4)
                mos = [pT[k % 4][k // 4] for k in range(8)]
                yield from rms_p([m_.t[:, :N] for m_ in mos], [m_.k[0] for m_ in mos], 1)
                for k in range(8):
                    mo = mos[k]
                    self.stt(mo.t[:, :N], mo.t[:, :N], V('g_post', k), RSH[1].t[:, :N], ALU.mult, ALU.mult,
                             mo.all + VEC.all + RSH[1].all, mo.all)
                    self.tt('pool', xb.t[:, k, :N], xb.t[:, k, :N], mo.t[:, :N], ALU.add, [xb.k[k]] + mo.all, [xb.k[k]])
                    yield
                self.dma('sp', x1v[:, :, b * N:(b + 1) * N], xb.t[:, :, :N], xb.all, [X1S], px1sem[b % 3])
                pflags[('tail', b)] = True
                yield

            def par(*gs):
                gs = list(gs)
                while gs:
                    for g_ in list(gs):
                        try:
                            next(g_)
                        except StopIteration:
                            gs.remove(g_)
                        yield

            rr([chain(S1head(0), S1rest(0))], [1])
            for b in range(NBLK):
                later = []
                if b > 0:
                    later.append(S3tail(b - 1))
                if b + 1 < NBLK:
                    later.append(S1rest(b + 1))
                mparts = []
                if b > 0:
                    mparts.append(S3front(b - 1))
                mparts.append(par(*later))
                streams = [chain(*mparts), S2(b)]
                weights = [8, 1]
                if b + 1 < NBLK:
                    streams.append(S1head(b + 1))
                    weights.append(1)
                rr(streams, weights)
            rr([chain(S3front(NBLK - 1), S3tail(NBLK - 1))], [1])
            self.cp('pool', WKP.t, HF.t, HF.all, WKP.all)
            S.barrier()
            A.off = markA
            N_ = NS
            XB = A.alloc("xb", [8, N_], F32, nk=8)
            XN = A.alloc("xn", [8, N_], BF16, nk=8)
            PBF = [A.alloc("pbf%d" % i, [N_ + 1], F32) for i in range(3)]
            DT = [A.alloc("dt%d" % i, [N_], F32) for i in range(3)]
            Q = [A.alloc("q%d" % i, [4, N_], F32, nk=4) for i in range(11)]
            LIN = A.alloc("lin", [N_], BF16)
            GS = A.alloc("gs", [N_], BF16)
            AR_ = A.alloc("ar", [4, 2, N_], BF16, nk=4)
            BT = A.alloc("bt", [4, N_], BF16, nk=4)
            KT = A.alloc("kt", [4, N_], BF16, nk=4)
            VBT = A.alloc("vbt", [4, N_], BF16, nk=4)
            TOK = A.alloc("tok", [3, 4, CH], BF16)
            MBS = A.alloc("mbs", [4, 128], BF16)
            MKS = A.alloc("mks", [4, 128], BF16)
            XX = [A.alloc("xx%d" % i, [2, 4, CH], BF16) for i in range(2)]
            XT0 = A.alloc("xt0", [4, CH], BF16)
            WF = A.alloc("wf", [4, CH], F32)
            WB = [A.alloc("wb%d" % i, [4, CH], BF16) for i in range(2)]
            HF = A.alloc("hf", [4, CH], F32)
            HB = A.alloc("hb", [4, CH], BF16)
            HTMP = A.alloc("htmp", [4, CH], F32)
            XBB = A.alloc("xbb", [4, N_ + 3], F32, nk=4)
            XCB = A.alloc("xcb", [N_], BF16)
            HSB = A.alloc("hsb", [4, N_], BF16, nk=4)
            HCAR = A.alloc("hcar", [4], F32)
            ZB = A.alloc("zb", [4, N_], BF16, nk=4)
            MIXB = A.alloc("mixb", [8, N_], BF16, nk=8)
            GT = [A.alloc("gt%d" % i, [N_], F32) for i in range(2)]
            DTJ = [[A.alloc("dtj%d_%d" % (j_, i), [N_], F32) for i in range(3)] for j_ in range(4)]
            GTJ = [[A.alloc("gtj%d_%d" % (j_, i), [N_], F32) for i in range(2)] for j_ in range(4)]
            XCBJ = [A.alloc("xcbj%d" % j_, [N_], BF16) for j_ in range(4)]
            HS = A.alloc("hs", [4, NS, 64], F32, nk=4)
            SPCS = [[A.alloc("spc%d_%d" % (h_, i), [512], F32) for i in range(5)] for h_ in range(2)]
            SST = {n_: A.alloc(n_, sh, F32) for n_, sh in
                   (("sshift", [14, NS]), ("slconv", [4, 3, NS]), ("slh", [4, NS]))}
            print("phase A arena used", A.off, "top", A.top)

            self.memset('pool', HF.t, 0.0, HF.all)
            self.memset('pool', HB.t, 0.0, HB.all)
            self.memset('pool', HCAR.t, 0.0, HCAR.all)
            self.memset('pool', XBB.t, 0.0, XBB.all)
            ssem = [S.new_dma_sem() for _ in range(4)]
            self.dma('sp', SST["sshift"].t, s_shift.rearrange("p (a b) -> p a b", a=14), [], SST["sshift"].all, ssem[0])
            self.dma('sp', SST["slconv"].t, s_lconv.rearrange("p (a b c) -> p a b c", a=4, b=3), [], SST["slconv"].all, ssem[1])
            self.dma('sp', SST["slh"].t, s_lh.rearrange("p (a b) -> p a b", a=4), [], SST["slh"].all, ssem[2])
            self.dma('sp', HS.t, s_wkv.rearrange("p (a b c) -> p a b c", a=4, b=NS), [], HS.all, ssem[3])

            xsem = S.new_dma_sem()
            x1sem = S.new_dma_sem()
            R_, K_, V_, SG, CSG, ASIG, KKN, ECW, ENCW, G_, BON = Q
            YFM = SG
            MOF = None
            xTv = xT.rearrange("(k p) n -> p k n", p=128)
            x1v = x1s.rearrange("(k p) n -> p k n", p=128)
            yTv = yT.rearrange("(k p) n -> p k n", p=128)

            PMS = [PM[0], PM[1], PW]
            pm_res = {}

            def pm_next():
                while True:
                    for d_ in range(3):
                        i_ = (self.pmi + d_) % 3
                        b_ = PMS[i_]
                        lw = b_.k[0].last_w
                        if i_ in pm_res:
                            if lw is not None and lw != pm_res[i_] and lw[1] != 'pe':
                                del pm_res[i_]
                        if i_ not in pm_res:
                            pm_res[i_] = lw
                            self.pmi = (i_ + 1) % 3
                            return b_
                    assert getattr(_TL, 'sw', None) is not None, "no free PSUM bank in sequential mode"
                    co_switch()

            def rms(src, nch, N, scale_div):
                for k in range(nch):
                    sq = SQ[k % 2]
                    self.act(sq.t[:, :N], src.t[:, k, :N], AF.Square, [src.k[k]], sq.all)
                    self.mm(PST.t[:, :N], ONESF, sq.t[:, :N], k == 0, k == nch - 1, sq.all + CON.all, PST.all)
                self.act(RSTD.t[:, :N], PST.t[:, :N], AF.Sqrt, PST.all, RSTD.all, scale=1.0 / scale_div, bias=1e-6)
                self.recip(RSTD.t[:, :N], RSTD.t[:, :N], RSTD.all, RSTD.all)

            def proj(oc, N):
                pm = pm_next()
                for k in range(8):
                    self.mm(pm.t[:, :N], W_IN.t[:, k, oc * 128:(oc + 1) * 128], XN.t[:, k, :N], k == 0, k == 7,
                            [WK(oc), XN.k[k]], pm.all)
                return pm

            def blockA(c0, N, sample, last):
                self.dma('sp', XB.t[:, :, :N], xTv[:, :, c0:c0 + N], [], XB.all, xsem)
                rms(XB, 8, N, D)
                for k in range(8):
                    self.stt(XN.t[:, k, :N], XB.t[:, k, :N], V('g_pre', k), RSTD.t[:, :N], ALU.mult, ALU.mult,
                             [XB.k[k], VEC.k[0], RSTD.k[0]], [XN.k[k]])
                WA = DT[2]
                for oc in range(14):
                    pm = proj(oc, N)
                    if oc < 12:
                        dst = Q[oc // 4]
                        dst_ap, dst_k = dst.t[:, oc % 4, :N], [dst.k[oc % 4]]
                    else:
                        dst_ap, dst_k = WA.t[:, :N], WA.all
                    dt_ = DT[oc % 2]
                    if not sample:
                        pb = PBF[oc % 3]
                        self.cp('act', pb.t[:, 1:N + 1], pm.t[:, :N], pm.all, pb.all)
                        self.cp('pool', pb.t[:, 0:1], SHO.t[:, oc, 0:1], SHO.all, pb.all)
                        self.cp('pool', SHO.t[:, oc, 0:1], pb.t[:, N:N + 1], pb.all, SHO.all)
                        self.tt('dve', dt_.t[:, :N], pb.t[:, 0:N], pb.t[:, 1:N + 1], ALU.subtract, pb.all, dt_.all)
                        self.stt(dst_ap, dt_.t[:, :N], V('mu', oc), pb.t[:, 1:N + 1], ALU.mult, ALU.add,
                                 dt_.all + pb.all + VEC.all, dst_k)
                    else:
                        pcur = SHO.t[:, oc, 1:1 + NS]
                        self.cp('act', pcur, pm.t[:, :N], pm.all, SHO.all)
                        self.tt('dve', dt_.t[:, :N], SST["sshift"].t[:, oc, :], pcur, ALU.subtract,
                                SHO.all + SST["sshift"].all, dt_.all)
                        self.stt(dst_ap, dt_.t[:, :N], V('mu', oc), pcur, ALU.mult, ALU.add,
                                 dt_.all + SHO.all + VEC.all, dst_k)
                    if oc == 12:
                        self.act(LIN.t[0:64, :N], WA.t[0:64, :N], AF.Tanh, WA.all, LIN.all)
                        self.cp('pool', LIN.t[64:128, :N], WA.t[64:128, :N], WA.all, LIN.all)
                    if oc == 13:
                        self.act(GS.t[:, :N], WA.t[:, :N], AF.Sigmoid, WA.all, GS.all)
                for j in range(4):
                    pm = proj(14 + j, N)
                    if not sample:
                        self.cp('act', XBB.t[:, j, 3:N + 3], pm.t[:, :N], pm.all, [XBB.k[j]])
                    else:
                        self.cp('act', LCO.t[:, j, 2, 1:1 + NS], pm.t[:, :N], pm.all, LCO.all)
                def _body1(j):
                    DT, GT, XCB = DTJ[j], GTJ[j], XCBJ[j]
                    cs = slice(j * 128, (j + 1) * 128)
                    pm = pm_next()
                    self.mm(pm.t[:, :N], LORA.t[0:64, cs], LIN.t[0:64, :N], True, True, LORA.all + LIN.all, pm.all)
                    self.act(SG.t[:, j, :N], pm.t[:, :N], AF.Sigmoid, pm.all + VEC.all, [SG.k[j]], bias=V('w0', j))
                    pm = pm_next()
                    self.mm(pm.t[:, :N], LORA.t[64:128, cs], LIN.t[64:128, :N], True, True, LORA.all + LIN.all, pm.all)
                    self.act(ASIG.t[:, j, :N], pm.t[:, :N], AF.Sigmoid, pm.all + VEC.all, [ASIG.k[j]], bias=V('a0', j))
                    pm = pm_next()
                    self.mm(pm.t[:, :N], GUP.t[:, cs], GS.t[:, :N], True, True, GUP.all + GS.all, pm.all)
                    self.cp('act', G_.t[:, j, :N], pm.t[:, :N], pm.all, [G_.k[j]])
                    t0, t1 = DT[0], DT[1]
                    self.act(t0.t[:, :N], K_.t[:, j, :N], AF.Square, [K_.k[j]] + VEC.all, t0.all, scale=V('k_k', j))
                    pm = pm_next()
                    self.mm(pm.t[:, :N], BONES, t0.t[:, :N], True, True, t0.all + CON.all, pm.all)
                    self.act(t1.t[:, :N], pm.t[:, :N], AF.Sqrt, pm.all, t1.all)
                    self.ts('dve', t1.t[:, :N], t1.t[:, :N], 1e-12, ALU.max, t1.all, t1.all)
                    self.recip(t1.t[:, :N], t1.t[:, :N], t1.all, t1.all)
                    self.stt(KKN.t[:, j, :N], K_.t[:, j, :N], V('k_k', j), t1.t[:, :N], ALU.mult, ALU.mult,
                             [K_.k[j]] + VEC.all + t1.all, [KKN.k[j]])
                    self.ts('dve', t0.t[:, :N], ASIG.t[:, j, :N], -1.0, ALU.add, [ASIG.k[j]] + VEC.all, t0.all,
                            s2=V('k_a', j), op1=ALU.mult)
                    self.stt(K_.t[:, j, :N], t0.t[:, :N], 1.0, K_.t[:, j, :N], ALU.add, ALU.mult,
                             t0.all + [K_.k[j]], [K_.k[j]])
                    self.stt(t0.t[:, :N], R_.t[:, j, :N], V('r_k', j), K_.t[:, j, :N], ALU.mult, ALU.mult,
                             [R_.k[j], K_.k[j]] + VEC.all, t0.all)
                    pm = pm_next()
                    self.mm(pm.t[:, :N], BONES, t0.t[:, :N], True, True, t0.all + CON.all, pm.all)
                    self.tt('dve', BON.t[:, j, :N], pm.t[:, :N], V_.t[:, j, :N], ALU.mult, pm.all + [V_.k[j]], [BON.k[j]])
                co_run([(lambda j=j: _body1(j)) for j in range(4)])
                if not sample:
                    wkv_prompt(N)
                else:
                    wkv_sample()
                def _body2(j):
                    DT, GT, XCB = DTJ[j], GTJ[j], XCBJ[j]
                    xc, gxs, gas, uu = GT[0], DT[0], DT[1], DT[2]
                    if not sample:
                        taps = [XBB.t[:, j, i:i + N] for i in range(4)]
                        tr_ = [XBB.k[j]]
                    else:
                        taps = [SST["slconv"].t[:, j, i, :] for i in range(3)] + [LCO.t[:, j, 2, 1:1 + NS]]
                        tr_ = SST["slconv"].all + LCO.all
                    self.act(xc.t[:, :N], taps[3], AF.Identity, tr_ + VEC.all, xc.all, scale=V('lcw', 12 + j), bias=V('lcb', j))
                    for i in range(3):
                        self.stt(xc.t[:, :N], taps[i], V('lcw', 4 * i + j), xc.t[:, :N], ALU.mult, ALU.add,
                                 tr_ + VEC.all + xc.all, xc.all)
                    if not sample:
                        self.cp('pool', XBB.t[:, j, 0:3], XBB.t[:, j, N:N + 3], [XBB.k[j]], [XBB.k[j]])
                        if last:
                            self.cp('pool', LCO.t[:, j, :, 0], XBB.t[:, j, N:N + 3], [XBB.k[j]], LCO.all)
                    else:
                        self.cp('pool', LCO.t[:, j, 0:2, 1:1 + NS], SST["slconv"].t[:, j, 1:3, :], SST["slconv"].all, LCO.all)
                    self.cp('act', XCB.t[:, :N], xc.t[:, :N], xc.all, XCB.all)
                    pgx, pga = pm_next(), pm_next()
                    for h2 in range(2):
                        ps_ = slice(64 * h2, 64 * h2 + 64)
                        self.mm(pgx.t[ps_, :N], GXA.t[ps_, 0, j, :], XCB.t[ps_, :N], True, True, GXA.all + XCB.all, pgx.all)
                        self.mm(pga.t[ps_, :N], GXA.t[ps_, 1, j, :], XCB.t[ps_, :N], True, True, GXA.all + XCB.all, pga.all)
                    self.act(gxs.t[:, :N], pgx.t[:, :N], AF.Sigmoid, pgx.all + VEC.all, gxs.all, bias=V('gxb', j))
                    self.act(gas.t[:, :N], pga.t[:, :N], AF.Sigmoid, pga.all + VEC.all, gas.all, bias=V('gab', j))
                    a_ = GT[1]
                    self.act(a_.t[:, :N], gas.t[:, :N], AF.Exp, gas.all + DER.all, a_.all, scale=DER.t[:, j:j + 1])
                    self.act(gas.t[:, :N], gas.t[:, :N], AF.Exp, gas.all + DER.all, gas.all, scale=DER.t[:, 4 + j:5 + j])
                    self.ts('dve', gas.t[:, :N], gas.t[:, :N], -1.0, ALU.mult, gas.all, gas.all, s2=1.0, op1=ALU.add)
                    self.act(gas.t[:, :N], gas.t[:, :N], AF.Sqrt, gas.all, gas.all)
                    self.tt('dve', uu.t[:, :N], gxs.t[:, :N], xc.t[:, :N], ALU.mult, gxs.all + xc.all, uu.all)
                    self.tt('dve', uu.t[:, :N], uu.t[:, :N], gas.t[:, :N], ALU.mult, uu.all + gas.all, uu.all)
                    hs_ = xc
                    if not sample:
                        self.scan(hs_.t[:, :N], a_.t[:, :N], uu.t[:, :N], HCAR.t[:, j:j + 1], a_.all + uu.all + HCAR.all, hs_.all)
                        self.cp('pool', HCAR.t[:, j:j + 1], hs_.t[:, N - 1:N], hs_.all, HCAR.all)
                        if last:
                            self.cp('pool', LHO.t[:, j, 0:1], hs_.t[:, N - 1:N], hs_.all, LHO.all)
                    else:
                        self.tt('dve', hs_.t[:, :N], a_.t[:, :N], SST["slh"].t[:, j, :], ALU.mult, a_.all + SST["slh"].all, hs_.all)
                        self.tt('dve', hs_.t[:, :N], hs_.t[:, :N], uu.t[:, :N], ALU.add, hs_.all + uu.all, hs_.all)
                        self.cp('pool', LHO.t[:, j, 1:1 + NS], hs_.t[:, :N], hs_.all, LHO.all)
                    self.cp('act', HSB.t[:, j, :N], hs_.t[:, :N], hs_.all, [HSB.k[j]])
                co_run([(lambda j=j: _body2(j)) for j in range(4)])
                def _body3(j):
                    DT, GT, XCB = DTJ[j], GTJ[j], XCBJ[j]
                    t0, t1, t2 = DT[0], DT[1], DT[2]
                    pm = pm_next()
                    self.mm(pm.t[:, :N], BONES, YFM.t[:, j, :N], True, True, [YFM.k[j]] + CON.all, pm.all)
                    self.stt(t0.t[:, :N], pm.t[:, :N], -1.0 / 64, YFM.t[:, j, :N], ALU.mult, ALU.add, pm.all + [YFM.k[j]], t0.all)
                    self.act(t1.t[:, :N], t0.t[:, :N], AF.Square, t0.all, t1.all)
                    pm = pm_next()
                    self.mm(pm.t[:, :N], BONES, t1.t[:, :N], True, True, t1.all + CON.all, pm.all)
                    self.act(t2.t[:, :N], pm.t[:, :N], AF.Sqrt, pm.all, t2.all, scale=1.0 / 64, bias=64e-5)
                    self.recip(t2.t[:, :N], t2.t[:, :N], t2.all, t2.all)
                    self.tt('dve', t0.t[:, :N], t0.t[:, :N], t2.t[:, :N], ALU.mult, t0.all + t2.all, t0.all)
                    self.ts('dve', t0.t[:, :N], t0.t[:, :N], V('lnx_w', j), ALU.mult, t0.all + VEC.all, t0.all,
                            s2=V('lnx_b', j), op1=ALU.add)
                    self.tt('dve', t0.t[:, :N], t0.t[:, :N], BON.t[:, j, :N], ALU.add, t0.all + [BON.k[j]], t0.all)
                    self.tt('dve', ZB.t[:, j, :N], t0.t[:, :N], G_.t[:, j, :N], ALU.mult, t0.all + [G_.k[j]], [ZB.k[j]])
                co_run([(lambda j=j: _body3(j)) for j in range(4)])
                def _body4(oc):
                    DT, GT = DTJ[oc % 4], GTJ[oc % 4]
                    cs = slice(oc * 128, (oc + 1) * 128)
                    pga = proj(18 + oc, N)
                    self.act(GT[0].t[:, :N], pga.t[:, :N], AF.Sigmoid, pga.all, GT[0].all)
                    pgb = proj(26 + oc, N)
                    self.act(GT[1].t[:, :N], pgb.t[:, :N], AF.Sigmoid, pgb.all, GT[1].all)
                    poa = pm_next()
                    for j in range(4):
                        self.mm(poa.t[:, :N], WOR.t[:, j, cs], ZB.t[:, j, :N], j == 0, j == 3, WOR.all + [ZB.k[j]], poa.all)
                    self.tt('dve', GT[0].t[:, :N], poa.t[:, :N], GT[0].t[:, :N], ALU.mult, poa.all + GT[0].all, GT[0].all)
                    pob = pm_next()
                    for j in range(4):
                        self.mm(pob.t[:, :N], WOL.t[:, j, cs], HSB.t[:, j, :N], j == 0, j == 3, WOL.all + [HSB.k[j]], pob.all)
                    self.tt('dve', GT[1].t[:, :N], pob.t[:, :N], GT[1].t[:, :N], ALU.mult, pob.all + GT[1].all, GT[1].all)
                    self.tt('dve', MIXB.t[:, oc, :N], GT[0].t[:, :N], GT[1].t[:, :N], ALU.add, GT[0].all + GT[1].all, [MIXB.k[oc]])
                co_run([(lambda oc=oc: _body4(oc)) for oc in range(4)])
                co_run([(lambda oc=oc: _body4(oc)) for oc in range(4, 8)])
                MO = [R_, K_]
                def _body5(oc):
                    DT, GT = DTJ[oc % 4], GTJ[oc % 4]
                    cs = slice(oc * 128, (oc + 1) * 128)
                    pm = pm_next()
                    for k in range(8):
                        self.mm(pm.t[:, :N], WO.t[:, k, cs], MIXB.t[:, k, :N], k == 0, k == 7, WO.all + [MIXB.k[k]], pm.all)
                    mo = MO[oc // 4]
                    self.cp('act', mo.t[:, oc % 4, :N], pm.t[:, :N], pm.all, [mo.k[oc % 4]])
                co_run([(lambda oc=oc: _body5(oc)) for oc in range(4)])
                co_run([(lambda oc=oc: _body5(oc)) for oc in range(4, 8)])
                for k in range(8):
                    mo = MO[k // 4]
                    sq = SQ[k % 2]
                    self.act(sq.t[:, :N], mo.t[:, k % 4, :N], AF.Square, [mo.k[k % 4]], sq.all)
                    self.mm(PST.t[:, :N], ONESF, sq.t[:, :N], k == 0, k == 7, sq.all + CON.all, PST.all)
                self.act(RSTD.t[:, :N], PST.t[:, :N], AF.Sqrt, PST.all, RSTD.all, scale=1.0 / D, bias=1e-6)
                self.recip(RSTD.t[:, :N], RSTD.t[:, :N], RSTD.all, RSTD.all)
                for k in range(8):
                    mo = MO[k // 4]
                    self.stt(mo.t[:, k % 4, :N], mo.t[:, k % 4, :N], V('g_post', k), RSTD.t[:, :N], ALU.mult, ALU.mult,
                             [mo.k[k % 4]] + VEC.all + RSTD.all, [mo.k[k % 4]])
                    self.tt('dve', XB.t[:, k, :N], XB.t[:, k, :N], mo.t[:, k % 4, :N], ALU.add, [XB.k[k], mo.k[k % 4]], [XB.k[k]])
                self.dma('sp', x1v[:, :, c0:c0 + N], XB.t[:, :, :N], XB.all, [X1S], x1sem)

            def wkv_prompt(N):
                nchk = N // CH
                c = CDEC
                for j in range(4):
                    for ci in range(nchk):
                        sl = slice(ci * CH, (ci + 1) * CH)
                        self.scan(CSG.t[:, j, sl], ONES64, SG.t[:, j, sl], 0.0, [SG.k[j]] + CON.all, [CSG.k[j]])
                    self.act(ECW.t[:, j, :N], CSG.t[:, j, :N], AF.Exp, [CSG.k[j]], [ECW.k[j]], scale=-c)
                    self.act(ENCW.t[:, j, :N], CSG.t[:, j, :N], AF.Exp, [CSG.k[j]], [ENCW.k[j]], scale=c)
                    v3 = lambda b_, a, bnd: b_.t[:, j, :N].rearrange("p (c t) -> p c t", t=CH)[:, :, a:bnd]
                    at3 = AR_.t[:, j, 0, :N].rearrange("p (c t) -> p c t", t=CH)
                    self.stt(at3[:, :, 1:CH], v3(KKN, 1, CH), -1.0, v3(ECW, 0, CH - 1), ALU.mult, ALU.mult,
                             [KKN.k[j], ECW.k[j]], [AR_.k[j]])
                    self.ts('dve', at3[:, :, 0:1], v3(KKN, 0, 1), -1.0, ALU.mult, [KKN.k[j]], [AR_.k[j]])
                    self.tt('dve', AR_.t[:, j, 1, :N], R_.t[:, j, :N], ECW.t[:, j, :N], ALU.mult, [R_.k[j], ECW.k[j]], [AR_.k[j]])
                    t0 = DT[0]
                    self.tt('dve', t0.t[:, :N], KKN.t[:, j, :N], ASIG.t[:, j, :N], ALU.mult, [KKN.k[j], ASIG.k[j]], t0.all)
                    self.tt('dve', BT.t[:, j, :N], t0.t[:, :N], ENCW.t[:, j, :N], ALU.mult, t0.all + [ENCW.k[j]], [BT.k[j]])
                    self.tt('dve', KT.t[:, j, :N], K_.t[:, j, :N], ENCW.t[:, j, :N], ALU.mult, [K_.k[j], ENCW.k[j]], [KT.k[j]])
                    self.cp('act', VBT.t[:, j, :N], V_.t[:, j, :N], [V_.k[j]], [VBT.k[j]])
                m1b = MASK1.unsqueeze(1).to_broadcast([128, 4, 128])
                m2b = MASK2.unsqueeze(1).to_broadcast([128, 4, CH])
                pa4 = PA.t[:, :].rearrange("p (a b) -> p a b", a=4)
                pb4 = PB_.t[:, :].rearrange("p (a b) -> p a b", a=4)
                px = PX.t[:, :].rearrange("p (s a b) -> p s a b", s=2, a=4)
                pw = PW.t[:, :].rearrange("p (s a b) -> p s a b", s=2, a=4)
                pt = PT.t[:, 0:3 * 4 * CH].rearrange("p (s a b) -> p s a b", s=3, a=4)
                for ci in range(nchk):
                    sl = slice(ci * CH, (ci + 1) * CH)
                    allk = lambda b_: b_.all
                    for si, src in enumerate((BT, KT, VBT)):
                        for h in range(8):
                            j, h2 = h // 2, h % 2
                            p_ = slice(64 * h2, 64 * h2 + 64)
                            self.tr(pt[p_, si, j, :], src.t[p_, j, sl], IDB.t[p_, 64 * h2:64 * h2 + 64], [src.k[j]] + IDB.all, PT.all)
                    self.cp('act', TOK.t, pt, PT.all, TOK.all)
                    for h in range(8):
                        j, h2 = h // 2, h % 2
                        p_ = slice(64 * h2, 64 * h2 + 64)
                        self.mm(pa4[p_, j, :], BT.t[p_, j, sl], AR_.t[p_, j, :, sl], True, True, [BT.k[j], AR_.k[j]], PA.all)
                        self.mm(pb4[p_, j, :], KT.t[p_, j, sl], AR_.t[p_, j, :, sl], True, True, [KT.k[j], AR_.k[j]], PB_.all)
                        self.mm(px[p_, 1, j, :], AR_.t[p_, j, 0, sl], BT.t[p_, j, sl], True, True, [BT.k[j], AR_.k[j]], PX.all)
                    self.tt('dve', MBS.t, pa4, m1b, ALU.mult, PA.all + CON.all, MBS.all)
                    self.tt('dve', MKS.t, pb4, m1b, ALU.mult, PB_.all + CON.all, MKS.all)
                    self.tt('dve', XT0.t, px[:, 1], m2b, ALU.mult, PX.all + CON.all, XT0.all)
                    for h in range(8):
                        j, h2 = h // 2, h % 2
                        p_ = slice(64 * h2, 64 * h2 + 64)
                        self.mm(pw[p_, 0, j, :], AR_.t[p_, j, 0, sl], HB.t[p_, j, :], True, False, [AR_.k[j]] + HB.all, PW.all)
                        self.mm(pw[p_, 0, j, :], MKS.t[p_, j, 0:CH], TOK.t[p_, 2, j, :], False, True, MKS.all + TOK.all, PW.all)
                    self.cp('act', WF.t, pw[:, 0], PW.all, WF.all)
                    self.cp('dve', WB[0].t, pw[:, 0], PW.all, WB[0].all)
                    Xc, XTc = (MBS.t[:, :, 0:CH], MBS.all), (XT0.t, XT0.all)
                    for it in range(6):
                        wb_in, wb_out = WB[it % 2], WB[(it + 1) % 2]
                        for h in range(8):
                            j, h2 = h // 2, h % 2
                            p_ = slice(64 * h2, 64 * h2 + 64)
                            self.mm(pw[p_, 1, j, :], Xc[0][p_, j, :], wb_in.t[p_, j, :], True, True, Xc[1] + wb_in.all, PW.all)
                        if it < 5:
                            xx = XX[it % 2]
                            for h in range(8):
                                j, h2 = h // 2, h % 2
                                p_ = slice(64 * h2, 64 * h2 + 64)
                                self.mm(px[p_, 0, j, :], XTc[0][p_, j, :], Xc[0][p_, j, :], True, True, Xc[1] + XTc[1], PX.all)
                                self.mm(px[p_, 1, j, :], Xc[0][p_, j, :], XTc[0][p_, j, :], True, True, Xc[1] + XTc[1], PX.all)
                        self.tt('dve', WF.t, WF.t, pw[:, 1], ALU.add, WF.all + PW.all, WF.all)
                        self.cp('act', wb_out.t, WF.t, WF.all, wb_out.all)
                        if it < 5:
                            self.cp('act', xx.t, px, PX.all, xx.all)
                            Xc, XTc = (xx.t[:, 0], xx.all), (xx.t[:, 1], xx.all)
                    UB = WB[0]
                    for h in range(8):
                        j, h2 = h // 2, h % 2
                        p_ = slice(64 * h2, 64 * h2 + 64)
                        self.mm(pa4[p_, j, 0:CH], HB.t[p_, j, :], AR_.t[p_, j, 1, sl], True, False, HB.all + [AR_.k[j]], PA.all)
                        self.mm(pa4[p_, j, 0:CH], UB.t[p_, j, :], MBS.t[p_, j, CH:128], False, False, UB.all + MBS.all, PA.all)
                        self.mm(pa4[p_, j, 0:CH], TOK.t[p_, 2, j, :], MKS.t[p_, j, CH:128], False, True, TOK.all + MKS.all, PA.all)
                    self.cp('act', YFM.t[:, :, sl], pa4[:, :, 0:CH], PA.all, YFM.all)
                    for h in range(8):
                        j, h2 = h // 2, h % 2
                        p_ = slice(64 * h2, 64 * h2 + 64)
                        self.mm(pb4[p_, j, 0:CH], TOK.t[p_, 0, j, :], UB.t[p_, j, :], True, False, TOK.all + UB.all, PB_.all)
                        self.mm(pb4[p_, j, 0:CH], TOK.t[p_, 1, j, :], TOK.t[p_, 2, j, :], False, True, TOK.all, PB_.all)
                    self.tt('dve', HTMP.t, pb4[:, :, 0:CH], HF.t, ALU.add, PB_.all + HF.all, HTMP.all)
                    pc = ECW.t[:, :, ci * CH + CH - 1:ci * CH + CH].to_broadcast([128, 4, CH])
                    self.tt('dve', HF.t, HTMP.t, pc, ALU.mult, HTMP.all + ECW.all, HF.all)
                    self.cp('act', HB.t, HF.t, HF.all, HB.all)

            def wkv_sample():
                N = NS
                WD, BV = ECW, ENCW
                for j in range(4):
                    self.act(WD.t[:, j, :N], SG.t[:, j, :N], AF.Exp, [SG.k[j]], [WD.k[j]], scale=-CDEC)
                    self.tt('dve', BV.t[:, j, :N], KKN.t[:, j, :N], ASIG.t[:, j, :N], ALU.mult, [KKN.k[j], ASIG.k[j]], [BV.k[j]])
                i64b = I64.unsqueeze(1).to_broadcast([128, 8, 64])
                ov = o_wkvs.rearrange("p (a b c) -> p a b c", a=4, b=NS)

                def piece(j, half, bk, spc, sems, pi_):
                    PA_, PB2, PX_ = bk
                    ns = slice(half * 8, half * 8 + 8)
                    bc = lambda b_: b_.t[:, j, ns].unsqueeze(2).to_broadcast([128, 8, 64])
                    hs3 = HS.t[:, j, ns, :]
                    r3 = lambda s_: s_.t[:, :].rearrange("p (a b) -> p a b", a=8)
                    P1, VD, HN, TMP = spc[0], spc[1], spc[2 + pi_ % 2], spc[4]
                    p1, vd, hn, tmp = r3(P1), r3(VD), r3(HN), r3(TMP)
                    self.stt(p1, hs3, -1.0, bc(KKN), ALU.mult, ALU.mult, [HS.k[j], KKN.k[j]], P1.all)
                    self.mm(PA_.t[:, :], BONES, P1.t[:, :], True, True, P1.all + CON.all, PA_.all)
                    self.tt('pool', vd, bc(V_), i64b, ALU.mult, [V_.k[j]] + CON.all, VD.all)
                    self.mm(PB2.t[:, :], BONES, VD.t[:, :], True, True, VD.all + CON.all, PB2.all)
                    self.tt('pool', hn, hs3, bc(WD), ALU.mult, [HS.k[j], WD.k[j]], HN.all)
                    self.tt('dve', tmp, r3(PA_), bc(BV), ALU.mult, PA_.all + [BV.k[j]], TMP.all)
                    self.tt('pool', hn, hn, tmp, ALU.add, HN.all + TMP.all, HN.all)
                    self.tt('dve', tmp, r3(PB2), bc(K_), ALU.mult, PB2.all + [K_.k[j]], TMP.all)
                    self.tt('pool', hn, hn, tmp, ALU.add, HN.all + TMP.all, HN.all)
                    self.dma('sp', ov[:, j, ns, :], hn, HN.all, [], sems[pi_ % 2])
                    self.tt('pool', p1, hn, bc(R_), ALU.mult, HN.all + [R_.k[j]], P1.all)
                    self.mm(PX_.t[:, :], BONES, P1.t[:, :], True, True, P1.all + CON.all, PX_.all)
                    self.tt('dve', tmp, r3(PX_), i64b, ALU.mult, PX_.all + CON.all, TMP.all)
                    self.reduce(YFM.t[:, j, ns], tmp, TMP.all, [YFM.k[j]])

                def runner(half, bk, spc, sems):
                    for j in range(4):
                        piece(j, half, bk, spc, sems, j)
                co_run([lambda: runner(0, (PA, PB_, PX), SPCS[0], self.wkvs_sems[0:2]),
                        lambda: runner(1, (PM[0], PM[1], PST), SPCS[1], self.wkvs_sems[2:4])])

            self.wkvs_sems = [S.new_dma_sem() for _ in range(4)]
            blockA(T, NS, True, False)

            osem = [S.new_dma_sem() for _ in range(5)]
            self.dma('sp', o_shift.rearrange("p (a b) -> p a b", a=14), SHO.t, SHO.all, [], osem[0])
            self.dma('sp', o_lconv.rearrange("p (a b c) -> p a b c", a=4, b=3), LCO.t, LCO.all, [], osem[1])
            self.dma('sp', o_lh.rearrange("p (a b) -> p a b", a=4), LHO.t, LHO.all, [], osem[2])
            self.dma('sp', o_wkvp.rearrange("p (a b) -> p a b", a=4), WKP.t, WKP.all, [], osem[4])
            S.barrier()
            A.off, A.top = mark
            WUP = A.alloc("wup", [8, 2 * DFF], BF16, nk=8)
            WDN = A.alloc("wdn", [22, D], BF16, nk=22)
            N_ = NBB
            XB2 = [A.alloc("xb2_%d" % i, [8, N_], F32, nk=8) for i in range(2)]
            XN2 = [A.alloc("xn2_%d" % i, [8, N_ + 2], BF16, nk=8) for i in range(2)]
            UU = [A.alloc("uu%d" % i, [N_], F32) for i in range(10)]
            HM = A.alloc("hm", [22, N_], BF16, nk=22)
            FO = A.alloc("fo", [8, N_], F32, nk=8)
            SFC = A.alloc("sfc", [44, 2, NS], F32)
            FCO = A.alloc("fco", [44, 2, 17], F32)
            print("phase B arena used", A.off, "top", A.top)
            PMB = [Buf("pmb%d" % i, ps[i][:, :], 1, True) for i in (0, 1, 3, 4, 5, 6)]
            PDN = Buf("pdn", pst[:, :].bitcast(F32), 1, True)
            wsb = [S.new_dma_sem() for _ in range(10)]
            wuv = wup.rearrange("(k p) n -> p k n", p=128)
            WUPT = {}
            qi = 0
            for g_ in range(4):
                for half in range(2):
                    c0_ = half * DFF + 768 * g_
                    c1_ = half * DFF + min(768 * (g_ + 1), DFF)
                    WUPT[(g_, half)] = Tk("wupt%d_%d" % (g_, half))
                    self.dma('pool', WUP.t[:, :, c0_:c1_], wuv[:, :, c0_:c1_], [], [WUPT[(g_, half)]], wsb[qi])
                    qi += 1
            wdv = wdn.rearrange("(k p) n -> p k n", p=128)
            self.dma('pool', WDN.t[:, 0:11, :], wdv[:, 0:11, :], [], WDN.k[0:11], wsb[8])
            self.dma('pool', WDN.t[:, 11:22, :], wdv[:, 11:22, :], [], WDN.k[11:22], wsb[9])
            sfs = S.new_dma_sem()
            self.dma('sp', SFC.t, s_fconv.rearrange("p (a b c) -> p a b c", a=44, b=2), [], SFC.all, sfs)
            self.memset('pool', FCO.t, 0.0, FCO.all)
            x2sem = [S.new_dma_sem() for _ in range(2)]
            ysem = S.new_dma_sem()
            blocks = [(bi * NBB, NBB, False) for bi in range(T // NBB)] + [(T, NS, True)]
            nb_ = len(blocks)

            def rms_g(src, N):
                for k in range(8):
                    sq = SQ[k % 2]
                    sqb = sq.t[:, :].bitcast(BF16)
                    self.act(sqb[:, :N], src.t[:, k, :N], AF.Square, [src.k[k]], sq.all)
                    self.mm(PST.t[:, :N], ONESB, sqb[:, :N], k == 0, k == 7, sq.all + ONB.all, PST.all)
                    yield
                self.act(RSTD.t[:, :N], PST.t[:, :N], AF.Ln, PST.all, RSTD.all, scale=1.0 / D, bias=1e-6)
                self.act(RSTD.t[:, :N], RSTD.t[:, :N], AF.Exp, RSTD.all, RSTD.all, scale=-0.5)
                yield

            def P1(b):
                c0, N, sample = blocks[b]
                xb, xn = XB2[b % 2], XN2[b % 2]
                self.dma('sp', xb.t[:, :, :N], x1v[:, :, c0:c0 + N], [X1S], xb.all, x2sem[b % 2])
                yield
                yield from rms_g(xb, N)
                if b == 0:
                    self.memset('pool', xn.t[:, :, 0:2], 0.0, xn.all)
                elif not sample:
                    self.cp('pool', xn.t[:, :, 0:2], XN2[(b - 1) % 2].t[:, :, NBB:NBB + 2], XN2[(b - 1) % 2].all, xn.all)
                for k in range(8):
                    self.stt(xn.t[:, k, 2:N + 2], xb.t[:, k, :N], V('g_pre2', k), RSTD.t[:, :N], ALU.mult, ALU.mult,
                             [xb.k[k]] + VEC.all + RSTD.all, [xn.k[k]])
                    yield
                bflags[('p1', b)] = True

            pmb_busy = [False] * 6

            def acq_b():
                while True:
                    for i_ in range(6):
                        if not pmb_busy[i_]:
                            pmb_busy[i_] = True
                            return i_
                    yield

            def up_chunk_g(b, oc, pm, out_u):
                c0, N, sample = blocks[b]
                pbi = yield from acq_b()
                pm = PMB[pbi]
                xn = XN2[b % 2]
                last = (b == nb_ - 2)
                NN = N if sample else N + 2
                lo = 2 if sample else 0
                for k in range(8):
                    self.mm(pm.t[:, :NN], WUP.t[:, k, oc * 128:(oc + 1) * 128], xn.t[:, k, lo:lo + NN], k == 0, k == 7,
                            [WUPT[((oc % 22) // 6, oc // 22)], xn.k[k]], pm.all)
                yield
                if not sample:
                    taps = [pm.t[:, i:i + N] for i in range(3)]
                    tr_ = pm.all
                    if last:
                        self.cp('act', FCO.t[:, oc, :, 0], pm.t[:, N:N + 2], pm.all, FCO.all)
                else:
                    self.cp('act', FCO.t[:, oc, 1, 1:1 + NS], pm.t[:, :N], pm.all, FCO.all)
                    self.cp('pool', FCO.t[:, oc, 0, 1:1 + NS], SFC.t[:, oc, 1, :], SFC.all, FCO.all)
                    taps = [SFC.t[:, oc, 0, :], SFC.t[:, oc, 1, :], FCO.t[:, oc, 1, 1:1 + NS]]
                    tr_ = SFC.all + FCO.all
                self.act(out_u.t[:, :N], taps[2], AF.Identity, tr_ + VEC.all, out_u.all, scale=V('fcw', 88 + oc), bias=V('fcb', oc))
                yield
                for i in (1, 0):
                    self.stt(out_u.t[:, :N], taps[i], V('fcw', 44 * i + oc), out_u.t[:, :N], ALU.mult, ALU.add,
                             tr_ + VEC.all + out_u.all, out_u.all)
                    if i == 0:
                        pmb_busy[pbi] = False
                    yield

            hms_v = XN2[0].t[:, 0:2, 32:208].rearrange("p a (b c) -> p a b c", b=11)
            HMSK = [Tk("hms%d" % i_) for i_ in range(22)]
            bflags = {}
            uu_seq = [0]

            def hm_of(b, i):
                if blocks[b][2]:
                    return hms_v[:, i // 11, i % 11, :], HMSK[i]
                return HM.t[:, i, :blocks[b][1]], HM.k[i]

            def pair_g(b, i):
                c0, N, sample = blocks[b]
                if sample:
                    while ('p1', b) not in bflags:
                        yield
                q_ = uu_seq[0]
                uu_seq[0] += 1
                ug, uv = UU[(2 * q_) % 10], UU[(2 * q_ + 1) % 10]
                yield from up_chunk_g(b, i, None, ug)
                self.act(ug.t[:, :N], ug.t[:, :N], AF.Gelu_apprx_tanh, ug.all, ug.all)
                yield
                yield from up_chunk_g(b, 22 + i, None, uv)
                hm_ap, hm_k = hm_of(b, i)
                self.tt('pool', hm_ap, ug.t[:, :N], uv.t[:, :N], ALU.mult, ug.all + uv.all, [hm_k])
                yield

            def down_g(b):
                c0, N, sample = blocks[b]
                for oc in range(8):
                    pm = PDN
                    for i in range(22):
                        hm_ap, hm_k = hm_of(b, i)
                        self.mm(pm.t[:, :N], WDN.t[:, i, oc * 128:(oc + 1) * 128], hm_ap, i == 0, i == 21,
                                [WDN.k[i], hm_k], pm.all)
                    self.cp('act', FO.t[:, oc, :N], pm.t[:, :N], pm.all, [FO.k[oc]])
                    yield

            def tail_g(b):
                c0, N, sample = blocks[b]
                xb = XB2[b % 2]
                yield from rms_g(FO, N)
                for k in range(8):
                    self.stt(FO.t[:, k, :N], FO.t[:, k, :N], V('g_post2', k), RSTD.t[:, :N], ALU.mult, ALU.mult,
                             [FO.k[k]] + VEC.all + RSTD.all, [FO.k[k]])
                    self.tt('pool', FO.t[:, k, :N], FO.t[:, k, :N], xb.t[:, k, :N], ALU.add, [FO.k[k], xb.k[k]], [FO.k[k]])
                    yield
                self.dma('sp', yTv[:, :, c0:c0 + N], FO.t[:, :, :N], FO.all, [], ysem)
                yield

            def chain(*gs):
                for g in gs:
                    yield from g

            def window(gens, width):
                gens = list(gens)
                act_ = []
                idx = 0
                while idx < len(gens) or act_:
                    if len(act_) < width and idx < len(gens):
                        act_.append(gens[idx])
                        idx += 1
                    for g in list(act_):
                        try:
                            next(g)
                        except StopIteration:
                            act_.remove(g)
                        yield

            def rr(streams, weights):
                streams = list(streams)
                alive = [True] * len(streams)
                while any(alive):
                    for si, g in enumerate(streams):
                        if not alive[si]:
                            continue
                        for _ in range(weights[si]):
                            try:
                                next(g)
                            except StopIteration:
                                alive[si] = False
                                break

            rr([P1(0)], [1])
            for b in range(nb_ - 1):
                plist = [pair_g(b, i) for i in range(22)]
                if b == nb_ - 2:
                    plist += [pair_g(nb_ - 1, i) for i in range(22)]
                main = [window(plist, 5)]
                if b > 0:
                    main = [down_g(b - 1)] + main
                side = []
                if b > 0:
                    side.append(tail_g(b - 1))
                if b + 1 < nb_:
                    side.append(P1(b + 1))
                rr([chain(*main), chain(*side)], [12, 1])
            rr([chain(down_g(nb_ - 2), tail_g(nb_ - 2), down_g(nb_ - 1), tail_g(nb_ - 1))], [1])

            self.dma('sp', o_fconv.rearrange("p (a b c) -> p a b c", a=44, b=2), FCO.t, FCO.all, [], osem[3])
            S.finish([])
            cnt = S.emit()
            print("instr counts", cnt)
        return nc


def _colpack(v):
    v = np.asarray(v, np.float32).reshape(-1)
    n = v.shape[0] // 128
    return np.ascontiguousarray(v.reshape(n, 128).T)


def _consts():
    c = np.zeros((128, NCONST), np.float32)
    p = np.arange(128)[:, None]
    q = np.arange(128)[None, :]
    c[:, C_ONES:C_ONES + 128] = 1.0
    c[:, C_BONES:C_BONES + 128] = (p // 64 == q // 64)
    c[:, C_IDENT:C_IDENT + 128] = (p == q)
    s = p % 64
    t = q % 64
    m1 = np.where(q < 64, s < t, s <= t)
    c[:, C_MASK1:C_MASK1 + 128] = m1
    q64 = np.arange(64)[None, :]
    c[:, C_MASK2:C_MASK2 + 64] = (s > q64)
    c[:, C_I64:C_I64 + 64] = (s == q64)
    return c


_NC_CACHE = {}


def _get_nc(debug=None):
    key = tuple(debug or [])
    if key not in _NC_CACHE:
        _NC_CACHE[key] = Builder(debug).build()
    return _NC_CACHE[key]


def _prep_inputs(inp):
    f = lambda a: np.ascontiguousarray(np.asarray(a, np.float32))
    g = lambda n: f(inp[n])[0]
    vec = np.zeros((128, NVEC), np.float32)

    def put(name, arr):
        a = _colpack(arr)
        vec[:, VOFF[name]:VOFF[name] + a.shape[1]] = a
    put('g_pre', g('norm_pre_mix'))
    put('g_post', g('norm_post_mix'))
    put('g_pre2', g('norm_pre_ffn'))
    put('g_post2', g('norm_post_ffn'))
    put('mu', g('rwkv_mu'))
    put('w0', g('rwkv_w0'))
    put('a0', g('rwkv_a0'))
    put('k_k', g('rwkv_k_k'))
    put('k_a', g('rwkv_k_a'))
    put('r_k', g('rwkv_r_k'))
    put('lnx_w', g('rwkv_lnx_w'))
    put('lnx_b', g('rwkv_lnx_b'))
    put('lcw', g('lru_conv_w'))
    put('lcb', g('lru_conv_b'))
    put('gxb', g('lru_gx_b'))
    put('gab', g('lru_ga_b'))
    put('lam', g('lru_lambda'))
    put('fcw', g('ffn_conv_w'))
    put('fcb', g('ffn_conv_b'))
    shared = {
        "w_in": g('w_in'), "vecs": vec, "consts": _consts(),
        "lora": np.ascontiguousarray(np.concatenate([g('rwkv_w_up'), g('rwkv_a_up')], axis=0)),
        "gup": g('rwkv_g_up'),
        "wor": g('rwkv_w_out'), "wol": g('lru_w_out'), "wo": g('w_o'),
        "wup": g('ffn_up'), "wdn": g('ffn_down'),
    }
    gx, ga = g('lru_gx_w'), g('lru_ga_w')
    gxa = np.stack([gx, ga], 0)
    gxa = gxa.reshape(2, 4, 2, 64, 64).transpose(2, 3, 0, 1, 4)
    shared["gxa"] = np.ascontiguousarray(gxa.reshape(128, 2 * 4 * 64))
    xp = f(inp['x_prompt'])
    xs = f(inp['x_sample'])[:, 0, :]
    sh = g('state_rwkv_shift')[:, 0, :]
    wkv = g('state_rwkv_wkv')
    lc = g('state_lru_conv')
    lh = g('state_lru_h')
    fc = g('state_ffn_conv')
    maps = []
    for c in range(NCORES):
        n0 = c * NS
        m = dict(shared)
        m["xT"] = np.ascontiguousarray(np.concatenate([xp[c].T, xs[n0:n0 + NS].T], axis=1))
        m["s_shift"] = np.ascontiguousarray(sh[n0:n0 + NS].reshape(NS, 14, 128).transpose(2, 1, 0).reshape(128, -1))
        m["s_lconv"] = np.ascontiguousarray(lc[n0:n0 + NS].reshape(NS, 3, 4, 128).transpose(3, 2, 1, 0).reshape(128, -1))
        m["s_lh"] = np.ascontiguousarray(lh[n0:n0 + NS].reshape(NS, 4, 128).transpose(2, 1, 0).reshape(128, -1))
        m["s_fconv"] = np.ascontiguousarray(fc[n0:n0 + NS].reshape(NS, 2, 44, 128).transpose(3, 2, 1, 0).reshape(128, -1))
        w = wkv[n0:n0 + NS].reshape(NS, 4, 2, 64, 64)
        m["s_wkv"] = np.ascontiguousarray(w.transpose(2, 4, 1, 0, 3).reshape(128, -1))
        maps.append(m)
    return maps


def _assemble(results):
    yp = np.zeros((NCORES, T, D), np.float32)
    ys = np.zeros((NCORES * NS, 1, D), np.float32)
    p_shift = np.zeros((1, NCORES, 1, RP), np.float32)
    p_wkv = np.zeros((1, NCORES, 8, 64, 64), np.float32)
    p_lconv = np.zeros((1, NCORES, 3, RW), np.float32)
    p_lh = np.zeros((1, NCORES, RW), np.float32)
    p_fconv = np.zeros((1, NCORES, 2, 2 * DFF), np.float32)
    s_shift = np.zeros((1, NCORES * NS, 1, RP), np.float32)
    s_wkv = np.zeros((1, NCORES * NS, 8, 64, 64), np.float32)
    s_lconv = np.zeros((1, NCORES * NS, 3, RW), np.float32)
    s_lh = np.zeros((1, NCORES * NS, RW), np.float32)
    s_fconv = np.zeros((1, NCORES * NS, 2, 2 * DFF), np.float32)
    for c, r in enumerate(results):
        n0 = c * NS
        yT = r["yT"]
        yp[c] = yT[:, :T].T
        ys[n0:n0 + NS, 0] = yT[:, T:].T
        a = r["o_shift"].reshape(128, 14, 17).transpose(2, 1, 0).reshape(17, RP)
        p_shift[0, c, 0] = a[0]
        s_shift[0, n0:n0 + NS, 0] = a[1:]
        a = r["o_lconv"].reshape(128, 4, 3, 17).transpose(3, 2, 1, 0).reshape(17, 3, RW)
        p_lconv[0, c] = a[0]
        s_lconv[0, n0:n0 + NS] = a[1:]
        a = r["o_lh"].reshape(128, 4, 17).transpose(2, 1, 0).reshape(17, RW)
        p_lh[0, c] = a[0]
        s_lh[0, n0:n0 + NS] = a[1:]
        a = r["o_fconv"].reshape(128, 44, 2, 17).transpose(3, 2, 1, 0).reshape(17, 2, 2 * DFF)
        p_fconv[0, c] = a[0]
        s_fconv[0, n0:n0 + NS] = a[1:]
        a = r["o_wkvp"].reshape(2, 64, 4, 64)
        p_wkv[0, c] = a.transpose(2, 0, 3, 1).reshape(8, 64, 64)
        a = r["o_wkvs"].reshape(2, 64, 4, NS, 64)
        s_wkv[0, n0:n0 + NS] = a.transpose(3, 2, 0, 4, 1).reshape(NS, 8, 64, 64)
    return (yp, ys, p_shift, p_wkv, p_lconv, p_lh, p_fconv, s_shift, s_wkv, s_lconv, s_lh, s_fconv)


def kernel(**inputs):
    nc = _get_nc()
    maps = _prep_inputs(inputs)
    res = run_bass_kernel_spmd(nc, maps, core_ids=list(range(NCORES)))
    return _assemble(res.results)
```

```python
import contextlib
import math
import numpy as np
import concourse.bass as bass
import concourse.mybir as mybir
from concourse.bass_utils import run_bass_kernel_spmd

F32 = mybir.dt.float32
BF16 = mybir.dt.bfloat16
AF = mybir.ActivationFunctionType
ALU = mybir.AluOpType
AX = mybir.AxisListType

D = 1024
T = 2048
NS = 16
TT = T + NS
RW = 512
RP = 1792
INC = 4352
DFF = 2816
NCORES = 8
CH = 64
NBA = 128
NBB = 256
CDEC = math.exp(-0.5)

VSPEC = [('g_pre', 8), ('g_post', 8), ('g_pre2', 8), ('g_post2', 8), ('mu', 14), ('w0', 4), ('a0', 4),
         ('k_k', 4), ('k_a', 4), ('r_k', 4), ('lnx_w', 4), ('lnx_b', 4), ('lcw', 16), ('lcb', 4),
         ('gxb', 4), ('gab', 4), ('lam', 4), ('fcw', 132), ('fcb', 44)]
VOFF = {}
_o = 0
for _n, _c in VSPEC:
    VOFF[_n] = _o
    _o += _c
NVEC = _o
C_ONES = 0
C_BONES = 128
C_IDENT = 256
C_MASK1 = 384
C_MASK2 = 512
C_I64 = 576
NCONST = 640


class Tk:
    __slots__ = ("name", "excl", "last_w", "readers")

    def __init__(self, name, excl=False):
        self.name = name
        self.excl = excl
        self.last_w = None
        self.readers = []


class Op:
    __slots__ = ("eng", "fn", "waits", "idx", "signal", "dma_sem")

    def __init__(self, eng, fn):
        self.eng = eng
        self.fn = fn
        self.waits = []
        self.signal = False
        self.dma_sem = None


class Sched:
    ENGS = ("pe", "act", "dve", "pool", "sp")

    def __init__(self, nc):
        self.nc = nc
        self.ops = {e: [] for e in self.ENGS}
        self.waited = {e: {} for e in self.ENGS}
        self.dma_sems = []
        self.final_tokens = []
        self.pending = {e: [] for e in self.ENGS}

    def new_dma_sem(self):
        self.dma_sems.append([None, 0, None])
        return len(self.dma_sems) - 1

    def _add_wait(self, op, tok):
        if tok is None:
            return
        eng = op.eng
        if tok[0] == 'e':
            if tok[1] == 'pe' and eng == 'pe':
                return
            key = ('e', tok[1])
        else:
            key = ('d', tok[1])
        val = tok[2]
        if self.waited[eng].get(key, -1) >= val:
            return
        self.waited[eng][key] = val
        op.waits.append(tok)
        if tok[0] == 'e':
            self.ops[tok[1]][tok[2]].signal = True

    def op(self, eng, fn, reads=(), writes=(), dma_sem=None, extra=()):
        o = Op(eng, fn)
        o.idx = len(self.ops[eng])
        toks = list(extra)
        if self.pending[eng]:
            toks.extend(self.pending[eng])
            self.pending[eng] = []
        for r in reads:
            toks.append(r.last_w)
            if r.excl:
                toks.extend(r.readers)
        for w in writes:
            toks.append(w.last_w)
            toks.extend(w.readers)
        if dma_sem is not None:
            toks.append(self.dma_sems[dma_sem][2])
        for t in toks:
            self._add_wait(o, t)
        self.ops[eng].append(o)
        if dma_sem is not None:
            s = self.dma_sems[dma_sem]
            s[1] += 16
            o.dma_sem = dma_sem
            tok = ('d', dma_sem, s[1])
            s[2] = tok
        else:
            tok = ('e', eng, o.idx)
        for r in reads:
            if r.excl:
                r.last_w = tok
                r.readers = []
            else:
                r.readers.append(tok)
        for w in writes:
            w.last_w = tok
            w.readers = []
        co_switch()
        return tok

    def barrier(self):
        toks = []
        for e in self.ENGS:
            for o in reversed(self.ops[e]):
                if o.dma_sem is None:
                    toks.append(('e', e, o.idx))
                    break
        for i, s in enumerate(self.dma_sems):
            if s[2] is not None:
                toks.append(s[2])
        for e in self.ENGS:
            self.pending[e] = list(toks)

    def finish(self, toks):
        self.final_tokens = list(toks)

    def emit(self):
        nc = self.nc
        fin = Op('sp', None)
        fin.idx = len(self.ops['sp'])
        for s in self.dma_sems:
            self._add_wait(fin, s[2])
        for t in self.final_tokens:
            self._add_wait(fin, t)
        with contextlib.ExitStack() as st:
            esem = {}
            for e in self.ENGS:
                esem[e] = st.enter_context(nc.semaphore("s_" + e))
            for i, s in enumerate(self.dma_sems):
                s[0] = st.enter_context(nc.semaphore("d%d" % i))
            sigval = {}
            for e in self.ENGS:
                c = 0
                for o in self.ops[e]:
                    if o.signal and o.dma_sem is None:
                        c += 1
                        sigval[(e, o.idx)] = c
            block = st.enter_context(nc.Block())

            def mk(ename, extra=None):
                def body(eng):
                    def do_waits(o):
                        for t in o.waits:
                            if t[0] == 'e':
                                eng.wait_ge(esem[t[1]], sigval[(t[1], t[2])])
                            else:
                                eng.wait_ge(self.dma_sems[t[1]][0], t[2])
                    for o in self.ops[ename]:
                        do_waits(o)
                        inst = o.fn(eng)
                        if o.dma_sem is not None:
                            inst.then_inc(self.dma_sems[o.dma_sem][0], 16)
                        elif o.signal:
                            inst.then_inc(esem[ename], 1)
                    if extra is not None:
                        do_waits(extra)
                return body
            block.tensor(mk('pe'))
            block.scalar(mk('act'))
            block.vector(mk('dve'))
            block.gpsimd(mk('pool'))
            block.sync(mk('sp', fin))
        return {e: len(self.ops[e]) for e in self.ENGS}


class Buf:
    def __init__(self, name, ap, nk=1, excl=False):
        self.t = ap
        self.k = [Tk("%s%d" % (name, i), excl) for i in range(nk)]

    @property
    def all(self):
        return list(self.k)


class Arena:
    def __init__(self, tensor, nbytes):
        self.tensor = tensor
        self.nbytes = nbytes
        self.off = 0
        self.top = nbytes

    def alloc(self, name, free_shape, dtype, nk=1, from_top=False):
        sz = 4 if dtype == F32 else 2
        n = int(np.prod(free_shape))
        nb = (n * sz + 63) // 64 * 64
        if from_top:
            self.top -= nb
            o = self.top
        else:
            o = self.off
            self.off += nb
        assert self.off <= self.top, ("arena overflow", name, self.off, self.top)
        ap = self.tensor[:, o // 2:(o + n * sz) // 2]
        if dtype == F32:
            ap = ap.bitcast(F32)
        if len(free_shape) == 2:
            ap = ap.rearrange("p (a b) -> p a b", a=free_shape[0])
        elif len(free_shape) == 3:
            ap = ap.rearrange("p (a b c) -> p a b c", a=free_shape[0], b=free_shape[1])
        elif len(free_shape) == 4:
            ap = ap.rearrange("p (a b c d) -> p a b c d", a=free_shape[0], b=free_shape[1], c=free_shape[2])
        return Buf(name, ap, nk)


import threading
_TL = threading.local()


def co_switch():
    sw = getattr(_TL, 'sw', None)
    if sw is not None:
        sw()


def co_run(funcs):
    n = len(funcs)
    st = {'turn': 0, 'alive': [True] * n, 'err': None}
    cv = threading.Condition()

    def nxt(i):
        for d in range(1, n + 1):
            j = (i + d) % n
            if st['alive'][j]:
                st['turn'] = j
                return
        st['turn'] = -1

    def switch(i):
        with cv:
            nxt(i)
            cv.notify_all()
            while st['turn'] != i:
                cv.wait()

    def worker(i):
        with cv:
            while st['turn'] != i:
                cv.wait()
        _TL.sw = lambda: switch(i)
        try:
            funcs[i]()
        except BaseException as e:
            st['err'] = e
        finally:
            _TL.sw = None
            with cv:
                st['alive'][i] = False
                nxt(i)
                cv.notify_all()
    ths = [threading.Thread(target=worker, args=(i,)) for i in range(n)]
    for t in ths:
        t.start()
    for t in ths:
        t.join()
    if st['err'] is not None:
        raise st['err']


class Builder:
    def __init__(self, debug=None):
        self.debug = debug or []
        self.nc = bass.Bass("TRN2", target_bir_lowering=False)
        self.S = Sched(self.nc)
        self.dbg_outs = {}

    def act(self, out, in_, func, reads, writes, scale=None, bias=None, eng='act'):
        kw = {}
        if scale is not None:
            kw['scale'] = scale
        if bias is not None:
            kw['bias'] = bias
        return self.S.op('act', lambda e: e.activation(out=out, in_=in_, func=func, **kw), reads, writes)

    def tt(self, eng, out, in0, in1, op, reads, writes):
        return self.S.op(eng, lambda e: e.tensor_tensor(out=out, in0=in0, in1=in1, op=op), reads, writes)

    def ts(self, eng, out, in0, s1, op0, reads, writes, s2=None, op1=None):
        if op1 is None:
            return self.S.op(eng, lambda e: e.tensor_scalar(out=out, in0=in0, scalar1=s1, scalar2=None, op0=op0), reads, writes)
        return self.S.op(eng, lambda e: e.tensor_scalar(out=out, in0=in0, scalar1=s1, scalar2=s2, op0=op0, op1=op1), reads, writes)

    def stt(self, out, in0, scalar, in1, op0, op1, reads, writes):
        return self.S.op('dve', lambda e: e.scalar_tensor_tensor(out=out, in0=in0, scalar=scalar, in1=in1, op0=op0, op1=op1), reads, writes)

    def cp(self, eng, out, in_, reads, writes):
        if eng == 'act':
            return self.S.op('act', lambda e: e.copy(out=out, in_=in_), reads, writes)
        return self.S.op(eng, lambda e: e.tensor_copy(out=out, in_=in_), reads, writes)

    def mm(self, out, lhsT, rhs, start, stop, reads, writes):
        return self.S.op('pe', lambda e: e.matmul(out, lhsT=lhsT, rhs=rhs, start=start, stop=stop), reads, writes)

    def tr(self, out, in_, ident, reads, writes):
        return self.S.op('pe', lambda e: e.transpose(out, in_, ident), reads, writes)

    def dma(self, eng, out, in_, reads, writes, sem, extra=()):
        return self.S.op(eng, lambda e: e.dma_start(out=out, in_=in_), reads, writes, dma_sem=sem, extra=extra)

    def recip(self, out, in_, reads, writes):
        return self.S.op('dve', lambda e: e.reciprocal(out=out, in_=in_), reads, writes)

    def memset(self, eng, ap, val, writes):
        return self.S.op(eng, lambda e: e.memset(ap, val), (), writes)

    def scan(self, out, d0, d1, init, reads, writes, op0=None):
        op0 = ALU.mult if op0 is None else op0
        return self.S.op('dve', lambda e: e.tensor_tensor_scan(out=out, data0=d0, data1=d1, initial=init,
                                                              op0=op0, op1=ALU.add), reads, writes)

    def reduce(self, out, in_, reads, writes):
        return self.S.op('dve', lambda e: e.tensor_reduce(out=out, in_=in_, axis=AX.X, op=ALU.add), reads, writes)

    def build(self):
        nc = self.nc
        S = self.S
        dr = lambda name, shape, kind: nc.dram_tensor(name, list(shape), F32, kind=kind).ap()
        I = "ExternalInput"
        O = "ExternalOutput"
        xT = dr("xT", [D, TT], I)
        w_in = dr("w_in", [D, INC], I)
        vecs = dr("vecs", [128, NVEC], I)
        consts = dr("consts", [128, NCONST], I)
        lora = dr("lora", [128, RW], I)
        gup = dr("gup", [128, RW], I)
        gxa = dr("gxa", [128, 2 * 4 * 64], I)
        wor = dr("wor", [RW, D], I)
        wol = dr("wol", [RW, D], I)
        wo = dr("wo", [D, D], I)
        wup = dr("wup", [D, 2 * DFF], I)
        wdn = dr("wdn", [DFF, D], I)
        s_shift = dr("s_shift", [128, 14 * NS], I)
        s_lconv = dr("s_lconv", [128, 4 * 3 * NS], I)
        s_lh = dr("s_lh", [128, 4 * NS], I)
        s_fconv = dr("s_fconv", [128, 44 * 2 * NS], I)
        s_wkv = dr("s_wkv", [128, 4 * NS * 64], I)
        yT = dr("yT", [D, TT], O)
        o_shift = dr("o_shift", [128, 14 * 17], O)
        o_lconv = dr("o_lconv", [128, 4 * 3 * 17], O)
        o_lh = dr("o_lh", [128, 4 * 17], O)
        o_fconv = dr("o_fconv", [128, 44 * 2 * 17], O)
        o_wkvp = dr("o_wkvp", [128, 4 * 64], O)
        o_wkvs = dr("o_wkvs", [128, 4 * NS * 64], O)
        x1s = nc.dram_tensor("x1s", [D, TT], F32, kind="Internal").ap()
        X1S = Tk("x1s")
        for name, shape in self.debug:
            self.dbg_outs[name] = dr("dbg_" + name, shape, O)

        with contextlib.ExitStack() as st:
            ARB = 212736
            art = st.enter_context(nc.sbuf_tensor("arena", [128, ARB // 2], BF16))
            ps = [st.enter_context(nc.psum_tensor("ps%d" % i, [128, 512], F32)) for i in range(7)]
            pst = st.enter_context(nc.psum_tensor("pst", [128, 1024], BF16))
            PM = [Buf("pm0", ps[0][:, :], 1, True), Buf("pm1", ps[1][:, :], 1, True)]
            PST = Buf("pstat", ps[2][:, :], 1, True)
            PA = Buf("pa", ps[3][:, :], 1, True)
            PB_ = Buf("pb", ps[4][:, :], 1, True)
            PX = Buf("px", ps[5][:, :], 1, True)
            PW = Buf("pw", ps[6][:, :], 1, True)
            PT = Buf("pt", pst[:, :], 1, True)
            self.pmi = 0

            A = Arena(art, ARB)
            VEC = A.alloc("vec", [NVEC], F32, from_top=True)
            CON = A.alloc("con", [NCONST], F32, from_top=True)
            DER = A.alloc("der", [8], F32, from_top=True)
            DR2 = A.alloc("dr2", [28], F32, from_top=True)
            IDB = A.alloc("idb", [128], BF16, from_top=True)
            ONB = A.alloc("onb", [256], BF16, from_top=True)
            M1B = A.alloc("m1b", [128], F32, from_top=True)
            SHO = A.alloc("sho", [14, 17], F32, from_top=True)
            LCO = A.alloc("lco", [4, 3, 17], F32, from_top=True)
            LHO = A.alloc("lho", [4, 17], F32, from_top=True)
            WKP = A.alloc("wkp", [4, 64], F32, from_top=True)
            SQ = [A.alloc("sq%d" % i, [max(NBA, NBB)], F32, from_top=True) for i in range(2)]
            RSTD = A.alloc("rstd", [max(NBA, NBB)], F32, from_top=True)

            def V(name, j=0):
                o = VOFF[name] + j
                return VEC.t[:, o:o + 1]
            ONESF = CON.t[:, C_ONES:C_ONES + 128]
            BONES = CON.t[:, C_BONES:C_BONES + 128]
            MASK1 = CON.t[:, C_MASK1:C_MASK1 + 128]
            MASK2 = CON.t[:, C_MASK2:C_MASK2 + 64]
            I64 = CON.t[:, C_I64:C_I64 + 64]
            ONES64 = CON.t[:, C_ONES:C_ONES + 64]

            sem_c = S.new_dma_sem()
            self.dma('sp', VEC.t, vecs[:, :], [], VEC.all, sem_c)
            sem_c2 = S.new_dma_sem()
            self.dma('sp', CON.t, consts[:, :], [], CON.all, sem_c2)
            self.cp('dve', IDB.t, CON.t[:, C_IDENT:C_IDENT + 128], CON.all, IDB.all)
            self.cp('dve', ONB.t, CON.t[:, C_ONES:C_ONES + 256], CON.all, ONB.all)
            ONESB = ONB.t[:, 0:128]
            BONESB = ONB.t[:, 128:256]
            self.act(DER.t[:, 0:4], VEC.t[:, VOFF['lam']:VOFF['lam'] + 4], AF.Exp, VEC.all, DER.all, scale=-1.0)
            self.act(DER.t[:, 0:4], DER.t[:, 0:4], AF.Ln, DER.all, DER.all, bias=1.0)
            self.ts('dve', DER.t[:, 4:8], DER.t[:, 0:4], -16.0, ALU.mult, DER.all, DER.all)
            self.ts('dve', DER.t[:, 0:4], DER.t[:, 0:4], -8.0, ALU.mult, DER.all, DER.all)
            for b_ in (SHO, LCO, LHO):
                self.memset('pool', b_.t, 0.0, b_.all)
            for i_, nm in enumerate(('w0', 'a0', 'k_a', 'gxb', 'gab')):
                self.ts('dve', DR2.t[:, 4 * i_:4 * i_ + 4], VEC.t[:, VOFF[nm]:VOFF[nm] + 4], 0.5, ALU.mult, VEC.all + DR2.all, DR2.all)
            self.ts('dve', DR2.t[:, 20:24], DER.t[:, 0:4], 0.5, ALU.mult, DER.all + DR2.all, DR2.all)
            self.cp('dve', DR2.t[:, 24:28], DER.t[:, 0:4], DER.all + DR2.all, DR2.all)
            H2 = lambda i_, j_: DR2.t[:, 4 * i_ + j_:4 * i_ + j_ + 1]
            OMM = A.alloc("omm", [14], F32, from_top=True)
            self.ts('dve', OMM.t, VEC.t[:, VOFF['mu']:VOFF['mu'] + 14], -1.0, ALU.mult, VEC.all, OMM.all, s2=1.0, op1=ALU.add)

            mark = (A.off, A.top)
            W_IN = A.alloc("w_in", [8, INC], BF16, nk=8)
            WOR = A.alloc("wor", [4, D], BF16)
            WOL = A.alloc("wol", [4, D], BF16)
            WO = A.alloc("wo", [8, D], BF16)
            LORA = A.alloc("lora", [RW], BF16)
            GUP = A.alloc("gup", [RW], BF16)
            GXA = A.alloc("gxa", [2, 4, 64], BF16)
            wsems = [S.new_dma_sem() for _ in range(8)]
            wiv = w_in.rearrange("(k p) n -> p k n", p=128)
            WIG = [(0, 1792), (1792, 2304), (2304, 3328), (3328, 4352)]
            WINT = [Tk("wint%d" % g_) for g_ in range(4)]
            for g_, (c0_, c1_) in enumerate(WIG):
                self.dma('pool', W_IN.t[:, :, c0_:c1_], wiv[:, :, c0_:c1_], [], [WINT[g_]], wsems[g_])

            def WK(oc):
                c_ = oc * 128
                return WINT[0 if c_ < 1792 else 1 if c_ < 2304 else 2 if c_ < 3328 else 3]
            ws2 = [S.new_dma_sem() for _ in range(6)]
            self.dma('pool', LORA.t, lora[:, :], [], LORA.all, ws2[0])
            self.dma('pool', GUP.t, gup[:, :], [], GUP.all, ws2[1])
            self.dma('pool', GXA.t, gxa.rearrange("p (a b c) -> p a b c", a=2, b=4), [], GXA.all, ws2[2])
            self.dma('pool', WOR.t, wor.rearrange("(k p) n -> p k n", p=128), [], WOR.all, ws2[3])
            self.dma('pool', WOL.t, wol.rearrange("(k p) n -> p k n", p=128), [], WOL.all, ws2[4])
            self.dma('pool', WO.t, wo.rearrange("(k p) n -> p k n", p=128), [], WO.all, ws2[5])


            markA = A.off
            xTv = xT.rearrange("(k p) n -> p k n", p=128)
            x1v = x1s.rearrange("(k p) n -> p k n", p=128)
            yTv = yT.rearrange("(k p) n -> p k n", p=128)
            N = NBA
            NCK = N // CH
            pXB = [A.alloc("pxb%d" % i, [8, N], F32, nk=8) for i in range(3)]
            pXN = [A.alloc("pxn%d" % i, [8, N + 3], BF16, nk=8) for i in range(3)]
            _sqv = [SQ[i].t[:, :].bitcast(BF16) for i in range(2)]
            SQA = [Buf("sqa%d" % i, _sqv[i // 2][:, (i % 2) * 128:(i % 2) * 128 + 128]) for i in range(4)]
            RSH = [Buf("rsh%d" % i, RSTD.t[:, i * 128:(i + 1) * 128]) for i in range(2)]
            pflags = {}

            def wait_pflags(keys):
                while not all(k_ in pflags for k_ in keys):
                    yield
            pPBF = [A.alloc("ppbf%d" % i, [N], F32) for i in range(3)]
            pR, pK, pV, pSG, pCSG, pASIG, pKKN, pECW, pENCW = [A.alloc("pq%d" % i, [4, N], F32, nk=4) for i in range(9)]
            pLIN = A.alloc("plin", [N], BF16)
            pGS = A.alloc("pgs", [N], BF16)
            pT = [[A.alloc("pt%d_%d" % (j, i), [N], F32) for i in range(2)] for j in range(4)]
            pL = [[A.alloc("pl%d_%d" % (j, i), [N], F32) for i in range(5)] for j in range(4)]
            pXCB = [A.alloc("pxcb%d" % j, [N], BF16) for j in range(4)]
            dbl = lambda nm, sh, dt, nk=1: [A.alloc("%s%d" % (nm, i), sh, dt, nk=nk) for i in range(2)]
            pAR = dbl("par", [4, 2, N], BF16, 4)
            pBT = dbl("pbt", [4, N], BF16, 4)
            pKT = dbl("pkt", [4, N], BF16, 4)
            pVB = dbl("pvb", [4, N], BF16, 4)
            pPC = dbl("ppc", [4, NCK], F32, 4)
            pG = dbl("pg", [4, N], F32, 4)
            pBON = dbl("pbon", [4, N], F32, 4)
            pHSB = dbl("phsb", [4, N], BF16, 4)
            pYFM = dbl("pyfm", [4, N], F32, 4)
            TOK = A.alloc("ptok", [3, 4, CH], BF16)
            MBS = A.alloc("pmbs", [4, 128], BF16)
            MKS = A.alloc("pmks", [4, 128], BF16)
            XX = [A.alloc("pxx%d" % i, [2, 4, CH], BF16) for i in range(2)]
            XT0 = A.alloc("pxt0", [4, CH], BF16)
            WF = A.alloc("pwf", [4, CH], F32)
            WB = [A.alloc("pwb%d" % i, [4, CH], BF16) for i in range(2)]
            HF = A.alloc("phf", [4, CH], F32)
            HB = A.alloc("phb", [4, CH], BF16)
            HTMP = A.alloc("phtmp", [4, CH], F32)
            HCAR = A.alloc("phcar", [4], F32)
            pZB = A.alloc("pzb", [4, N], BF16, nk=4)
            pMIX = A.alloc("pmix", [8, N], BF16, nk=8)
            pGT = [A.alloc("pgt%d" % i, [N], F32) for i in range(4)]
            print("phase A prompt arena used", A.off, "top", A.top)
            self.memset('pool', HF.t, 0.0, HF.all)
            self.memset('pool', HB.t, 0.0, HB.all)
            self.memset('pool', HCAR.t, 0.0, HCAR.all)
            pxsem = [S.new_dma_sem() for _ in range(3)]
            px1sem = [S.new_dma_sem() for _ in range(3)]
            PMA = [PM[0], PM[1], PST]
            pm_busy = [False, False, False]
            NBLK = T // N

            def acquire():
                while True:
                    for i_ in range(3):
                        if not pm_busy[i_]:
                            pm_busy[i_] = True
                            return i_
                    yield

            def acquire2():
                while True:
                    fr = [i_ for i_ in range(3) if not pm_busy[i_]]
                    if len(fr) >= 2:
                        pm_busy[fr[0]] = True
                        pm_busy[fr[1]] = True
                        return fr[0], fr[1]
                    yield

            def release(i_):
                pm_busy[i_] = False

            class SyncPt:
                def __init__(self, n):
                    self.n = n
                    self.c = 0

            def wait_sync(sp):
                sp.c += 1
                while sp.c < sp.n:
                    yield

            def chain(*gs):
                for g_ in gs:
                    yield from g_

            def window(gens, width):
                gens = list(gens)
                act_ = []
                idx = 0
                while idx < len(gens) or act_:
                    while len(act_) < width and idx < len(gens):
                        act_.append(gens[idx])
                        idx += 1
                    for g_ in list(act_):
                        try:
                            next(g_)
                        except StopIteration:
                            act_.remove(g_)
                        yield

            def rr(streams, weights):
                streams = list(streams)
                alive = [True] * len(streams)
                while any(alive):
                    for si, g_ in enumerate(streams):
                        if not alive[si]:
                            continue
                        for _ in range(weights[si]):
                            try:
                                next(g_)
                            except StopIteration:
                                alive[si] = False
                                break

            def rms_p(src_aps, src_ks, half):
                pi = yield from acquire()
                pstat = PMA[pi]
                rs = RSH[half]
                for k in range(8):
                    sq = SQA[2 * half + k % 2]
                    self.act(sq.t[:, :N], src_aps[k], AF.Square, [src_ks[k]], sq.all)
                    self.mm(pstat.t[:, :N], ONESB, sq.t[:, :N], k == 0, k == 7, sq.all + ONB.all, pstat.all)
                    yield
                self.act(rs.t[:, :N], pstat.t[:, :N], AF.Ln, pstat.all, rs.all, scale=1.0 / D, bias=1e-6)
                release(pi)
                self.act(rs.t[:, :N], rs.t[:, :N], AF.Exp, rs.all, rs.all, scale=-0.5)
                yield

            def rw_proj(b, oc):
                par = b % 2
                last = (b == NBLK - 1)
                xn = pXN[b % 3]
                pi = yield from acquire()
                pm = PMA[pi]
                for k in range(8):
                    self.mm(pm.t[:, :N + 1], W_IN.t[:, k, oc * 128:(oc + 1) * 128], xn.t[:, k, 2:N + 3], k == 0, k == 7,
                            [WK(oc), xn.k[k]], pm.all)
                yield
                pc_ = pPBF[(oc + 2) % 14 % 3]
                self.act(pc_.t[:, :N], pm.t[:, 1:N + 1], AF.Identity, pm.all + OMM.all, pc_.all, scale=OMM.t[:, oc:oc + 1])
                if last:
                    self.cp('act', SHO.t[:, oc, 0:1], pm.t[:, N:N + 1], pm.all, SHO.all)
                yield
                if oc < 12:
                    dst = (pR, pK, pV)[oc // 4]
                    dst_ap, dst_k = dst.t[:, oc % 4, :N], [dst.k[oc % 4]]
                    dt_ = pL[oc % 4][3 + oc // 4] if oc < 8 else pL[oc % 4][0]
                else:
                    dst_ = pL[oc - 12][1]
                    dst_ap, dst_k = dst_.t[:, :N], dst_.all
                    dt_ = pL[oc - 12][2]
                self.stt(dst_ap, pm.t[:, 0:N], V('mu', oc), pc_.t[:, :N], ALU.mult, ALU.add, pm.all + pc_.all + VEC.all, dst_k)
                release(pi)
                yield
                if oc == 12:
                    self.act(pLIN.t[0:64, :N], dst_ap[0:64, :], AF.Tanh, dst_k, pLIN.all)
                    self.cp('pool', pLIN.t[64:128, :N], dst_ap[64:128, :], dst_k, pLIN.all)
                    yield
                if oc == 13:
                    self.act(dst_ap, dst_ap, AF.Tanh, dst_k, dst_k, scale=0.5)
                    self.ts('pool', pGS.t[:, :N], dst_ap, 1.0, ALU.add, dst_k, pGS.all, s2=0.5, op1=ALU.mult)
                    yield

            def rwkv_chain(b, j, sp):
                par = b % 2
                cs = slice(j * 128, (j + 1) * 128)
                t0, t1 = pT[j]
                G_, BON, AR_, BT, KT, VBT, PC = pG[par], pBON[par], pAR[par], pBT[par], pKT[par], pVB[par], pPC[par]
                if b >= 2:
                    yield from wait_pflags([('tail', b - 2)])
                pi = yield from acquire()
                pm = PMA[pi]
                self.mm(pm.t[:, :N], LORA.t[0:64, cs], pLIN.t[0:64, :N], True, True, LORA.all + pLIN.all, pm.all)
                yield
                self.act(pSG.t[:, j, :N], pm.t[:, :N], AF.Tanh, pm.all + DR2.all, [pSG.k[j]], scale=0.5, bias=H2(0, j))
                release(pi)
                yield
                pi = yield from acquire()
                pm = PMA[pi]
                self.mm(pm.t[:, :N], LORA.t[64:128, cs], pLIN.t[64:128, :N], True, True, LORA.all + pLIN.all, pm.all)
                yield
                self.act(pASIG.t[:, j, :N], pm.t[:, :N], AF.Tanh, pm.all + DR2.all, [pASIG.k[j]], scale=0.5, bias=H2(1, j))
                release(pi)
                yield
                pi = yield from acquire()
                pm = PMA[pi]
                self.mm(pm.t[:, :N], GUP.t[:, cs], pGS.t[:, :N], True, True, GUP.all + pGS.all, pm.all)
                yield
                self.cp('act', G_.t[:, j, :N], pm.t[:, :N], pm.all, [G_.k[j]])
                release(pi)
                yield
                t0b = t0.t[:, :].bitcast(BF16)
                self.act(t0b[:, :N], pK.t[:, j, :N], AF.Square, [pK.k[j]] + VEC.all, t0.all, scale=V('k_k', j))
                yield
                pi = yield from acquire()
                pm = PMA[pi]
                self.mm(pm.t[:, :N], BONESB, t0b[:, :N], True, True, t0.all + ONB.all, pm.all)
                yield
                self.ts('dve', t1.t[:, :N], pm.t[:, :N], 2.0 ** -60, ALU.max, pm.all, t1.all)
                release(pi)
                yield
                yield from wait_sync(sp)
                self.act(t1.t[:, :N], t1.t[:, :N], AF.Ln, t1.all, t1.all)
                yield
                self.act(t1.t[:, :N], t1.t[:, :N], AF.Exp, t1.all, t1.all, scale=-0.5)
                yield
                self.stt(pKKN.t[:, j, :N], pK.t[:, j, :N], V('k_k', j), t1.t[:, :N], ALU.mult, ALU.mult,
                         [pK.k[j]] + VEC.all + t1.all, [pKKN.k[j]])
                yield
                self.ts('dve', t0.t[:, :N], pASIG.t[:, j, :N], -1.0, ALU.add, [pASIG.k[j]] + DR2.all, t0.all,
                        s2=H2(2, j), op1=ALU.mult)
                yield
                self.stt(pK.t[:, j, :N], t0.t[:, :N], 1.0, pK.t[:, j, :N], ALU.add, ALU.mult, t0.all + [pK.k[j]], [pK.k[j]])
                yield
                self.stt(t0b[:, :N], pR.t[:, j, :N], V('r_k', j), pK.t[:, j, :N], ALU.mult, ALU.mult,
                         [pR.k[j], pK.k[j]] + VEC.all, t0.all)
                yield
                pi = yield from acquire()
                pm = PMA[pi]
                self.mm(pm.t[:, :N], BONESB, t0b[:, :N], True, True, t0.all + ONB.all, pm.all)
                yield
                self.tt('dve', BON.t[:, j, :N], pm.t[:, :N], pV.t[:, j, :N], ALU.mult, pm.all + [pV.k[j]], [BON.k[j]])
                release(pi)
                yield
                for ci in range(NCK):
                    sl = slice(ci * CH, (ci + 1) * CH)
                    self.scan(pCSG.t[:, j, sl], ONES64, pSG.t[:, j, sl], 0.0, [pSG.k[j]] + CON.all, [pCSG.k[j]], op0=ALU.add)
                    yield
                self.act(pECW.t[:, j, :N], pCSG.t[:, j, :N], AF.Exp, [pCSG.k[j]], [pECW.k[j]], scale=-0.5 * CDEC)
                yield
                self.act(pENCW.t[:, j, :N], pCSG.t[:, j, :N], AF.Exp, [pCSG.k[j]], [pENCW.k[j]], scale=0.5 * CDEC)
                yield
                v3 = lambda b_, a, bnd: b_.t[:, j, :N].rearrange("p (c t) -> p c t", t=CH)[:, :, a:bnd]
                at3 = AR_.t[:, j, 0, :N].rearrange("p (c t) -> p c t", t=CH)
                self.stt(at3[:, :, 1:CH], v3(pKKN, 1, CH), -1.0, v3(pECW, 0, CH - 1), ALU.mult, ALU.mult,
                         [pKKN.k[j], pECW.k[j]], [AR_.k[j]])
                self.ts('pool', at3[:, :, 0:1], v3(pKKN, 0, 1), -1.0, ALU.mult, [pKKN.k[j]], [AR_.k[j]])
                yield
                self.tt('pool', AR_.t[:, j, 1, :N], pR.t[:, j, :N], pECW.t[:, j, :N], ALU.mult, [pR.k[j], pECW.k[j]], [AR_.k[j]])
                yield
                self.stt(t0.t[:, :N], pASIG.t[:, j, :N], 1.0, pKKN.t[:, j, :N], ALU.add, ALU.mult, [pKKN.k[j], pASIG.k[j]], t0.all)
                yield
                self.stt(BT.t[:, j, :N], t0.t[:, :N], 0.5, pENCW.t[:, j, :N], ALU.mult, ALU.mult, t0.all + [pENCW.k[j]], [BT.k[j]])
                yield
                self.tt('pool', KT.t[:, j, :N], pK.t[:, j, :N], pENCW.t[:, j, :N], ALU.mult, [pK.k[j], pENCW.k[j]], [KT.k[j]])
                yield
                self.cp('pool', VBT.t[:, j, :N], pV.t[:, j, :N], [pV.k[j]], [VBT.k[j]])
                self.cp('pool', PC.t[:, j, :], v3(pECW, CH - 1, CH), [pECW.k[j]], [PC.k[j]])
                yield

            def lru_chain(b, j, sp):
                par = b % 2
                last = (b == NBLK - 1)
                xn = pXN[b % 3]
                xc, gxs, gas, a_, uu = pL[j]
                oc = 14 + j
                pi = yield from acquire()
                pm = PMA[pi]
                for k in range(8):
                    self.mm(pm.t[:, :N + 3], W_IN.t[:, k, oc * 128:(oc + 1) * 128], xn.t[:, k, 0:N + 3], k == 0, k == 7,
                            [WK(oc), xn.k[k]], pm.all)
                yield
                self.act(xc.t[:, :N], pm.t[:, 3:N + 3], AF.Identity, pm.all + VEC.all, xc.all, scale=V('lcw', 12 + j), bias=V('lcb', j))
                if last:
                    self.cp('act', LCO.t[:, j, :, 0], pm.t[:, N:N + 3], pm.all, LCO.all)
                yield
                for i in range(3):
                    self.stt(xc.t[:, :N], pm.t[:, i:i + N], V('lcw', 4 * i + j), xc.t[:, :N], ALU.mult, ALU.add,
                             pm.all + VEC.all + xc.all, xc.all)
                    yield
                release(pi)
                xcb = pXCB[j]
                self.cp('pool', xcb.t[:, :N], xc.t[:, :N], xc.all, xcb.all)
                yield
                pi1, pi2 = yield from acquire2()
                pgx, pga = PMA[pi1], PMA[pi2]
                for h2 in range(2):
                    ps_ = slice(64 * h2, 64 * h2 + 64)
                    self.mm(pgx.t[ps_, :N], GXA.t[ps_, 0, j, :], xcb.t[ps_, :N], True, True, GXA.all + xcb.all, pgx.all)
                    self.mm(pga.t[ps_, :N], GXA.t[ps_, 1, j, :], xcb.t[ps_, :N], True, True, GXA.all + xcb.all, pga.all)
                yield
                self.act(gxs.t[:, :N], pgx.t[:, :N], AF.Tanh, pgx.all + DR2.all, gxs.all, scale=0.5, bias=H2(3, j))
                release(pi1)
                self.act(gas.t[:, :N], pga.t[:, :N], AF.Tanh, pga.all + DR2.all, gas.all, scale=0.5, bias=H2(4, j))
                release(pi2)
                yield
                self.act(a_.t[:, :N], gas.t[:, :N], AF.Exp, gas.all + DR2.all, a_.all, scale=H2(5, j), bias=H2(5, j))
                self.act(gas.t[:, :N], gas.t[:, :N], AF.Exp, gas.all + DR2.all, gas.all, scale=H2(6, j), bias=H2(6, j))
                yield
                self.ts('dve', gas.t[:, :N], gas.t[:, :N], 1.0 - 2.0 ** -23, ALU.min, gas.all, gas.all)
                self.stt(uu.t[:, :N], gxs.t[:, :N], 1.0, xc.t[:, :N], ALU.add, ALU.mult, gxs.all + xc.all, uu.all)
                yield
                yield from wait_sync(sp)
                self.act(gas.t[:, :N], gas.t[:, :N], AF.Ln, gas.all, gas.all, scale=-1.0, bias=1.0)
                yield
                self.act(gas.t[:, :N], gas.t[:, :N], AF.Exp, gas.all, gas.all, scale=0.5, bias=math.log(0.5))
                yield
                self.tt('dve', uu.t[:, :N], uu.t[:, :N], gas.t[:, :N], ALU.mult, uu.all + gas.all, uu.all)
                yield
                self.scan(xc.t[:, :N], a_.t[:, :N], uu.t[:, :N], HCAR.t[:, j:j + 1], a_.all + uu.all + HCAR.all, xc.all)
                yield
                self.cp('pool', HCAR.t[:, j:j + 1], xc.t[:, N - 1:N], xc.all, HCAR.all)
                if last:
                    self.cp('pool', LHO.t[:, j, 0:1], xc.t[:, N - 1:N], xc.all, LHO.all)
                self.cp('pool', pHSB[par].t[:, j, :N], xc.t[:, :N], xc.all, [pHSB[par].k[j]])
                yield

            def S1head(b):
                xb, xn = pXB[b % 3], pXN[b % 3]
                self.dma('sp', xb.t[:, :, :N], xTv[:, :, b * N:(b + 1) * N], [], xb.all, pxsem[b % 3])
                yield
                yield from rms_p([xb.t[:, k, :N] for k in range(8)], xb.k, 0)
                if b == 0:
                    self.memset('pool', xn.t[:, :, 0:3], 0.0, xn.all)
                else:
                    self.cp('pool', xn.t[:, :, 0:3], pXN[(b - 1) % 3].t[:, :, N:N + 3], pXN[(b - 1) % 3].all, xn.all)
                for k in range(8):
                    self.stt(xn.t[:, k, 3:N + 3], xb.t[:, k, :N], V('g_pre', k), RSH[0].t[:, :N], ALU.mult, ALU.mult,
                             [xb.k[k]] + VEC.all + RSH[0].all, [xn.k[k]])
                    yield
                pflags[('head', b)] = True

            def S1rest(b):
                yield from wait_pflags([('head', b)])
                yield from window([rw_proj(b, oc) for oc in (12, 13, 0, 1, 2, 3, 4, 5, 6, 7, 8, 9, 10, 11)], 3)
                chains = []
                sp = SyncPt(8)
                for j in range(4):
                    chains.append(lru_chain(b, j, sp))
                    chains.append(rwkv_chain(b, j, sp))
                yield from window(chains, 8)

            def S2(b):
                par = b % 2
                AR_, BT, KT, VBT, PC, YFM = pAR[par], pBT[par], pKT[par], pVB[par], pPC[par], pYFM[par]
                m1b = MASK1.unsqueeze(1).to_broadcast([128, 4, 128])
                m2b = MASK2.unsqueeze(1).to_broadcast([128, 4, CH])
                pa4 = PA.t[:, :].rearrange("p (a b) -> p a b", a=4)
                pb4 = PB_.t[:, :].rearrange("p (a b) -> p a b", a=4)
                px = PX.t[:, :].rearrange("p (s a b) -> p s a b", s=2, a=4)
                pw = PW.t[:, :].rearrange("p (s a b) -> p s a b", s=2, a=4)
                pt = PT.t[:, 0:3 * 4 * CH].rearrange("p (s a b) -> p s a b", s=3, a=4)
                H8 = [(h // 2, slice(64 * (h % 2), 64 * (h % 2) + 64), h % 2) for h in range(8)]
                for ci in range(NCK):
                    sl = slice(ci * CH, (ci + 1) * CH)
                    for si, src in enumerate((BT, KT, VBT)):
                        for j, p_, h2 in H8:
                            self.tr(pt[p_, si, j, :], src.t[p_, j, sl], IDB.t[p_, 64 * h2:64 * h2 + 64], [src.k[j]] + IDB.all, PT.all)
                    self.cp('act', TOK.t, pt, PT.all, TOK.all)
                    yield
                    for j, p_, h2 in H8:
                        self.mm(pa4[p_, j, :], BT.t[p_, j, sl], AR_.t[p_, j, :, sl], True, True, [BT.k[j], AR_.k[j]], PA.all)
                    self.tt('dve', MBS.t, pa4, m1b, ALU.mult, PA.all + CON.all, MBS.all)
                    yield
                    for j, p_, h2 in H8:
                        self.mm(pb4[p_, j, :], KT.t[p_, j, sl], AR_.t[p_, j, :, sl], True, True, [KT.k[j], AR_.k[j]], PB_.all)
                    self.tt('dve', MKS.t, pb4, m1b, ALU.mult, PB_.all + CON.all, MKS.all)
                    yield
                    for j, p_, h2 in H8:
                        self.mm(px[p_, 1, j, :], AR_.t[p_, j, 0, sl], BT.t[p_, j, sl], True, True, [BT.k[j], AR_.k[j]], PX.all)
                    self.tt('dve', XT0.t, px[:, 1], m2b, ALU.mult, PX.all + CON.all, XT0.all)
                    yield
                    for j, p_, h2 in H8:
                        self.mm(pw[p_, 0, j, :], AR_.t[p_, j, 0, sl], HB.t[p_, j, :], True, False, [AR_.k[j]] + HB.all, PW.all)
                        self.mm(pw[p_, 0, j, :], MKS.t[p_, j, 0:CH], TOK.t[p_, 2, j, :], False, True, MKS.all + TOK.all, PW.all)
                    self.cp('act', WF.t, pw[:, 0], PW.all, WF.all)
                    self.cp('dve', WB[0].t, pw[:, 0], PW.all, WB[0].all)
                    yield
                    Xc, XTc = (MBS.t[:, :, 0:CH], MBS.all), (XT0.t, XT0.all)
                    for it in range(6):
                        wb_in, wb_out = WB[it % 2], WB[(it + 1) % 2]
                        for j, p_, h2 in H8:
                            self.mm(pw[p_, 1, j, :], Xc[0][p_, j, :], wb_in.t[p_, j, :], True, True, Xc[1] + wb_in.all, PW.all)
                        if it < 5:
                            xx = XX[it % 2]
                            for j, p_, h2 in H8:
                                self.mm(px[p_, 0, j, :], XTc[0][p_, j, :], Xc[0][p_, j, :], True, True, Xc[1] + XTc[1], PX.all)
                                self.mm(px[p_, 1, j, :], Xc[0][p_, j, :], XTc[0][p_, j, :], True, True, Xc[1] + XTc[1], PX.all)
                        yield
                        self.tt('dve', WF.t, WF.t, pw[:, 1], ALU.add, WF.all + PW.all, WF.all)
                        if it < 5:
                            self.cp('act', xx.t, px, PX.all, xx.all)
                            Xc, XTc = (xx.t[:, 0], xx.all), (xx.t[:, 1], xx.all)
                        yield
                        self.cp('pool', wb_out.t, WF.t, WF.all, wb_out.all)
                        yield
                    UB = WB[0]
                    for j, p_, h2 in H8:
                        self.mm(pa4[p_, j, 0:CH], HB.t[p_, j, :], AR_.t[p_, j, 1, sl], True, False, HB.all + [AR_.k[j]], PA.all)
                        self.mm(pa4[p_, j, 0:CH], UB.t[p_, j, :], MBS.t[p_, j, CH:128], False, False, UB.all + MBS.all, PA.all)
                        self.mm(pa4[p_, j, 0:CH], TOK.t[p_, 2, j, :], MKS.t[p_, j, CH:128], False, True, TOK.all + MKS.all, PA.all)
                    self.cp('act', YFM.t[:, :, sl], pa4[:, :, 0:CH], PA.all, YFM.all)
                    yield
                    for j, p_, h2 in H8:
                        self.mm(pb4[p_, j, 0:CH], TOK.t[p_, 0, j, :], UB.t[p_, j, :], True, False, TOK.all + UB.all, PB_.all)
                        self.mm(pb4[p_, j, 0:CH], TOK.t[p_, 1, j, :], TOK.t[p_, 2, j, :], False, True, TOK.all, PB_.all)
                    self.tt('dve', HTMP.t, pb4[:, :, 0:CH], HF.t, ALU.add, PB_.all + HF.all, HTMP.all)
                    yield
                    pc = PC.t[:, :, ci:ci + 1].to_broadcast([128, 4, CH])
                    self.tt('dve', HF.t, HTMP.t, pc, ALU.mult, HTMP.all + PC.all, HF.all)
                    self.cp('pool', HB.t, HF.t, HF.all, HB.all)
                    yield

            def gn_chain(b, j):
                par = b % 2
                YFM, BON, G_ = pYFM[par], pBON[par], pG[par]
                t0, t1 = pT[j]
                t2 = pL[j][0]
                pi = yield from acquire()
                pm = PMA[pi]
                self.mm(pm.t[:, :N], BONES, YFM.t[:, j, :N], True, True, [YFM.k[j]] + CON.all, pm.all)
                yield
                self.stt(t0.t[:, :N], pm.t[:, :N], -1.0 / 64, YFM.t[:, j, :N], ALU.mult, ALU.add, pm.all + [YFM.k[j]], t0.all)
                release(pi)
                yield
                t1b = t1.t[:, :].bitcast(BF16)
                self.act(t1b[:, :N], t0.t[:, :N], AF.Square, t0.all, t1.all)
                yield
                pi = yield from acquire()
                pm = PMA[pi]
                self.mm(pm.t[:, :N], BONESB, t1b[:, :N], True, True, t1.all + ONB.all, pm.all)
                yield
                self.act(t2.t[:, :N], pm.t[:, :N], AF.Ln, pm.all, t2.all, scale=1.0 / 64, bias=64e-5)
                release(pi)
                yield
                self.act(t2.t[:, :N], t2.t[:, :N], AF.Exp, t2.all, t2.all, scale=-0.5)
                yield
                self.tt('dve', t0.t[:, :N], t0.t[:, :N], t2.t[:, :N], ALU.mult, t0.all + t2.all, t0.all)
                yield
                self.ts('dve', t0.t[:, :N], t0.t[:, :N], V('lnx_w', j), ALU.mult, t0.all + VEC.all, t0.all,
                        s2=V('lnx_b', j), op1=ALU.add)
                yield
                self.tt('pool', t0.t[:, :N], t0.t[:, :N], BON.t[:, j, :N], ALU.add, t0.all + [BON.k[j]], t0.all)
                yield
                self.tt('pool', pZB.t[:, j, :N], t0.t[:, :N], G_.t[:, j, :N], ALU.mult, t0.all + [G_.k[j]], [pZB.k[j]])
                yield

            def gproj(b, oc, gt):
                xn = pXN[b % 3]
                pi = yield from acquire()
                pm = PMA[pi]
                for k in range(8):
                    self.mm(pm.t[:, :N], W_IN.t[:, k, oc * 128:(oc + 1) * 128], xn.t[:, k, 3:N + 3], k == 0, k == 7,
                            [WK(oc), xn.k[k]], pm.all)
                yield
                self.act(gt.t[:, :N], pm.t[:, :N], AF.Tanh, pm.all, gt.all, scale=0.5)
                release(pi)
                yield

            def mix_chain(b, oc):
                par = b % 2
                cs = slice(oc * 128, (oc + 1) * 128)
                ga, gb = pGT[(2 * oc) % 4], pGT[(2 * oc + 1) % 4]
                yield from gproj(b, 18 + oc, ga)
                yield from gproj(b, 26 + oc, gb)
                pi = yield from acquire()
                pm = PMA[pi]
                for j in range(4):
                    self.mm(pm.t[:, :N], WOR.t[:, j, cs], pZB.t[:, j, :N], j == 0, j == 3, WOR.all + [pZB.k[j]], pm.all)
                yield
                self.stt(ga.t[:, :N], ga.t[:, :N], 1.0, pm.t[:, :N], ALU.add, ALU.mult, pm.all + ga.all, ga.all)
                release(pi)
                yield
                pi = yield from acquire()
                pm = PMA[pi]
                for j in range(4):
                    self.mm(pm.t[:, :N], WOL.t[:, j, cs], pHSB[par].t[:, j, :N], j == 0, j == 3, WOL.all + [pHSB[par].k[j]], pm.all)
                yield
                self.stt(gb.t[:, :N], gb.t[:, :N], 1.0, pm.t[:, :N], ALU.add, ALU.mult, pm.all + gb.all, gb.all)
                release(pi)
                yield
                self.tt('pool', pMIX.t[:, oc, :N], ga.t[:, :N], gb.t[:, :N], ALU.add, ga.all + gb.all, [pMIX.k[oc]])
                yield

            def wo_chain(b, oc):
                cs = slice(oc * 128, (oc + 1) * 128)
                mo = pT[oc % 4][oc // 4]
                pi = yield from acquire()
                pm = PMA[pi]
                for k in range(8):
                    self.mm(pm.t[:, :N], WO.t[:, k, cs], pMIX.t[:, k, :N], k == 0, k == 7, WO.all + [pMIX.k[k]], pm.all)
                yield
                self.act(mo.t[:, :N], pm.t[:, :N], AF.Identity, pm.all, mo.all, scale=0.5)
                release(pi)
                yield

            class _MO:
                pass
            MOB = _MO()
            MOB.k = pR.k + pK.k

            def S3front(b):
                yield from window([gn_chain(b, j) for j in range(4)], 4)
                yield from window([mix_chain(b, oc) for oc in range(8)], 2)

            def S3tail(b):
                xb = pXB[b % 3]
                yield from window([wo_chain(b, oc) for oc in range(8)], 4)
                mos = [pT[k % 4][k // 4] for k in range(8)]
                yield from rms_p([m_.t[:, :N] for m_ in mos], [m_.k[0] for m_ in mos], 1)
                for k in range(8):
                    mo = mos[k]
                    self.stt(mo.t[:, :N], mo.t[:, :N], V('g_post', k), RSH[1].t[:, :N], ALU.mult, ALU.mult,
                             mo.all + VEC.all + RSH[1].all, mo.all)
                    self.tt('pool', xb.t[:, k, :N], xb.t[:, k, :N], mo.t[:, :N], ALU.add, [xb.k[k]] + mo.all, [xb.k[k]])
                    yield
                self.dma('sp', x1v[:, :, b * N:(b + 1) * N], xb.t[:, :, :N], xb.all, [X1S], px1sem[b % 3])
                pflags[('tail', b)] = True
                yield

            def par(*gs):
                gs = list(gs)
                while gs:
                    for g_ in list(gs):
                        try:
                            next(g_)
                        except StopIteration:
                            gs.remove(g_)
                        yield

            rr([chain(S1head(0), S1rest(0))], [1])
            for b in range(NBLK):
                later = []
                if b > 0:
                    later.append(S3tail(b - 1))
                if b + 1 < NBLK:
                    later.append(S1rest(b + 1))
                mparts = []
                if b > 0:
                    mparts.append(S3front(b - 1))
                mparts.append(par(*later))
                streams = [chain(*mparts), S2(b)]
                weights = [8, 1]
                if b + 1 < NBLK:
                    streams.append(S1head(b + 1))
                    weights.append(1)
                rr(streams, weights)
            rr([chain(S3front(NBLK - 1), S3tail(NBLK - 1))], [1])
            self.cp('pool', WKP.t, HF.t, HF.all, WKP.all)
            S.barrier()
            A.off = markA
            N_ = NS
            XB = A.alloc("xb", [8, N_], F32, nk=8)
            XN = A.alloc("xn", [8, N_], BF16, nk=8)
            PBF = [A.alloc("pbf%d" % i, [N_ + 1], F32) for i in range(3)]
            DT = [A.alloc("dt%d" % i, [N_], F32) for i in range(3)]
            Q = [A.alloc("q%d" % i, [4, N_], F32, nk=4) for i in range(11)]
            LIN = A.alloc("lin", [N_], BF16)
            GS = A.alloc("gs", [N_], BF16)
            AR_ = A.alloc("ar", [4, 2, N_], BF16, nk=4)
            BT = A.alloc("bt", [4, N_], BF16, nk=4)
            KT = A.alloc("kt", [4, N_], BF16, nk=4)
            VBT = A.alloc("vbt", [4, N_], BF16, nk=4)
            TOK = A.alloc("tok", [3, 4, CH], BF16)
            MBS = A.alloc("mbs", [4, 128], BF16)
            MKS = A.alloc("mks", [4, 128], BF16)
            XX = [A.alloc("xx%d" % i, [2, 4, CH], BF16) for i in range(2)]
            XT0 = A.alloc("xt0", [4, CH], BF16)
            WF = A.alloc("wf", [4, CH], F32)
            WB = [A.alloc("wb%d" % i, [4, CH], BF16) for i in range(2)]
            HF = A.alloc("hf", [4, CH], F32)
            HB = A.alloc("hb", [4, CH], BF16)
            HTMP = A.alloc("htmp", [4, CH], F32)
            XBB = A.alloc("xbb", [4, N_ + 3], F32, nk=4)
            XCB = A.alloc("xcb", [N_], BF16)
            HSB = A.alloc("hsb", [4, N_], BF16, nk=4)
            HCAR = A.alloc("hcar", [4], F32)
            ZB = A.alloc("zb", [4, N_], BF16, nk=4)
            MIXB = A.alloc("mixb", [8, N_], BF16, nk=8)
            GT = [A.alloc("gt%d" % i, [N_], F32) for i in range(2)]
            DTJ = [[A.alloc("dtj%d_%d" % (j_, i), [N_], F32) for i in range(3)] for j_ in range(4)]
            GTJ = [[A.alloc("gtj%d_%d" % (j_, i), [N_], F32) for i in range(2)] for j_ in range(4)]
            XCBJ = [A.alloc("xcbj%d" % j_, [N_], BF16) for j_ in range(4)]
            HS = A.alloc("hs", [4, NS, 64], F32, nk=4)
            SPCS = [[A.alloc("spc%d_%d" % (h_, i), [512], F32) for i in range(5)] for h_ in range(2)]
            SST = {n_: A.alloc(n_, sh, F32) for n_, sh in
                   (("sshift", [14, NS]), ("slconv", [4, 3, NS]), ("slh", [4, NS]))}
            print("phase A arena used", A.off, "top", A.top)

            self.memset('pool', HF.t, 0.0, HF.all)
            self.memset('pool', HB.t, 0.0, HB.all)
            self.memset('pool', HCAR.t, 0.0, HCAR.all)
            self.memset('pool', XBB.t, 0.0, XBB.all)
            ssem = [S.new_dma_sem() for _ in range(4)]
            self.dma('sp', SST["sshift"].t, s_shift.rearrange("p (a b) -> p a b", a=14), [], SST["sshift"].all, ssem[0])
            self.dma('sp', SST["slconv"].t, s_lconv.rearrange("p (a b c) -> p a b c", a=4, b=3), [], SST["slconv"].all, ssem[1])
            self.dma('sp', SST["slh"].t, s_lh.rearrange("p (a b) -> p a b", a=4), [], SST["slh"].all, ssem[2])
            self.dma('sp', HS.t, s_wkv.rearrange("p (a b c) -> p a b c", a=4, b=NS), [], HS.all, ssem[3])

            xsem = S.new_dma_sem()
            x1sem = S.new_dma_sem()
            R_, K_, V_, SG, CSG, ASIG, KKN, ECW, ENCW, G_, BON = Q
            YFM = SG
            MOF = None
            xTv = xT.rearrange("(k p) n -> p k n", p=128)
            x1v = x1s.rearrange("(k p) n -> p k n", p=128)
            yTv = yT.rearrange("(k p) n -> p k n", p=128)

            PMS = [PM[0], PM[1], PW]
            pm_res = {}

            def pm_next():
                while True:
                    for d_ in range(3):
                        i_ = (self.pmi + d_) % 3
                        b_ = PMS[i_]
                        lw = b_.k[0].last_w
                        if i_ in pm_res:
                            if lw is not None and lw != pm_res[i_] and lw[1] != 'pe':
                                del pm_res[i_]
                        if i_ not in pm_res:
                            pm_res[i_] = lw
                            self.pmi = (i_ + 1) % 3
                            return b_
                    assert getattr(_TL, 'sw', None) is not None, "no free PSUM bank in sequential mode"
                    co_switch()

            def rms(src, nch, N, scale_div):
                for k in range(nch):
                    sq = SQ[k % 2]
                    self.act(sq.t[:, :N], src.t[:, k, :N], AF.Square, [src.k[k]], sq.all)
                    self.mm(PST.t[:, :N], ONESF, sq.t[:, :N], k == 0, k == nch - 1, sq.all + CON.all, PST.all)
                self.act(RSTD.t[:, :N], PST.t[:, :N], AF.Ln, PST.all, RSTD.all, scale=1.0 / scale_div, bias=1e-6)
                self.act(RSTD.t[:, :N], RSTD.t[:, :N], AF.Exp, RSTD.all, RSTD.all, scale=-0.5)

            def proj(oc, N):
                pm = pm_next()
                for k in range(8):
                    self.mm(pm.t[:, :N], W_IN.t[:, k, oc * 128:(oc + 1) * 128], XN.t[:, k, :N], k == 0, k == 7,
                            [WK(oc), XN.k[k]], pm.all)
                return pm

            def blockA(c0, N, sample, last):
                self.dma('sp', XB.t[:, :, :N], xTv[:, :, c0:c0 + N], [], XB.all, xsem)
                rms(XB, 8, N, D)
                for k in range(8):
                    self.stt(XN.t[:, k, :N], XB.t[:, k, :N], V('g_pre', k), RSTD.t[:, :N], ALU.mult, ALU.mult,
                             [XB.k[k], VEC.k[0], RSTD.k[0]], [XN.k[k]])
                WA = DT[2]
                for oc in range(14):
                    pm = proj(oc, N)
                    if oc < 12:
                        dst = Q[oc // 4]
                        dst_ap, dst_k = dst.t[:, oc % 4, :N], [dst.k[oc % 4]]
                    else:
                        dst_ap, dst_k = WA.t[:, :N], WA.all
                    dt_ = DT[oc % 2]
                    if not sample:
                        pb = PBF[oc % 3]
                        self.cp('act', pb.t[:, 1:N + 1], pm.t[:, :N], pm.all, pb.all)
                        self.cp('pool', pb.t[:, 0:1], SHO.t[:, oc, 0:1], SHO.all, pb.all)
                        self.cp('pool', SHO.t[:, oc, 0:1], pb.t[:, N:N + 1], pb.all, SHO.all)
                        self.tt('dve', dt_.t[:, :N], pb.t[:, 0:N], pb.t[:, 1:N + 1], ALU.subtract, pb.all, dt_.all)
                        self.stt(dst_ap, dt_.t[:, :N], V('mu', oc), pb.t[:, 1:N + 1], ALU.mult, ALU.add,
                                 dt_.all + pb.all + VEC.all, dst_k)
                    else:
                        pcur = SHO.t[:, oc, 1:1 + NS]
                        self.cp('act', pcur, pm.t[:, :N], pm.all, SHO.all)
                        self.tt('dve', dt_.t[:, :N], SST["sshift"].t[:, oc, :], pcur, ALU.subtract,
                                SHO.all + SST["sshift"].all, dt_.all)
                        self.stt(dst_ap, dt_.t[:, :N], V('mu', oc), pcur, ALU.mult, ALU.add,
                                 dt_.all + SHO.all + VEC.all, dst_k)
                    if oc == 12:
                        self.act(LIN.t[0:64, :N], WA.t[0:64, :N], AF.Tanh, WA.all, LIN.all)
                        self.cp('pool', LIN.t[64:128, :N], WA.t[64:128, :N], WA.all, LIN.all)
                    if oc == 13:
                        self.act(GS.t[:, :N], WA.t[:, :N], AF.Sigmoid, WA.all, GS.all)
                for j in range(4):
                    pm = proj(14 + j, N)
                    if not sample:
                        self.cp('act', XBB.t[:, j, 3:N + 3], pm.t[:, :N], pm.all, [XBB.k[j]])
                    else:
                        self.cp('act', LCO.t[:, j, 2, 1:1 + NS], pm.t[:, :N], pm.all, LCO.all)
                def _body1(j):
                    DT, GT, XCB = DTJ[j], GTJ[j], XCBJ[j]
                    cs = slice(j * 128, (j + 1) * 128)
                    pm = pm_next()
                    self.mm(pm.t[:, :N], LORA.t[0:64, cs], LIN.t[0:64, :N], True, True, LORA.all + LIN.all, pm.all)
                    self.act(SG.t[:, j, :N], pm.t[:, :N], AF.Sigmoid, pm.all + VEC.all, [SG.k[j]], bias=V('w0', j))
                    pm = pm_next()
                    self.mm(pm.t[:, :N], LORA.t[64:128, cs], LIN.t[64:128, :N], True, True, LORA.all + LIN.all, pm.all)
                    self.act(ASIG.t[:, j, :N], pm.t[:, :N], AF.Sigmoid, pm.all + VEC.all, [ASIG.k[j]], bias=V('a0', j))
                    pm = pm_next()
                    self.mm(pm.t[:, :N], GUP.t[:, cs], GS.t[:, :N], True, True, GUP.all + GS.all, pm.all)
                    self.cp('act', G_.t[:, j, :N], pm.t[:, :N], pm.all, [G_.k[j]])
                    t0, t1 = DT[0], DT[1]
                    self.act(t0.t[:, :N], K_.t[:, j, :N], AF.Square, [K_.k[j]] + VEC.all, t0.all, scale=V('k_k', j))
                    pm = pm_next()
                    self.mm(pm.t[:, :N], BONES, t0.t[:, :N], True, True, t0.all + CON.all, pm.all)
                    self.act(t1.t[:, :N], pm.t[:, :N], AF.Sqrt, pm.all, t1.all)
                    self.ts('dve', t1.t[:, :N], t1.t[:, :N], 1e-12, ALU.max, t1.all, t1.all)
                    self.recip(t1.t[:, :N], t1.t[:, :N], t1.all, t1.all)
                    self.stt(KKN.t[:, j, :N], K_.t[:, j, :N], V('k_k', j), t1.t[:, :N], ALU.mult, ALU.mult,
                             [K_.k[j]] + VEC.all + t1.all, [KKN.k[j]])
                    self.ts('dve', t0.t[:, :N], ASIG.t[:, j, :N], -1.0, ALU.add, [ASIG.k[j]] + VEC.all, t0.all,
                            s2=V('k_a', j), op1=ALU.mult)
                    self.stt(K_.t[:, j, :N], t0.t[:, :N], 1.0, K_.t[:, j, :N], ALU.add, ALU.mult,
                             t0.all + [K_.k[j]], [K_.k[j]])
                    self.stt(t0.t[:, :N], R_.t[:, j, :N], V('r_k', j), K_.t[:, j, :N], ALU.mult, ALU.mult,
                             [R_.k[j], K_.k[j]] + VEC.all, t0.all)
                    pm = pm_next()
                    self.mm(pm.t[:, :N], BONES, t0.t[:, :N], True, True, t0.all + CON.all, pm.all)
                    self.tt('dve', BON.t[:, j, :N], pm.t[:, :N], V_.t[:, j, :N], ALU.mult, pm.all + [V_.k[j]], [BON.k[j]])
                co_run([(lambda j=j: _body1(j)) for j in range(4)])
                if not sample:
                    wkv_prompt(N)
                else:
                    wkv_sample()
                def _body2(j):
                    DT, GT, XCB = DTJ[j], GTJ[j], XCBJ[j]
                    xc, gxs, gas, uu = GT[0], DT[0], DT[1], DT[2]
                    if not sample:
                        taps = [XBB.t[:, j, i:i + N] for i in range(4)]
                        tr_ = [XBB.k[j]]
                    else:
                        taps = [SST["slconv"].t[:, j, i, :] for i in range(3)] + [LCO.t[:, j, 2, 1:1 + NS]]
                        tr_ = SST["slconv"].all + LCO.all
                    self.act(xc.t[:, :N], taps[3], AF.Identity, tr_ + VEC.all, xc.all, scale=V('lcw', 12 + j), bias=V('lcb', j))
                    for i in range(3):
                        self.stt(xc.t[:, :N], taps[i], V('lcw', 4 * i + j), xc.t[:, :N], ALU.mult, ALU.add,
                                 tr_ + VEC.all + xc.all, xc.all)
                    if not sample:
                        self.cp('pool', XBB.t[:, j, 0:3], XBB.t[:, j, N:N + 3], [XBB.k[j]], [XBB.k[j]])
                        if last:
                            self.cp('pool', LCO.t[:, j, :, 0], XBB.t[:, j, N:N + 3], [XBB.k[j]], LCO.all)
                    else:
                        self.cp('pool', LCO.t[:, j, 0:2, 1:1 + NS], SST["slconv"].t[:, j, 1:3, :], SST["slconv"].all, LCO.all)
                    self.cp('act', XCB.t[:, :N], xc.t[:, :N], xc.all, XCB.all)
                    pgx, pga = pm_next(), pm_next()
                    for h2 in range(2):
                        ps_ = slice(64 * h2, 64 * h2 + 64)
                        self.mm(pgx.t[ps_, :N], GXA.t[ps_, 0, j, :], XCB.t[ps_, :N], True, True, GXA.all + XCB.all, pgx.all)
                        self.mm(pga.t[ps_, :N], GXA.t[ps_, 1, j, :], XCB.t[ps_, :N], True, True, GXA.all + XCB.all, pga.all)
                    self.act(gxs.t[:, :N], pgx.t[:, :N], AF.Sigmoid, pgx.all + VEC.all, gxs.all, bias=V('gxb', j))
                    self.act(gas.t[:, :N], pga.t[:, :N], AF.Sigmoid, pga.all + VEC.all, gas.all, bias=V('gab', j))
                    a_ = GT[1]
                    self.act(a_.t[:, :N], gas.t[:, :N], AF.Exp, gas.all + DER.all, a_.all, scale=DER.t[:, j:j + 1])
                    self.act(gas.t[:, :N], gas.t[:, :N], AF.Exp, gas.all + DER.all, gas.all, scale=DER.t[:, 4 + j:5 + j])
                    self.ts('dve', gas.t[:, :N], gas.t[:, :N], -1.0, ALU.mult, gas.all, gas.all, s2=1.0, op1=ALU.add)
                    self.act(gas.t[:, :N], gas.t[:, :N], AF.Sqrt, gas.all, gas.all)
                    self.tt('dve', uu.t[:, :N], gxs.t[:, :N], xc.t[:, :N], ALU.mult, gxs.all + xc.all, uu.all)
                    self.tt('dve', uu.t[:, :N], uu.t[:, :N], gas.t[:, :N], ALU.mult, uu.all + gas.all, uu.all)
                    hs_ = xc
                    if not sample:
                        self.scan(hs_.t[:, :N], a_.t[:, :N], uu.t[:, :N], HCAR.t[:, j:j + 1], a_.all + uu.all + HCAR.all, hs_.all)
                        self.cp('pool', HCAR.t[:, j:j + 1], hs_.t[:, N - 1:N], hs_.all, HCAR.all)
                        if last:
                            self.cp('pool', LHO.t[:, j, 0:1], hs_.t[:, N - 1:N], hs_.all, LHO.all)
                    else:
                        self.tt('dve', hs_.t[:, :N], a_.t[:, :N], SST["slh"].t[:, j, :], ALU.mult, a_.all + SST["slh"].all, hs_.all)
                        self.tt('dve', hs_.t[:, :N], hs_.t[:, :N], uu.t[:, :N], ALU.add, hs_.all + uu.all, hs_.all)
                        self.cp('pool', LHO.t[:, j, 1:1 + NS], hs_.t[:, :N], hs_.all, LHO.all)
                    self.cp('act', HSB.t[:, j, :N], hs_.t[:, :N], hs_.all, [HSB.k[j]])
                co_run([(lambda j=j: _body2(j)) for j in range(4)])
                def _body3(j):
                    DT, GT, XCB = DTJ[j], GTJ[j], XCBJ[j]
                    t0, t1, t2 = DT[0], DT[1], DT[2]
                    pm = pm_next()
                    self.mm(pm.t[:, :N], BONES, YFM.t[:, j, :N], True, True, [YFM.k[j]] + CON.all, pm.all)
                    self.stt(t0.t[:, :N], pm.t[:, :N], -1.0 / 64, YFM.t[:, j, :N], ALU.mult, ALU.add, pm.all + [YFM.k[j]], t0.all)
                    self.act(t1.t[:, :N], t0.t[:, :N], AF.Square, t0.all, t1.all)
                    pm = pm_next()
                    self.mm(pm.t[:, :N], BONES, t1.t[:, :N], True, True, t1.all + CON.all, pm.all)
                    self.act(t2.t[:, :N], pm.t[:, :N], AF.Sqrt, pm.all, t2.all, scale=1.0 / 64, bias=64e-5)
                    self.recip(t2.t[:, :N], t2.t[:, :N], t2.all, t2.all)
                    self.tt('dve', t0.t[:, :N], t0.t[:, :N], t2.t[:, :N], ALU.mult, t0.all + t2.all, t0.all)
                    self.ts('dve', t0.t[:, :N], t0.t[:, :N], V('lnx_w', j), ALU.mult, t0.all + VEC.all, t0.all,
                            s2=V('lnx_b', j), op1=ALU.add)
                    self.tt('dve', t0.t[:, :N], t0.t[:, :N], BON.t[:, j, :N], ALU.add, t0.all + [BON.k[j]], t0.all)
                    self.tt('dve', ZB.t[:, j, :N], t0.t[:, :N], G_.t[:, j, :N], ALU.mult, t0.all + [G_.k[j]], [ZB.k[j]])
                co_run([(lambda j=j: _body3(j)) for j in range(4)])
                def _body4(oc):
                    DT, GT = DTJ[oc % 4], GTJ[oc % 4]
                    cs = slice(oc * 128, (oc + 1) * 128)
                    pga = proj(18 + oc, N)
                    self.act(GT[0].t[:, :N], pga.t[:, :N], AF.Sigmoid, pga.all, GT[0].all)
                    pgb = proj(26 + oc, N)
                    self.act(GT[1].t[:, :N], pgb.t[:, :N], AF.Sigmoid, pgb.all, GT[1].all)
                    poa = pm_next()
                    for j in range(4):
                        self.mm(poa.t[:, :N], WOR.t[:, j, cs], ZB.t[:, j, :N], j == 0, j == 3, WOR.all + [ZB.k[j]], poa.all)
                    self.tt('dve', GT[0].t[:, :N], poa.t[:, :N], GT[0].t[:, :N], ALU.mult, poa.all + GT[0].all, GT[0].all)
                    pob = pm_next()
                    for j in range(4):
                        self.mm(pob.t[:, :N], WOL.t[:, j, cs], HSB.t[:, j, :N], j == 0, j == 3, WOL.all + [HSB.k[j]], pob.all)
                    self.tt('dve', GT[1].t[:, :N], pob.t[:, :N], GT[1].t[:, :N], ALU.mult, pob.all + GT[1].all, GT[1].all)
                    self.tt('dve', MIXB.t[:, oc, :N], GT[0].t[:, :N], GT[1].t[:, :N], ALU.add, GT[0].all + GT[1].all, [MIXB.k[oc]])
                co_run([(lambda oc=oc: _body4(oc)) for oc in range(4)])
                co_run([(lambda oc=oc: _body4(oc)) for oc in range(4, 8)])
                MO = [R_, K_]
                def _body5(oc):
                    DT, GT = DTJ[oc % 4], GTJ[oc % 4]
                    cs = slice(oc * 128, (oc + 1) * 128)
                    pm = pm_next()
                    for k in range(8):
                        self.mm(pm.t[:, :N], WO.t[:, k, cs], MIXB.t[:, k, :N], k == 0, k == 7, WO.all + [MIXB.k[k]], pm.all)
                    mo = MO[oc // 4]
                    self.cp('act', mo.t[:, oc % 4, :N], pm.t[:, :N], pm.all, [mo.k[oc % 4]])
                co_run([(lambda oc=oc: _body5(oc)) for oc in range(4)])
                co_run([(lambda oc=oc: _body5(oc)) for oc in range(4, 8)])
                for k in range(8):
                    mo = MO[k // 4]
                    sq = SQ[k % 2]
                    self.act(sq.t[:, :N], mo.t[:, k % 4, :N], AF.Square, [mo.k[k % 4]], sq.all)
                    self.mm(PST.t[:, :N], ONESF, sq.t[:, :N], k == 0, k == 7, sq.all + CON.all, PST.all)
                self.act(RSTD.t[:, :N], PST.t[:, :N], AF.Sqrt, PST.all, RSTD.all, scale=1.0 / D, bias=1e-6)
                self.recip(RSTD.t[:, :N], RSTD.t[:, :N], RSTD.all, RSTD.all)
                for k in range(8):
                    mo = MO[k // 4]
                    self.stt(mo.t[:, k % 4, :N], mo.t[:, k % 4, :N], V('g_post', k), RSTD.t[:, :N], ALU.mult, ALU.mult,
                             [mo.k[k % 4]] + VEC.all + RSTD.all, [mo.k[k % 4]])
                    self.tt('dve', XB.t[:, k, :N], XB.t[:, k, :N], mo.t[:, k % 4, :N], ALU.add, [XB.k[k], mo.k[k % 4]], [XB.k[k]])
                self.dma('sp', x1v[:, :, c0:c0 + N], XB.t[:, :, :N], XB.all, [X1S], x1sem)

            def wkv_prompt(N):
                nchk = N // CH
                c = CDEC
                for j in range(4):
                    for ci in range(nchk):
                        sl = slice(ci * CH, (ci + 1) * CH)
                        self.scan(CSG.t[:, j, sl], ONES64, SG.t[:, j, sl], 0.0, [SG.k[j]] + CON.all, [CSG.k[j]])
                    self.act(ECW.t[:, j, :N], CSG.t[:, j, :N], AF.Exp, [CSG.k[j]], [ECW.k[j]], scale=-c)
                    self.act(ENCW.t[:, j, :N], CSG.t[:, j, :N], AF.Exp, [CSG.k[j]], [ENCW.k[j]], scale=c)
                    v3 = lambda b_, a, bnd: b_.t[:, j, :N].rearrange("p (c t) -> p c t", t=CH)[:, :, a:bnd]
                    at3 = AR_.t[:, j, 0, :N].rearrange("p (c t) -> p c t", t=CH)
                    self.stt(at3[:, :, 1:CH], v3(KKN, 1, CH), -1.0, v3(ECW, 0, CH - 1), ALU.mult, ALU.mult,
                             [KKN.k[j], ECW.k[j]], [AR_.k[j]])
                    self.ts('dve', at3[:, :, 0:1], v3(KKN, 0, 1), -1.0, ALU.mult, [KKN.k[j]], [AR_.k[j]])
                    self.tt('dve', AR_.t[:, j, 1, :N], R_.t[:, j, :N], ECW.t[:, j, :N], ALU.mult, [R_.k[j], ECW.k[j]], [AR_.k[j]])
                    t0 = DT[0]
                    self.tt('dve', t0.t[:, :N], KKN.t[:, j, :N], ASIG.t[:, j, :N], ALU.mult, [KKN.k[j], ASIG.k[j]], t0.all)
                    self.tt('dve', BT.t[:, j, :N], t0.t[:, :N], ENCW.t[:, j, :N], ALU.mult, t0.all + [ENCW.k[j]], [BT.k[j]])
                    self.tt('dve', KT.t[:, j, :N], K_.t[:, j, :N], ENCW.t[:, j, :N], ALU.mult, [K_.k[j], ENCW.k[j]], [KT.k[j]])
                    self.cp('act', VBT.t[:, j, :N], V_.t[:, j, :N], [V_.k[j]], [VBT.k[j]])
                m1b = MASK1.unsqueeze(1).to_broadcast([128, 4, 128])
                m2b = MASK2.unsqueeze(1).to_broadcast([128, 4, CH])
                pa4 = PA.t[:, :].rearrange("p (a b) -> p a b", a=4)
                pb4 = PB_.t[:, :].rearrange("p (a b) -> p a b", a=4)
                px = PX.t[:, :].rearrange("p (s a b) -> p s a b", s=2, a=4)
                pw = PW.t[:, :].rearrange("p (s a b) -> p s a b", s=2, a=4)
                pt = PT.t[:, 0:3 * 4 * CH].rearrange("p (s a b) -> p s a b", s=3, a=4)
                for ci in range(nchk):
                    sl = slice(ci * CH, (ci + 1) * CH)
                    allk = lambda b_: b_.all
                    for si, src in enumerate((BT, KT, VBT)):
                        for h in range(8):
                            j, h2 = h // 2, h % 2
                            p_ = slice(64 * h2, 64 * h2 + 64)
                            self.tr(pt[p_, si, j, :], src.t[p_, j, sl], IDB.t[p_, 64 * h2:64 * h2 + 64], [src.k[j]] + IDB.all, PT.all)
                    self.cp('act', TOK.t, pt, PT.all, TOK.all)
                    for h in range(8):
                        j, h2 = h // 2, h % 2
                        p_ = slice(64 * h2, 64 * h2 + 64)
                        self.mm(pa4[p_, j, :], BT.t[p_, j, sl], AR_.t[p_, j, :, sl], True, True, [BT.k[j], AR_.k[j]], PA.all)
                        self.mm(pb4[p_, j, :], KT.t[p_, j, sl], AR_.t[p_, j, :, sl], True, True, [KT.k[j], AR_.k[j]], PB_.all)
                        self.mm(px[p_, 1, j, :], AR_.t[p_, j, 0, sl], BT.t[p_, j, sl], True, True, [BT.k[j], AR_.k[j]], PX.all)
                    self.tt('dve', MBS.t, pa4, m1b, ALU.mult, PA.all + CON.all, MBS.all)
                    self.tt('dve', MKS.t, pb4, m1b, ALU.mult, PB_.all + CON.all, MKS.all)
                    self.tt('dve', XT0.t, px[:, 1], m2b, ALU.mult, PX.all + CON.all, XT0.all)
                    for h in range(8):
                        j, h2 = h // 2, h % 2
                        p_ = slice(64 * h2, 64 * h2 + 64)
                        self.mm(pw[p_, 0, j, :], AR_.t[p_, j, 0, sl], HB.t[p_, j, :], True, False, [AR_.k[j]] + HB.all, PW.all)
                        self.mm(pw[p_, 0, j, :], MKS.t[p_, j, 0:CH], TOK.t[p_, 2, j, :], False, True, MKS.all + TOK.all, PW.all)
                    self.cp('act', WF.t, pw[:, 0], PW.all, WF.all)
                    self.cp('dve', WB[0].t, pw[:, 0], PW.all, WB[0].all)
                    Xc, XTc = (MBS.t[:, :, 0:CH], MBS.all), (XT0.t, XT0.all)
                    for it in range(6):
                        wb_in, wb_out = WB[it % 2], WB[(it + 1) % 2]
                        for h in range(8):
                            j, h2 = h // 2, h % 2
                            p_ = slice(64 * h2, 64 * h2 + 64)
                            self.mm(pw[p_, 1, j, :], Xc[0][p_, j, :], wb_in.t[p_, j, :], True, True, Xc[1] + wb_in.all, PW.all)
                        if it < 5:
                            xx = XX[it % 2]
                            for h in range(8):
                                j, h2 = h // 2, h % 2
                                p_ = slice(64 * h2, 64 * h2 + 64)
                                self.mm(px[p_, 0, j, :], XTc[0][p_, j, :], Xc[0][p_, j, :], True, True, Xc[1] + XTc[1], PX.all)
                                self.mm(px[p_, 1, j, :], Xc[0][p_, j, :], XTc[0][p_, j, :], True, True, Xc[1] + XTc[1], PX.all)
                        self.tt('dve', WF.t, WF.t, pw[:, 1], ALU.add, WF.all + PW.all, WF.all)
                        self.cp('act', wb_out.t, WF.t, WF.all, wb_out.all)
                        if it < 5:
                            self.cp('act', xx.t, px, PX.all, xx.all)
                            Xc, XTc = (xx.t[:, 0], xx.all), (xx.t[:, 1], xx.all)
                    UB = WB[0]
                    for h in range(8):
                        j, h2 = h // 2, h % 2
                        p_ = slice(64 * h2, 64 * h2 + 64)
                        self.mm(pa4[p_, j, 0:CH], HB.t[p_, j, :], AR_.t[p_, j, 1, sl], True, False, HB.all + [AR_.k[j]], PA.all)
                        self.mm(pa4[p_, j, 0:CH], UB.t[p_, j, :], MBS.t[p_, j, CH:128], False, False, UB.all + MBS.all, PA.all)
                        self.mm(pa4[p_, j, 0:CH], TOK.t[p_, 2, j, :], MKS.t[p_, j, CH:128], False, True, TOK.all + MKS.all, PA.all)
                    self.cp('act', YFM.t[:, :, sl], pa4[:, :, 0:CH], PA.all, YFM.all)
                    for h in range(8):
                        j, h2 = h // 2, h % 2
                        p_ = slice(64 * h2, 64 * h2 + 64)
                        self.mm(pb4[p_, j, 0:CH], TOK.t[p_, 0, j, :], UB.t[p_, j, :], True, False, TOK.all + UB.all, PB_.all)
                        self.mm(pb4[p_, j, 0:CH], TOK.t[p_, 1, j, :], TOK.t[p_, 2, j, :], False, True, TOK.all, PB_.all)
                    self.tt('dve', HTMP.t, pb4[:, :, 0:CH], HF.t, ALU.add, PB_.all + HF.all, HTMP.all)
                    pc = ECW.t[:, :, ci * CH + CH - 1:ci * CH + CH].to_broadcast([128, 4, CH])
                    self.tt('dve', HF.t, HTMP.t, pc, ALU.mult, HTMP.all + ECW.all, HF.all)
                    self.cp('act', HB.t, HF.t, HF.all, HB.all)

            def wkv_sample():
                N = NS
                WD, BV = ECW, ENCW
                for j in range(4):
                    self.act(WD.t[:, j, :N], SG.t[:, j, :N], AF.Exp, [SG.k[j]], [WD.k[j]], scale=-CDEC)
                    self.tt('dve', BV.t[:, j, :N], KKN.t[:, j, :N], ASIG.t[:, j, :N], ALU.mult, [KKN.k[j], ASIG.k[j]], [BV.k[j]])
                i64b = I64.unsqueeze(1).to_broadcast([128, 8, 64])
                ov = o_wkvs.rearrange("p (a b c) -> p a b c", a=4, b=NS)

                def piece(j, half, bk, spc, sems, pi_):
                    PA_, PB2, PX_ = bk
                    ns = slice(half * 8, half * 8 + 8)
                    bc = lambda b_: b_.t[:, j, ns].unsqueeze(2).to_broadcast([128, 8, 64])
                    hs3 = HS.t[:, j, ns, :]
                    r3 = lambda s_: s_.t[:, :].rearrange("p (a b) -> p a b", a=8)
                    P1, VD, HN, TMP = spc[0], spc[1], spc[2 + pi_ % 2], spc[4]
                    p1, vd, hn, tmp = r3(P1), r3(VD), r3(HN), r3(TMP)
                    self.stt(p1, hs3, -1.0, bc(KKN), ALU.mult, ALU.mult, [HS.k[j], KKN.k[j]], P1.all)
                    self.mm(PA_.t[:, :], BONES, P1.t[:, :], True, True, P1.all + CON.all, PA_.all)
                    self.tt('pool', vd, bc(V_), i64b, ALU.mult, [V_.k[j]] + CON.all, VD.all)
                    self.mm(PB2.t[:, :], BONES, VD.t[:, :], True, True, VD.all + CON.all, PB2.all)
                    self.tt('pool', hn, hs3, bc(WD), ALU.mult, [HS.k[j], WD.k[j]], HN.all)
                    self.tt('dve', tmp, r3(PA_), bc(BV), ALU.mult, PA_.all + [BV.k[j]], TMP.all)
                    self.tt('pool', hn, hn, tmp, ALU.add, HN.all + TMP.all, HN.all)
                    self.tt('dve', tmp, r3(PB2), bc(K_), ALU.mult, PB2.all + [K_.k[j]], TMP.all)
                    self.tt('pool', hn, hn, tmp, ALU.add, HN.all + TMP.all, HN.all)
                    self.dma('sp', ov[:, j, ns, :], hn, HN.all, [], sems[pi_ % 2])
                    self.tt('pool', p1, hn, bc(R_), ALU.mult, HN.all + [R_.k[j]], P1.all)
                    self.mm(PX_.t[:, :], BONES, P1.t[:, :], True, True, P1.all + CON.all, PX_.all)
                    self.tt('dve', tmp, r3(PX_), i64b, ALU.mult, PX_.all + CON.all, TMP.all)
                    self.reduce(YFM.t[:, j, ns], tmp, TMP.all, [YFM.k[j]])

                def runner(half, bk, spc, sems):
                    for j in range(4):
                        piece(j, half, bk, spc, sems, j)
                co_run([lambda: runner(0, (PA, PB_, PX), SPCS[0], self.wkvs_sems[0:2]),
                        lambda: runner(1, (PM[0], PM[1], PST), SPCS[1], self.wkvs_sems[2:4])])

            self.wkvs_sems = [S.new_dma_sem() for _ in range(4)]
            blockA(T, NS, True, False)

            osem = [S.new_dma_sem() for _ in range(5)]
            self.dma('sp', o_shift.rearrange("p (a b) -> p a b", a=14), SHO.t, SHO.all, [], osem[0])
            self.dma('sp', o_lconv.rearrange("p (a b c) -> p a b c", a=4, b=3), LCO.t, LCO.all, [], osem[1])
            self.dma('sp', o_lh.rearrange("p (a b) -> p a b", a=4), LHO.t, LHO.all, [], osem[2])
            self.dma('sp', o_wkvp.rearrange("p (a b) -> p a b", a=4), WKP.t, WKP.all, [], osem[4])
            S.barrier()
            A.off, A.top = mark
            WUP = A.alloc("wup", [8, 2 * DFF], BF16, nk=8)
            WDN = A.alloc("wdn", [22, D], BF16, nk=22)
            N_ = NBB
            XB2 = [A.alloc("xb2_%d" % i, [8, N_], F32, nk=8) for i in range(2)]
            XN2 = [A.alloc("xn2_%d" % i, [8, N_ + 2], BF16, nk=8) for i in range(2)]
            UU = [A.alloc("uu%d" % i, [N_], F32) for i in range(10)]
            HM = A.alloc("hm", [22, N_], BF16, nk=22)
            FO = A.alloc("fo", [8, N_], F32, nk=8)
            SFC = A.alloc("sfc", [44, 2, NS], F32)
            FCO = A.alloc("fco", [44, 2, 17], F32)
            print("phase B arena used", A.off, "top", A.top)
            PMB = [Buf("pmb%d" % i, ps[i][:, :], 1, True) for i in (0, 1, 3, 4, 5, 6)]
            PDN = Buf("pdn", pst[:, :].bitcast(F32), 1, True)
            wsb = [S.new_dma_sem() for _ in range(10)]
            wuv = wup.rearrange("(k p) n -> p k n", p=128)
            WUPT = {}
            qi = 0
            for g_ in range(4):
                for half in range(2):
                    c0_ = half * DFF + 768 * g_
                    c1_ = half * DFF + min(768 * (g_ + 1), DFF)
                    WUPT[(g_, half)] = Tk("wupt%d_%d" % (g_, half))
                    self.dma('pool', WUP.t[:, :, c0_:c1_], wuv[:, :, c0_:c1_], [], [WUPT[(g_, half)]], wsb[qi])
                    qi += 1
            wdv = wdn.rearrange("(k p) n -> p k n", p=128)
            self.dma('pool', WDN.t[:, 0:11, :], wdv[:, 0:11, :], [], WDN.k[0:11], wsb[8])
            self.dma('pool', WDN.t[:, 11:22, :], wdv[:, 11:22, :], [], WDN.k[11:22], wsb[9])
            sfs = S.new_dma_sem()
            self.dma('sp', SFC.t, s_fconv.rearrange("p (a b c) -> p a b c", a=44, b=2), [], SFC.all, sfs)
            self.memset('pool', FCO.t, 0.0, FCO.all)
            x2sem = [S.new_dma_sem() for _ in range(2)]
            ysem = S.new_dma_sem()
            blocks = [(bi * NBB, NBB, False) for bi in range(T // NBB)] + [(T, NS, True)]
            nb_ = len(blocks)

            def rms_g(src, N):
                for k in range(8):
                    sq = SQ[k % 2]
                    sqb = sq.t[:, :].bitcast(BF16)
                    self.act(sqb[:, :N], src.t[:, k, :N], AF.Square, [src.k[k]], sq.all)
                    self.mm(PST.t[:, :N], ONESB, sqb[:, :N], k == 0, k == 7, sq.all + ONB.all, PST.all)
                    yield
                self.act(RSTD.t[:, :N], PST.t[:, :N], AF.Ln, PST.all, RSTD.all, scale=1.0 / D, bias=1e-6)
                self.act(RSTD.t[:, :N], RSTD.t[:, :N], AF.Exp, RSTD.all, RSTD.all, scale=-0.5)
                yield

            def P1(b):
                c0, N, sample = blocks[b]
                xb, xn = XB2[b % 2], XN2[b % 2]
                self.dma('sp', xb.t[:, :, :N], x1v[:, :, c0:c0 + N], [X1S], xb.all, x2sem[b % 2])
                yield
                yield from rms_g(xb, N)
                if b == 0:
                    self.memset('pool', xn.t[:, :, 0:2], 0.0, xn.all)
                elif not sample:
                    self.cp('pool', xn.t[:, :, 0:2], XN2[(b - 1) % 2].t[:, :, NBB:NBB + 2], XN2[(b - 1) % 2].all, xn.all)
                for k in range(8):
                    self.stt(xn.t[:, k, 2:N + 2], xb.t[:, k, :N], V('g_pre2', k), RSTD.t[:, :N], ALU.mult, ALU.mult,
                             [xb.k[k]] + VEC.all + RSTD.all, [xn.k[k]])
                    yield
                bflags[('p1', b)] = True

            pmb_busy = [False] * 6

            def acq_b():
                while True:
                    for i_ in range(6):
                        if not pmb_busy[i_]:
                            pmb_busy[i_] = True
                            return i_
                    yield

            def up_chunk_g(b, oc, pm, out_u):
                c0, N, sample = blocks[b]
                pbi = yield from acq_b()
                pm = PMB[pbi]
                xn = XN2[b % 2]
                last = (b == nb_ - 2)
                NN = N if sample else N + 2
                lo = 2 if sample else 0
                for k in range(8):
                    self.mm(pm.t[:, :NN], WUP.t[:, k, oc * 128:(oc + 1) * 128], xn.t[:, k, lo:lo + NN], k == 0, k == 7,
                            [WUPT[((oc % 22) // 6, oc // 22)], xn.k[k]], pm.all)
                yield
                if not sample:
                    taps = [pm.t[:, i:i + N] for i in range(3)]
                    tr_ = pm.all
                    if last:
                        self.cp('act', FCO.t[:, oc, :, 0], pm.t[:, N:N + 2], pm.all, FCO.all)
                else:
                    self.cp('act', FCO.t[:, oc, 1, 1:1 + NS], pm.t[:, :N], pm.all, FCO.all)
                    self.cp('pool', FCO.t[:, oc, 0, 1:1 + NS], SFC.t[:, oc, 1, :], SFC.all, FCO.all)
                    taps = [SFC.t[:, oc, 0, :], SFC.t[:, oc, 1, :], FCO.t[:, oc, 1, 1:1 + NS]]
                    tr_ = SFC.all + FCO.all
                self.act(out_u.t[:, :N], taps[2], AF.Identity, tr_ + VEC.all, out_u.all, scale=V('fcw', 88 + oc), bias=V('fcb', oc))
                yield
                for i in (1, 0):
                    self.stt(out_u.t[:, :N], taps[i], V('fcw', 44 * i + oc), out_u.t[:, :N], ALU.mult, ALU.add,
                             tr_ + VEC.all + out_u.all, out_u.all)
                    if i == 0:
                        pmb_busy[pbi] = False
                    yield

            hms_v = XN2[0].t[:, 0:2, 32:208].rearrange("p a (b c) -> p a b c", b=11)
            HMSK = [Tk("hms%d" % i_) for i_ in range(22)]
            bflags = {}
            uu_seq = [0]

            def hm_of(b, i):
                if blocks[b][2]:
                    return hms_v[:, i // 11, i % 11, :], HMSK[i]
                return HM.t[:, i, :blocks[b][1]], HM.k[i]

            def pair_g(b, i):
                c0, N, sample = blocks[b]
                if sample:
                    while ('p1', b) not in bflags:
                        yield
                q_ = uu_seq[0]
                uu_seq[0] += 1
                ug, uv = UU[(2 * q_) % 10], UU[(2 * q_ + 1) % 10]
                yield from up_chunk_g(b, i, None, ug)
                self.act(ug.t[:, :N], ug.t[:, :N], AF.Gelu_apprx_tanh, ug.all, ug.all)
                yield
                yield from up_chunk_g(b, 22 + i, None, uv)
                hm_ap, hm_k = hm_of(b, i)
                self.tt('pool', hm_ap, ug.t[:, :N], uv.t[:, :N], ALU.mult, ug.all + uv.all, [hm_k])
                yield

            def down_g(b):
                c0, N, sample = blocks[b]
                for oc in range(8):
                    pm = PDN
                    for i in range(22):
                        hm_ap, hm_k = hm_of(b, i)
                        self.mm(pm.t[:, :N], WDN.t[:, i, oc * 128:(oc + 1) * 128], hm_ap, i == 0, i == 21,
                                [WDN.k[i], hm_k], pm.all)
                    self.cp('act', FO.t[:, oc, :N], pm.t[:, :N], pm.all, [FO.k[oc]])
                    yield

            def tail_g(b):
                c0, N, sample = blocks[b]
                xb = XB2[b % 2]
                yield from rms_g(FO, N)
                for k in range(8):
                    self.stt(FO.t[:, k, :N], FO.t[:, k, :N], V('g_post2', k), RSTD.t[:, :N], ALU.mult, ALU.mult,
                             [FO.k[k]] + VEC.all + RSTD.all, [FO.k[k]])
                    self.tt('pool', FO.t[:, k, :N], FO.t[:, k, :N], xb.t[:, k, :N], ALU.add, [FO.k[k], xb.k[k]], [FO.k[k]])
                    yield
                self.dma('sp', yTv[:, :, c0:c0 + N], FO.t[:, :, :N], FO.all, [], ysem)
                yield

            def chain(*gs):
                for g in gs:
                    yield from g

            def window(gens, width):
                gens = list(gens)
                act_ = []
                idx = 0
                while idx < len(gens) or act_:
                    if len(act_) < width and idx < len(gens):
                        act_.append(gens[idx])
                        idx += 1
                    for g in list(act_):
                        try:
                            next(g)
                        except StopIteration:
                            act_.remove(g)
                        yield

            def rr(streams, weights):
                streams = list(streams)
                alive = [True] * len(streams)
                while any(alive):
                    for si, g in enumerate(streams):
                        if not alive[si]:
                            continue
                        for _ in range(weights[si]):
                            try:
                                next(g)
                            except StopIteration:
                                alive[si] = False
                                break

            rr([P1(0)], [1])
            for b in range(nb_ - 1):
                plist = [pair_g(b, i) for i in range(22)]
                if b == nb_ - 2:
                    plist += [pair_g(nb_ - 1, i) for i in range(22)]
                main = [window(plist, 5)]
                if b > 0:
                    main = [down_g(b - 1)] + main
                side = []
                if b > 0:
                    side.append(tail_g(b - 1))
                if b + 1 < nb_:
                    side.append(P1(b + 1))
                rr([chain(*main), chain(*side)], [12, 1])
            rr([chain(down_g(nb_ - 2), tail_g(nb_ - 2), down_g(nb_ - 1), tail_g(nb_ - 1))], [1])

            self.dma('sp', o_fconv.rearrange("p (a b c) -> p a b c", a=44, b=2), FCO.t, FCO.all, [], osem[3])
            S.finish([])
            cnt = S.emit()
            print("instr counts", cnt)
        return nc


def _colpack(v):
    v = np.asarray(v, np.float32).reshape(-1)
    n = v.shape[0] // 128
    return np.ascontiguousarray(v.reshape(n, 128).T)


def _consts():
    c = np.zeros((128, NCONST), np.float32)
    p = np.arange(128)[:, None]
    q = np.arange(128)[None, :]
    c[:, C_ONES:C_ONES + 128] = 1.0
    c[:, C_BONES:C_BONES + 128] = (p // 64 == q // 64)
    c[:, C_IDENT:C_IDENT + 128] = (p == q)
    s = p % 64
    t = q % 64
    m1 = np.where(q < 64, s < t, s <= t)
    c[:, C_MASK1:C_MASK1 + 128] = m1
    q64 = np.arange(64)[None, :]
    c[:, C_MASK2:C_MASK2 + 64] = (s > q64)
    c[:, C_I64:C_I64 + 64] = (s == q64)
    return c


_NC_CACHE = {}


def _get_nc(debug=None):
    key = tuple(debug or [])
    if key not in _NC_CACHE:
        _NC_CACHE[key] = Builder(debug).build()
    return _NC_CACHE[key]


def _prep_inputs(inp):
    f = lambda a: np.ascontiguousarray(np.asarray(a, np.float32))
    g = lambda n: f(inp[n])[0]
    vec = np.zeros((128, NVEC), np.float32)

    def put(name, arr):
        a = _colpack(arr)
        vec[:, VOFF[name]:VOFF[name] + a.shape[1]] = a
    put('g_pre', g('norm_pre_mix'))
    put('g_post', g('norm_post_mix'))
    put('g_pre2', g('norm_pre_ffn'))
    put('g_post2', g('norm_post_ffn'))
    put('mu', g('rwkv_mu'))
    put('w0', g('rwkv_w0'))
    put('a0', g('rwkv_a0'))
    put('k_k', g('rwkv_k_k'))
    put('k_a', g('rwkv_k_a'))
    put('r_k', g('rwkv_r_k'))
    put('lnx_w', g('rwkv_lnx_w'))
    put('lnx_b', g('rwkv_lnx_b'))
    put('lcw', g('lru_conv_w'))
    put('lcb', g('lru_conv_b'))
    put('gxb', g('lru_gx_b'))
    put('gab', g('lru_ga_b'))
    put('lam', g('lru_lambda'))
    put('fcw', g('ffn_conv_w'))
    put('fcb', g('ffn_conv_b'))
    shared = {
        "w_in": g('w_in'), "vecs": vec, "consts": _consts(),
        "lora": np.ascontiguousarray(np.concatenate([g('rwkv_w_up'), g('rwkv_a_up')], axis=0)),
        "gup": g('rwkv_g_up'),
        "wor": g('rwkv_w_out'), "wol": g('lru_w_out'), "wo": g('w_o'),
        "wup": g('ffn_up'), "wdn": g('ffn_down'),
    }
    gx, ga = g('lru_gx_w'), g('lru_ga_w')
    gxa = np.stack([gx, ga], 0)
    gxa = gxa.reshape(2, 4, 2, 64, 64).transpose(2, 3, 0, 1, 4)
    shared["gxa"] = np.ascontiguousarray(gxa.reshape(128, 2 * 4 * 64))
    xp = f(inp['x_prompt'])
    xs = f(inp['x_sample'])[:, 0, :]
    sh = g('state_rwkv_shift')[:, 0, :]
    wkv = g('state_rwkv_wkv')
    lc = g('state_lru_conv')
    lh = g('state_lru_h')
    fc = g('state_ffn_conv')
    maps = []
    for c in range(NCORES):
        n0 = c * NS
        m = dict(shared)
        m["xT"] = np.ascontiguousarray(np.concatenate([xp[c].T, xs[n0:n0 + NS].T], axis=1))
        m["s_shift"] = np.ascontiguousarray(sh[n0:n0 + NS].reshape(NS, 14, 128).transpose(2, 1, 0).reshape(128, -1))
        m["s_lconv"] = np.ascontiguousarray(lc[n0:n0 + NS].reshape(NS, 3, 4, 128).transpose(3, 2, 1, 0).reshape(128, -1))
        m["s_lh"] = np.ascontiguousarray(lh[n0:n0 + NS].reshape(NS, 4, 128).transpose(2, 1, 0).reshape(128, -1))
        m["s_fconv"] = np.ascontiguousarray(fc[n0:n0 + NS].reshape(NS, 2, 44, 128).transpose(3, 2, 1, 0).reshape(128, -1))
        w = wkv[n0:n0 + NS].reshape(NS, 4, 2, 64, 64)
        m["s_wkv"] = np.ascontiguousarray(w.transpose(2, 4, 1, 0, 3).reshape(128, -1))
        maps.append(m)
    return maps


def _assemble(results):
    yp = np.zeros((NCORES, T, D), np.float32)
    ys = np.zeros((NCORES * NS, 1, D), np.float32)
    p_shift = np.zeros((1, NCORES, 1, RP), np.float32)
    p_wkv = np.zeros((1, NCORES, 8, 64, 64), np.float32)
    p_lconv = np.zeros((1, NCORES, 3, RW), np.float32)
    p_lh = np.zeros((1, NCORES, RW), np.float32)
    p_fconv = np.zeros((1, NCORES, 2, 2 * DFF), np.float32)
    s_shift = np.zeros((1, NCORES * NS, 1, RP), np.float32)
    s_wkv = np.zeros((1, NCORES * NS, 8, 64, 64), np.float32)
    s_lconv = np.zeros((1, NCORES * NS, 3, RW), np.float32)
    s_lh = np.zeros((1, NCORES * NS, RW), np.float32)
    s_fconv = np.zeros((1, NCORES * NS, 2, 2 * DFF), np.float32)
    for c, r in enumerate(results):
        n0 = c * NS
        yT = r["yT"]
        yp[c] = yT[:, :T].T
        ys[n0:n0 + NS, 0] = yT[:, T:].T
        a = r["o_shift"].reshape(128, 14, 17).transpose(2, 1, 0).reshape(17, RP)
        p_shift[0, c, 0] = a[0]
        s_shift[0, n0:n0 + NS, 0] = a[1:]
        a = r["o_lconv"].reshape(128, 4, 3, 17).transpose(3, 2, 1, 0).reshape(17, 3, RW)
        p_lconv[0, c] = a[0]
        s_lconv[0, n0:n0 + NS] = a[1:]
        a = r["o_lh"].reshape(128, 4, 17).transpose(2, 1, 0).reshape(17, RW)
        p_lh[0, c] = a[0]
        s_lh[0, n0:n0 + NS] = a[1:]
        a = r["o_fconv"].reshape(128, 44, 2, 17).transpose(3, 2, 1, 0).reshape(17, 2, 2 * DFF)
        p_fconv[0, c] = a[0]
        s_fconv[0, n0:n0 + NS] = a[1:]
        a = r["o_wkvp"].reshape(2, 64, 4, 64)
        p_wkv[0, c] = a.transpose(2, 0, 3, 1).reshape(8, 64, 64)
        a = r["o_wkvs"].reshape(2, 64, 4, NS, 64)
        s_wkv[0, n0:n0 + NS] = a.transpose(3, 2, 0, 4, 1).reshape(NS, 8, 64, 64)
    return (yp, ys, p_shift, p_wkv, p_lconv, p_lh, p_fconv, s_shift, s_wkv, s_lconv, s_lh, s_fconv)


def kernel(**inputs):
    nc = _get_nc()
    maps = _prep_inputs(inputs)
    res = run_bass_kernel_spmd(nc, maps, core_ids=list(range(NCORES)))
    return _assemble(res.results)
```
